# Optimizing a Trainium2 kernel written in Bass

```python
import jax
import jax.numpy as jnp
from jax import lax
import numpy as np

D_MODEL = 2048
BATCH = 32
SEQ = 256
DEPTH = 2
DEC_BATCH = 4
DEC_SEQ = 1024
PAST_LEN = 256

GRID_W = 64
CHUNK = 128
A_HEADS = 4
A_WIDTH = D_MODEL // 4
B_GROUPS = 4
B_WIDTH = D_MODEL // 4
B_GROUP_DIM = B_WIDTH // B_GROUPS
C_WIDTH = D_MODEL // 2
C_HEAD_DIM = 64
C_HEADS = C_WIDTH // C_HEAD_DIM
N_DIR = 2
DECAY_RANK = 64
ICLR_RANK = 64
GATE_RANK = 160
C_IN = 3 * C_WIDTH + N_DIR * DECAY_RANK + N_DIR * ICLR_RANK + GATE_RANK
IN_COLS = 2 * A_WIDTH + B_WIDTH + C_IN
MIX_WIDTH = A_WIDTH + B_WIDTH + C_WIDTH
D_FF = 5632
N_MOD = 9
RMS_EPS = 1e-6
LN_EPS = 1e-5
LNX_EPS = 64e-5

kernel_name = "hybrid_gmlp_fnet_rwkv7_diffusion_step"


def _rmsnorm(x, g):
    xf = x.astype(jnp.float32)
    y = xf * lax.rsqrt(jnp.mean(xf * xf, axis=-1, keepdims=True) + RMS_EPS)
    return (y * g.astype(jnp.float32)).astype(x.dtype)


def _swiglu(h, w_in, w_out):
    gate, up = jnp.split(h @ w_in, 2, axis=-1)
    return (jax.nn.silu(gate) * up) @ w_out


def _shift_seq(z):
    h = z.shape[-1] // 2
    prev = jnp.pad(z[:, :-1, :h], ((0, 0), (1, 0), (0, 0)))
    nxt = jnp.pad(z[:, 1:, h:], ((0, 0), (0, 1), (0, 0)))
    return jnp.concatenate([prev, nxt], axis=-1)


def _shift_grid(z):
    bsz, length, ch = z.shape
    rows = length // GRID_W
    g = z.reshape(bsz, rows, GRID_W, ch)
    q = ch // 4
    left = jnp.pad(g[:, :, :-1, :q], ((0, 0), (0, 0), (1, 0), (0, 0)))
    right = jnp.pad(g[:, :, 1:, q:2 * q], ((0, 0), (0, 0), (0, 1), (0, 0)))
    up = jnp.pad(g[:, :-1, :, 2 * q:3 * q], ((0, 0), (1, 0), (0, 0), (0, 0)))
    down = jnp.pad(g[:, 1:, :, 3 * q:], ((0, 0), (0, 1), (0, 0), (0, 0)))
    return jnp.concatenate([left, right, up, down], axis=-1).reshape(bsz, length, ch)


def _spatial_gating(zu, zv, ln_g, ln_b, w_s, b_s):
    bsz, length, _ = zu.shape
    u = jax.nn.gelu(zu)
    v = jax.nn.gelu(zv).astype(jnp.float32)
    mu = jnp.mean(v, axis=-1, keepdims=True)
    var = jnp.mean(jnp.square(v - mu), axis=-1, keepdims=True)
    v = ((v - mu) * lax.rsqrt(var + LN_EPS) * ln_g + ln_b).astype(zu.dtype)
    v = v.reshape(bsz, length // CHUNK, CHUNK, A_HEADS, A_WIDTH // A_HEADS)
    mixed = jnp.einsum('hts,bnshd->bnthd', w_s, v) + b_s.T[None, None, :, :, None]
    return u * mixed.reshape(bsz, length, A_WIDTH).astype(zu.dtype)


def _fourier_mix(z):
    bsz, length, _ = z.shape
    zf = z.astype(jnp.float32).reshape(bsz, length, B_GROUPS, B_GROUP_DIM)
    zf = jnp.swapaxes(zf, 1, 2)
    out = jnp.fft.fft2(zf, norm='ortho').real
    return jnp.swapaxes(out, 1, 2).reshape(bsz, length, B_WIDTH).astype(z.dtype)


def _wkv_scan(s0, r, dec, k, v, kk, kka, reverse):
    def step(S, inp):
        r_t, d_t, k_t, v_t, kk_t, kka_t = inp
        sa = jnp.einsum('bhvk,bhk->bhv', S, kk_t)
        S = S * d_t[:, :, None, :] - sa[..., None] * kka_t[:, :, None, :] + v_t[..., None] * k_t[:, :, None, :]
        return S, jnp.einsum('bhvk,bhk->bhv', S, r_t)
    xs = tuple(jnp.swapaxes(t, 0, 1) for t in (r, dec, k, v, kk, kka))
    s_final, ys = lax.scan(step, s0, xs, reverse=reverse)
    return s_final, jnp.swapaxes(ys, 0, 1)


def _rwkv7_bidir(zc, grid, s0, mu, w0, w2, a0, a2, k_k, k_a, r_k, g2, lnx_g, lnx_b):
    bsz, length, _ = zc.shape
    f32 = jnp.float32
    zs = _shift_grid(zc) if grid else _shift_seq(zc)
    z = zc + (zs - zc) * mu
    idx = [C_WIDTH, 2 * C_WIDTH, 3 * C_WIDTH, 3 * C_WIDTH + N_DIR * DECAY_RANK,
           3 * C_WIDTH + N_DIR * (DECAY_RANK + ICLR_RANK)]
    r, k, v, wd, ad, gd = jnp.split(z, idx, axis=-1)
    wd = wd.reshape(bsz, length, N_DIR, DECAY_RANK)
    ad = ad.reshape(bsz, length, N_DIR, ICLR_RANK)
    w_log = -jax.nn.softplus(-(w0 + jnp.einsum('bldr,drc->bldc', jnp.tanh(wd), w2)).astype(f32)) - 0.5
    decay = jnp.exp(-jnp.exp(w_log))
    a = jax.nn.sigmoid((a0 + jnp.einsum('bldr,drc->bldc', ad, a2)).astype(f32))
    kf = k.astype(f32)
    kk = (kf * k_k).reshape(bsz, length, C_HEADS, C_HEAD_DIM)
    kk = kk / jnp.maximum(jnp.sqrt(jnp.sum(kk * kk, axis=-1, keepdims=True)), 1e-12)
    kk = kk.reshape(bsz, length, C_WIDTH)
    k_mod = kf[:, :, None, :] * (1.0 + (a - 1.0) * k_a)

    def heads(t):
        return t.reshape(bsz, length, C_HEADS, C_HEAD_DIM)

    rf = heads(r.astype(f32))
    vf = heads(v.astype(f32))
    s0f = s0.astype(f32)
    s_f, y_f = _wkv_scan(s0f[:, 0], rf, heads(decay[:, :, 0]), heads(k_mod[:, :, 0]), vf,
                         heads(kk), heads(kk * a[:, :, 0]), reverse=False)
    s_b, y_b = _wkv_scan(s0f[:, 1], rf, heads(decay[:, :, 1]), heads(k_mod[:, :, 1]), vf,
                         heads(kk), heads(kk * a[:, :, 1]), reverse=True)
    y = y_f + y_b
    m = jnp.mean(y, axis=-1, keepdims=True)
    var = jnp.mean(jnp.square(y - m), axis=-1, keepdims=True)
    y = ((y - m) * lax.rsqrt(var + LNX_EPS)).reshape(bsz, length, C_WIDTH) * lnx_g + lnx_b
    bonus = jnp.einsum('blhn,bldhn,hn->blh', rf,
                       k_mod.reshape(bsz, length, N_DIR, C_HEADS, C_HEAD_DIM), r_k.astype(f32))
    y = y + (bonus[..., None] * vf).reshape(bsz, length, C_WIDTH)
    g = jax.nn.sigmoid(gd) @ g2
    out = y.astype(zc.dtype) * g
    state = jnp.stack([s_f, s_b], axis=1).astype(s0.dtype)
    return out, state


def _token_mix(h, grid, s0, lp):
    z = h @ lp['w_in']
    zu, zv, zb, zc = jnp.split(z, [A_WIDTH, 2 * A_WIDTH, 2 * A_WIDTH + B_WIDTH], axis=-1)
    ya = _spatial_gating(zu, zv, lp['sgu_ln_g'], lp['sgu_ln_b'], lp['sgu_w'], lp['sgu_b'])
    yb = _fourier_mix(zb)
    yc, state = _rwkv7_bidir(zc, grid, s0, lp['shift_mu'], lp['decay_w0'], lp['decay_w2'],
                             lp['iclr_a0'], lp['iclr_a2'], lp['k_k'], lp['k_a'], lp['r_k'],
                             lp['gate_w2'], lp['lnx_g'], lp['lnx_b'])
    return jnp.concatenate([ya, yb, yc], axis=-1) @ lp['w_out'], state


def _layer(x, cond, s0, grid, lp):
    mod = (jax.nn.silu(cond) @ lp['w_mod'] + lp['b_mod']).reshape(cond.shape[0], 1, N_MOD, D_MODEL)
    norm_g = lp['norm_g']

    def modulated(x_, i):
        return _rmsnorm(x_, norm_g[i]) * (1.0 + mod[:, :, 3 * i + 1]) + mod[:, :, 3 * i]

    x = x + 0.5 * mod[:, :, 2] * _swiglu(modulated(x, 0), lp['ffn_w_in'][0], lp['ffn_w_out'][0])
    y, state = _token_mix(modulated(x, 1), grid, s0, lp)
    x = x + mod[:, :, 5] * y
    x = x + 0.5 * mod[:, :, 8] * _swiglu(modulated(x, 2), lp['ffn_w_in'][1], lp['ffn_w_out'][1])
    return x, state


def setup_inputs(seed: int = 0) -> dict:
    key = jax.random.key(seed)
    ks = jax.random.split(key, 28)
    f32 = jnp.float32
    d = D_MODEL

    def nrm(k, shape, scale):
        return jax.random.normal(k, shape, f32) * scale

    return {
        'x_prompt': nrm(ks[0], (BATCH, SEQ, d), 1.0),
        'x_sample': nrm(ks[1], (DEC_BATCH, DEC_SEQ, d), 1.0),
        'state_wkv': nrm(ks[2], (DEC_BATCH, DEPTH, N_DIR, C_HEADS, C_HEAD_DIM, C_HEAD_DIM), 0.5),
        'c': nrm(ks[3], (DEC_BATCH, d), 1.0),
        'c_ctx': nrm(ks[4], (d,), 1.0),
        'norm_g': 1.0 + nrm(ks[5], (DEPTH, 3, d), 0.02),
        'w_mod': nrm(ks[6], (DEPTH, d, N_MOD * d), 0.5 * d ** -0.5),
        'b_mod': nrm(ks[7], (DEPTH, N_MOD * d), 0.02),
        'ffn_w_in': nrm(ks[8], (DEPTH, 2, d, 2 * D_FF), d ** -0.5),
        'ffn_w_out': nrm(ks[9], (DEPTH, 2, D_FF, d), D_FF ** -0.5),
        'w_in': nrm(ks[10], (DEPTH, d, IN_COLS), d ** -0.5),
        'w_out': nrm(ks[11], (DEPTH, MIX_WIDTH, d), MIX_WIDTH ** -0.5),
        'sgu_ln_g': 1.0 + nrm(ks[12], (DEPTH, A_WIDTH), 0.02),
        'sgu_ln_b': nrm(ks[13], (DEPTH, A_WIDTH), 0.02),
        'sgu_w': nrm(ks[14], (DEPTH, A_HEADS, CHUNK, CHUNK), CHUNK ** -0.5),
        'sgu_b': 1.0 + nrm(ks[15], (DEPTH, A_HEADS, CHUNK), 0.1),
        'shift_mu': jax.random.uniform(ks[16], (DEPTH, C_IN), f32, 0.2, 0.8),
        'decay_w0': nrm(ks[17], (DEPTH, N_DIR, C_WIDTH), 0.5),
        'decay_w2': nrm(ks[18], (DEPTH, N_DIR, DECAY_RANK, C_WIDTH), 0.5 * DECAY_RANK ** -0.5),
        'iclr_a0': nrm(ks[19], (DEPTH, N_DIR, C_WIDTH), 0.1),
        'iclr_a2': nrm(ks[20], (DEPTH, N_DIR, ICLR_RANK, C_WIDTH), ICLR_RANK ** -0.5),
        'k_k': 0.85 + nrm(ks[21], (DEPTH, C_WIDTH), 0.05),
        'k_a': 1.0 + nrm(ks[22], (DEPTH, C_WIDTH), 0.05),
        'r_k': nrm(ks[23], (DEPTH, C_HEADS, C_HEAD_DIM), 0.1),
        'gate_w2': nrm(ks[24], (DEPTH, GATE_RANK, C_WIDTH), GATE_RANK ** -0.5),
        'lnx_g': 1.0 + nrm(ks[25], (DEPTH, C_WIDTH), 0.02),
        'lnx_b': nrm(ks[26], (DEPTH, C_WIDTH), 0.02),
        'final_g': 1.0 + nrm(ks[27], (d,), 0.02),
    }


def reference(x_prompt, x_sample, state_wkv, c, c_ctx, norm_g, w_mod, b_mod, ffn_w_in, ffn_w_out,
              w_in, w_out, sgu_ln_g, sgu_ln_b, sgu_w, sgu_b, shift_mu, decay_w0, decay_w2,
              iclr_a0, iclr_a2, k_k, k_a, r_k, gate_w2, lnx_g, lnx_b, final_g):
    xp = x_prompt
    xs = x_sample
    zero_state = jnp.zeros((x_prompt.shape[0], N_DIR, C_HEADS, C_HEAD_DIM, C_HEAD_DIM), x_prompt.dtype)
    ctx_cond = c_ctx[None, :]
    ctx_states = []
    for l in range(DEPTH):
        lp = {
            'norm_g': norm_g[l], 'w_mod': w_mod[l], 'b_mod': b_mod[l],
            'ffn_w_in': ffn_w_in[l], 'ffn_w_out': ffn_w_out[l],
            'w_in': w_in[l], 'w_out': w_out[l],
            'sgu_ln_g': sgu_ln_g[l], 'sgu_ln_b': sgu_ln_b[l], 'sgu_w': sgu_w[l], 'sgu_b': sgu_b[l],
            'shift_mu': shift_mu[l], 'decay_w0': decay_w0[l], 'decay_w2': decay_w2[l],
            'iclr_a0': iclr_a0[l], 'iclr_a2': iclr_a2[l], 'k_k': k_k[l], 'k_a': k_a[l],
            'r_k': r_k[l], 'gate_w2': gate_w2[l], 'lnx_g': lnx_g[l], 'lnx_b': lnx_b[l],
        }
        xp, s_ctx = _layer(xp, ctx_cond, zero_state, False, lp)
        ctx_states.append(s_ctx)
        xs, _ = _layer(xs, c, state_wkv[:, l], True, lp)
    y_prompt = _rmsnorm(xp, final_g)
    y_sample = _rmsnorm(xs, final_g)
    new_state_wkv = jnp.stack(ctx_states, axis=1)
    return (y_prompt, y_sample, new_state_wkv)
```

```python
import numpy as np
import concourse.bass as bass
import concourse.mybir as mybir
from concourse.bass_utils import run_bass_kernel_spmd

F32 = mybir.dt.float32
BF16 = mybir.dt.bfloat16
AF = mybir.ActivationFunctionType
OP = mybir.AluOpType
AX = mybir.AxisListType

D = 2048
KC = 16
NT = 1536
NTT = 12
DFF = 5632
NJ = 44
DEPTH = 2
CIN = 3488
INC = 5024
SB_BASE = 17408
SB_END = 229376


class K:
    def __init__(s, nc):
        s.nc = nc
        s.E = {'pe': nc.tensor, 'dve': nc.vector, 'act': nc.scalar, 'pool': nc.gpsimd, 'sp': nc.sync}
        s.csem = {e: nc.alloc_semaphore('c_' + e) for e in s.E}
        s.cnt = {e: 0 for e in s.E}
        s.seen = {}
        s.track = {}
        s.dpool = []
        s.dmap = {}
        s.dnext = 0
        s.sbuf_off = SB_BASE
        s.ninst = 0
        s.uid = 0

    def _wait(s, eng, tok):
        if tok is None:
            return
        sem, val, owner = tok
        if owner == eng and eng == 'pe':
            return
        kk = (eng, id(sem))
        if s.seen.get(kk, -1) >= val:
            return
        s.seen[kk] = val
        s.E[eng].wait_ge(sem, val)
        s.ninst += 1

    def _deps(s, eng, reads, writes):
        for key in reads:
            t = s.track.get(key)
            if t is not None:
                s._wait(eng, t['w'])
        for key in writes:
            t = s.track.get(key)
            if t is not None:
                s._wait(eng, t['w'])
                for r in t['r']:
                    s._wait(eng, r)

    def _commit(s, tok, reads, writes):
        for key in reads:
            t = s.track.setdefault(key, {'w': None, 'r': []})
            t['r'].append(tok)
            if len(t['r']) > 16:
                best = {}
                for r in t['r']:
                    q = id(r[0])
                    if q not in best or best[q][1] < r[1]:
                        best[q] = r
                t['r'] = list(best.values())
        for key in writes:
            s.track[key] = {'w': tok, 'r': []}

    def op(s, eng, fn, reads=(), writes=()):
        s._deps(eng, reads, writes)
        ins = fn()
        s.cnt[eng] += 1
        ins.then_inc(s.csem[eng], 1)
        s._commit((s.csem[eng], s.cnt[eng], eng), reads, writes)
        s.ninst += 1
        return ins

    def dma(s, q, out, in_, reads=(), writes=(), semkey=None):
        s._deps(q, reads, writes)
        if semkey is None:
            semkey = tuple(writes) if writes else tuple(reads)
        if semkey not in s.dmap:
            if s.dnext >= len(s.dpool):
                s.dpool.append([s.nc.alloc_semaphore('d%d' % len(s.dpool)), 0])
            s.dmap[semkey] = s.dnext
            s.dnext += 1
        ent = s.dpool[s.dmap[semkey]]
        ent[1] += 16
        ins = s.E[q].dma_start(out=out, in_=in_)
        ins.then_inc(ent[0], 16)
        s._commit((ent[0], ent[1], 'dma'), reads, writes)
        s.ninst += 1
        return ins

    def all_tokens(s):
        toks = [(s.csem[e], s.cnt[e], e) for e in s.E if s.cnt[e] > 0]
        toks += [(e[0], e[1], 'dma') for e in s.dpool if e[1] > 0]
        return toks

    def barrier(s):
        toks = s.all_tokens()
        for e in s.E:
            for t in toks:
                s._wait(e, t)
        s.track = {}
        s.dmap = {}
        s.dnext = 0

    def finish(s):
        for t in s.all_tokens():
            s._wait('sp', t)

    def sb(s, name, shape, dtype):
        nbytes = int(np.prod(shape[1:])) * (2 if dtype == BF16 else 4)
        off = (s.sbuf_off + 63) // 64 * 64
        s.sbuf_off = off + nbytes
        assert s.sbuf_off <= SB_END, (name, s.sbuf_off)
        s.uid += 1
        return s.nc.alloc_sbuf_tensor_at('%s_%d' % (name, s.uid), list(shape), dtype, offset=off)


class Ring:
    def __init__(s, k, name, shape, dtype, n):
        s.t = [k.sb('%s%d' % (name, i), shape, dtype) for i in range(n)]
        s.keys = ['%s_%d_%d' % (name, k.uid, i) for i in range(n)]
        s.i = 0
        s.n = n

    def next(s):
        j = s.i % s.n
        s.i += 1
        return s.t[j], s.keys[j]


def build_program(debug=(), stop_after=None, phases=None, mix_parts=None):
    nc = bass.Bass("TRN2", target_bir_lowering=False)
    V = nc.vector
    A = nc.scalar
    G = nc.gpsimd
    PE = nc.tensor

    def din(name, shape):
        return nc.dram_tensor(name, list(shape), F32, kind="ExternalInput").ap()

    def dout(name, shape):
        return nc.dram_tensor(name, list(shape), F32, kind="ExternalOutput").ap()

    def dscr(name, shape, dt=F32):
        kind = "ExternalOutput" if name in debug else "Internal"
        return nc.dram_tensor(name, list(shape), dt, kind=kind).ap()

    xin = din("xin", [NT, D])
    cond = din("cond", [2, D])
    s0_in = din("s0", [DEPTH, 2, 16, 64, 64])
    carry_in = din("carry", [128, 1])
    ind_in = din("ind", [2, 4, CIN])
    valid_in = din("valid", [NT, 4])
    CL_in = din("CL", [1024, 1024])
    nSL_in = din("nSL", [1024, 1024])
    CB_in = din("CB", [256, 256])
    nSB_in = din("nSB", [256, 256])
    CSd_in = din("CSd", [128, 256])
    tri_in = din("tri", [4, 128, 128])
    ident_in = din("ident", [128, 128])
    onesm_in = din("onesm", [128, 128])
    WSHAPES = dict([("norm_g", [DEPTH, 3, D]), ("w_mod", [DEPTH, D, 9 * D]), ("b_mod", [DEPTH, 9 * D]),
                        ("ffn_w_in", [DEPTH, 2, D, 2 * DFF]), ("ffn_w_out", [DEPTH, 2, DFF, D]),
                        ("w_in", [DEPTH, D, INC]), ("w_out", [DEPTH, D, D]),
                        ("sgu_ln_g", [DEPTH, 512]), ("sgu_ln_b", [DEPTH, 512]), ("sgu_w", [DEPTH, 4, 128, 128]),
                        ("sgu_b", [DEPTH, 4, 128]), ("shift_mu", [DEPTH, CIN]), ("decay_w0", [DEPTH, 2, 1024]),
                        ("decay_w2", [DEPTH, 2, 64, 1024]), ("iclr_a0", [DEPTH, 2, 1024]),
                        ("iclr_a2", [DEPTH, 2, 64, 1024]), ("k_k", [DEPTH, 1024]), ("k_a", [DEPTH, 1024]),
                        ("r_k", [DEPTH, 16, 64]), ("gate_w2", [DEPTH, 160, 1024]), ("lnx_g", [DEPTH, 1024]),
                        ("lnx_b", [DEPTH, 1024]), ("final_g", [D])])
    used_inputs = []

    class LazyW(dict):
        def __missing__(s, name):
            s[name] = din(name, WSHAPES[name])
            used_inputs.append(name)
            return s[name]
    W = LazyW()
    y_out = dout("y", [NT, D])
    st_out = dout("st", [DEPTH, 6, 2, 16, 64, 64])
    xs = dscr("xs", [KC, 128, NT])
    uT = dscr("uT", [4, 128, NT], BF16)
    vn = dscr("vn", [NT, 512], BF16)
    zbT = dscr("zbT", [4, 128, NT], BF16)
    zc = dscr("zc", [NT + 128, CIN])
    ztm = dscr("ztm", [NT, CIN])
    ymixT = dscr("ymixT", [KC, 128, NT], BF16)
    scP = dscr("scP", [NTT, 2, 64, 16, 64])
    scQ = dscr("scQ", [NTT, 2, 64, 16, 64])
    scG = dscr("scG", [NTT, 2, 64, 16, 128])
    scY = dscr("scY", [NTT, 2, 128, 1024])
    scD = dscr("scD", [NTT, 2, 64, 16])
    scV = dscr("scV", [NT, 1024])
    scGt = dscr("scGt", [NT, 1024])
    scB = dscr("scB", [NT, 16])
    hbT = dscr("hbT", [NJ, 128, NT], BF16)

    k = K(nc)
    ps = [nc.alloc_psum_tensor("ps%d" % i, [128, 512], F32) for i in range(8)]
    pk = ['ps%d' % i for i in range(8)]

    ident = k.sb("ident", [128, 128], F32)
    onesm = k.sb("onesm", [128, 128], F32)
    tri = k.sb("tri", [128, 4, 128], F32)
    carry = k.sb("carry", [128, 1], F32)
    valid = k.sb("valid", [128, NTT, 4], F32)
    epsr = k.sb("epsr", [128, 1], F32)
    epsx = k.sb("epsx", [128, 1], F32)
    epsl = k.sb("epsl", [128, 1], F32)
    modv = k.sb("modv", [128, DEPTH, 144, 2], F32)
    gsv = k.sb("gsv", [128, DEPTH, 3, KC, 2], F32)
    gtv = k.sb("gtv", [128, DEPTH, 3, KC, 2], F32)
    ngT = k.sb("ngT", [128, DEPTH, KC, 3], F32)
    fgT = k.sb("fgT", [128, KC, 1], F32)
    PERSIST_END = k.sbuf_off

    def phase_reset():
        k.barrier()
        k.sbuf_off = PERSIST_END

    k.dma('sp', ident[:], ident_in, writes=['ident'])
    k.dma('sp', onesm[:], onesm_in, writes=['onesm'])
    k.dma('sp', tri[:], tri_in.rearrange("f s t -> s f t"), writes=['tri'])
    k.dma('sp', carry[:], carry_in, writes=['carry'])
    k.dma('sp', valid[:], valid_in.rearrange("(tt p) o -> p tt o", p=128), writes=['valid'])
    k.op('dve', lambda: V.memset(epsr[:], 1e-6), writes=['epsr'])
    k.op('dve', lambda: V.memset(epsx[:], 64e-5), writes=['epsx'])
    k.op('dve', lambda: V.memset(epsl[:], 1e-5), writes=['epsl'])

    def rows_to_fm(src, R, C, dst_fn, tmp, tmpkey, pbank):
        k.dma('sp', tmp[0:R, 0:C], src, writes=[tmpkey])
        nj = C // 128
        for j in range(nj):
            k.op('pe', lambda: PE.transpose(out=ps[pbank][:, j * R:(j + 1) * R], in_=tmp[0:R, j * 128:(j + 1) * 128],
                                            identity=ident[0:R, 0:R]), reads=[tmpkey, 'ident'], writes=[pk[pbank]])
        for j in range(nj):
            k.op('dve', lambda: V.tensor_copy(out=dst_fn(j), in_=ps[pbank][:, j * R:(j + 1) * R]),
                 reads=[pk[pbank]], writes=['fm_dst'])

    def phase_mod():
        tmp = k.sb("rtmp", [128, D], F32)
        scT = k.sb("scT", [128, KC, 2], BF16)
        scf = k.sb("scf", [128, KC, 2], F32)
        bmT = k.sb("bmT", [128, DEPTH, 144], F32)
        c2 = k.sb("c2", [2, D], F32)
        k.dma('sp', c2[:], cond, writes=['c2'])
        k.op('act', lambda: A.activation(out=tmp[0:2, :], in_=c2[0:2, :], func=AF.Silu), reads=['c2'], writes=['rtmp'])
        for j in range(KC):
            k.op('pe', lambda: PE.transpose(out=ps[0][:, j * 2:(j + 1) * 2], in_=tmp[0:2, j * 128:(j + 1) * 128],
                                            identity=ident[0:2, 0:2]), reads=['rtmp', 'ident'], writes=['ps0'])
        k.op('dve', lambda: V.tensor_copy(out=scT[:].rearrange("p a b -> p (a b)"), in_=ps[0][:, 0:32]), reads=['ps0'], writes=['scT'])
        rows_to_fm(W["final_g"].rearrange("(o n) -> o n", o=1), 1, D, lambda j: fgT[:, j, :], tmp, 'rtmp', 1)
        for l in range(DEPTH):
            rows_to_fm(W["norm_g"][l], 3, D, lambda j: ngT[:, l, j, :], tmp, 'rtmp', 2 + l)
            bm = W["b_mod"][l].rearrange("(c p) -> c p", p=128)
            k.dma('sp', tmp[0:128, 0:128], bm[0:128, :], writes=['rtmp'])
            k.dma('sp', tmp[0:16, 128:256], bm[128:144, :], writes=['rtmp'])
            k.op('pe', lambda: PE.transpose(out=ps[4][:, 0:128], in_=tmp[0:128, 0:128], identity=ident[:]), reads=['rtmp', 'ident'], writes=['ps4'])
            k.op('pe', lambda: PE.transpose(out=ps[4][:, 128:144], in_=tmp[0:16, 128:256], identity=ident[0:16, 0:16]), reads=['rtmp', 'ident'], writes=['ps4'])
            k.op('dve', lambda: V.tensor_copy(out=bmT[:, l, :], in_=ps[4][:, 0:144]), reads=['ps4'], writes=['bmT'])
        stg = Ring(k, "wmst", [128, KC, 384], F32, 2)
        wbf = Ring(k, "wmbf", [128, KC, 384], BF16, 2)
        nb = 0
        for l in range(DEPTH):
            wm = W["w_mod"][l].rearrange("(kc p) n -> p kc n", p=128)
            for blk in range(48):
                st, sk = stg.next()
                wb, wk = wbf.next()
                k.dma('sp', st[:], wm[:, :, blk * 384:(blk + 1) * 384], writes=[sk])
                k.op('pool', lambda: G.tensor_copy(out=wb[:], in_=st[:]), reads=[sk], writes=[wk])
                pb = 5 + (nb % 2)
                nb += 1
                for q in range(3):
                    for kc in range(KC):
                        k.op('pe', lambda: PE.matmul(ps[pb][:, q * 2:(q + 1) * 2], wb[:, kc, q * 128:(q + 1) * 128], scT[:, kc, :],
                                                     start=(kc == 0), stop=(kc == KC - 1)), reads=[wk, 'scT'], writes=[pk[pb]])
                k.op('dve', lambda: V.tensor_tensor(out=modv[:, l, blk * 3:(blk + 1) * 3, :],
                                                    in0=ps[pb][:, 0:6].rearrange("p (a b) -> p a b", b=2),
                                                    in1=bmT[:, l, blk * 3:(blk + 1) * 3].unsqueeze(2).to_broadcast([128, 3, 2]), op=OP.add),
                     reads=[pk[pb], 'bmT'], writes=['modv'])
        for l in range(DEPTH):
            for i in range(3):
                sc = modv[:, l, (3 * i + 1) * 16:(3 * i + 2) * 16, :]
                gt = modv[:, l, (3 * i + 2) * 16:(3 * i + 3) * 16, :]
                k.op('dve', lambda: V.tensor_scalar(out=scf[:], in0=sc, scalar1=1.0, scalar2=None, op0=OP.add), reads=['modv'], writes=['scf'])
                k.op('dve', lambda: V.tensor_tensor(out=gsv[:, l, i], in0=scf[:], in1=ngT[:, l, :, i:i + 1].to_broadcast([128, KC, 2]), op=OP.mult),
                     reads=['scf', 'fm_dst'], writes=['gsv'])
                k.op('dve', lambda: V.tensor_scalar(out=gtv[:, l, i], in0=gt, scalar1=(1.0 if i == 1 else 0.5), scalar2=None, op0=OP.mult),
                     reads=['modv'], writes=['gtv'])

    def shiftv(l, i):
        return modv[:, l, (3 * i) * 16:(3 * i + 1) * 16, :]

    def phase_in():
        xt = Ring(k, "xt", [128, D], F32, 2)
        xo = Ring(k, "xo", [128, KC, 128], F32, 2)
        for tt in range(NTT):
            t_, tk = xt.next()
            o_, ok = xo.next()
            k.dma('sp', t_[:], xin[tt * 128:(tt + 1) * 128, :], writes=[tk])
            for g4 in range(4):
                pb = g4 % 4
                for q in range(4):
                    kc = g4 * 4 + q
                    k.op('pe', lambda: PE.transpose(out=ps[pb][:, q * 128:(q + 1) * 128], in_=t_[:, kc * 128:(kc + 1) * 128], identity=ident[:]),
                         reads=[tk, 'ident'], writes=[pk[pb]])
                eng = 'dve' if g4 % 2 == 0 else 'act'
                if eng == 'dve':
                    k.op('dve', lambda: V.tensor_copy(out=o_[:, g4 * 4:(g4 + 1) * 4, :].rearrange("p a b -> p (a b)"), in_=ps[pb][:]), reads=[pk[pb]], writes=[ok])
                else:
                    k.op('act', lambda: A.copy(out=o_[:, g4 * 4:(g4 + 1) * 4, :].rearrange("p a b -> p (a b)"), in_=ps[pb][:]), reads=[pk[pb]], writes=[ok])
            k.dma('sp', xs[:, :, tt * 128:(tt + 1) * 128].rearrange("kc p t -> p kc t"), o_[:], reads=[ok], writes=['xs'])

    def ada_norm(x, xkey, h, hkey, n, l, i, ci, sq, rs, pbank):
        for kc in range(KC):
            s_, sk = sq.next()
            k.op('act', lambda: A.activation(out=s_[:, 0:n], in_=x[:, kc, 0:n], func=AF.Square), reads=[xkey], writes=[sk])
            k.op('pe', lambda: PE.matmul(ps[pbank][:, 0:n], onesm[:], s_[:, 0:n], start=(kc == 0), stop=(kc == KC - 1)),
                 reads=[sk, 'onesm'], writes=[pk[pbank]])
        k.op('act', lambda: A.activation(out=rs[:, 0:n], in_=ps[pbank][:, 0:n], func=AF.Sqrt, bias=epsr[:, 0:1], scale=1.0), reads=[pk[pbank], 'epsr'], writes=['rs'])
        k.op('dve', lambda: V.reciprocal(out=rs[:, 0:n], in_=rs[:, 0:n]), reads=['rs'], writes=['rs'])
        for kc in range(KC):
            s_, sk = sq.next()
            k.op('dve', lambda: V.tensor_tensor(out=s_[:, 0:n], in0=x[:, kc, 0:n], in1=rs[:, 0:n], op=OP.mult), reads=[xkey, 'rs'], writes=[sk])
            if l is None:
                k.op('act', lambda: A.activation(out=h[:, kc, 0:n], in_=s_[:, 0:n], func=AF.Identity, scale=fgT[:, kc, 0:1], bias=0.0),
                     reads=[sk, 'fm_dst'], writes=[hkey])
            else:
                k.op('act', lambda: A.activation(out=h[:, kc, 0:n], in_=s_[:, 0:n], func=AF.Identity, scale=gsv[:, l, i, kc, ci:ci + 1],
                                                 bias=shiftv(l, i)[:, kc, ci:ci + 1]), reads=[sk, 'gsv', 'modv'], writes=[hkey])

    def phase_ffn(l, i):
        fi = 0 if i == 0 else 1
        wi = W["ffn_w_in"][l, fi].rearrange("(kc p) n -> p kc n", p=128)
        wo = W["ffn_w_out"][l, fi].rearrange("(j p) n -> p j n", p=128)
        x = k.sb("x", [128, KC, 512], F32)
        h = k.sb("h", [128, KC, NT], BF16)
        sq = Ring(k, "sq", [128, 512], F32, 2)
        rs = k.sb("rs", [128, 512], F32)
        sg = Ring(k, "sg", [128, 512], F32, 2)
        wst = Ring(k, "wst", [128, KC, 256], F32, 2)
        wbf = Ring(k, "wbf", [128, KC, 256], BF16, 2)
        hbo = Ring(k, "hbo", [128, 512], BF16, 3)
        for tg in range(3):
            ci = 0 if tg < 2 else 1
            ts = slice(tg * 512, (tg + 1) * 512)
            k.dma('sp', x[:], xs[:, :, ts].rearrange("kc p t -> p kc t"), reads=['xs'], writes=['x'])
            ada_norm(x, 'x', h[:, :, ts], 'h%d' % tg, 512, l, i, ci, sq, rs, 0)
        n = 0
        for j in range(NJ):
            st, sk = wst.next()
            wb, wk = wbf.next()
            k.dma('sp', st[:, :, 0:128], wi[:, :, j * 128:(j + 1) * 128], writes=[sk + 'a'])
            k.dma('sp', st[:, :, 128:256], wi[:, :, DFF + j * 128:DFF + (j + 1) * 128], writes=[sk + 'b'])
            k.op('pool', lambda: G.tensor_copy(out=wb[:], in_=st[:]), reads=[sk + 'a', sk + 'b'], writes=[wk])
            for tg in range(3):
                ts = slice(tg * 512, (tg + 1) * 512)
                pg = 1 + 2 * (n % 3)
                pu = pg + 1
                n += 1
                for kc in range(KC):
                    k.op('pe', lambda: PE.matmul(ps[pg][:], wb[:, kc, 0:128], h[:, kc, ts], start=(kc == 0), stop=(kc == KC - 1)),
                         reads=[wk, 'h%d' % tg], writes=[pk[pg]])
                for kc in range(KC):
                    k.op('pe', lambda: PE.matmul(ps[pu][:], wb[:, kc, 128:256], h[:, kc, ts], start=(kc == 0), stop=(kc == KC - 1)),
                         reads=[wk, 'h%d' % tg], writes=[pk[pu]])
                s_, sgk = sg.next()
                o_, okey = hbo.next()
                k.op('act', lambda: A.activation(out=s_[:], in_=ps[pg][:], func=AF.Silu), reads=[pk[pg]], writes=[sgk])
                k.op('dve', lambda: V.tensor_tensor(out=o_[:], in0=s_[:], in1=ps[pu][:], op=OP.mult), reads=[sgk, pk[pu]], writes=[okey])
                k.dma('sp', hbT[j, :, ts], o_[:], reads=[okey], writes=['hbT'], semkey=(okey, 'st'))
        phase_reset()
        x = k.sb("x", [128, KC, 512], F32)
        hb = k.sb("hb", [128, NJ, 512], BF16)
        ost = Ring(k, "ost", [128, 22, 128], F32, 2)
        obf = Ring(k, "obf", [128, NJ, 128], BF16, 2)
        for tg in range(3):
            ci = 0 if tg < 2 else 1
            ts = slice(tg * 512, (tg + 1) * 512)
            k.dma('sp', x[:], xs[:, :, ts].rearrange("kc p t -> p kc t"), reads=['xs'], writes=['x'])
            k.dma('sp', hb[:], hbT[:, :, ts].rearrange("j p t -> p j t"), reads=['hbT'], writes=['hb'])
            for oc in range(KC):
                ob, obk = obf.next()
                for half in range(2):
                    st, sk = ost.next()
                    k.dma('sp', st[:], wo[:, half * 22:(half + 1) * 22, oc * 128:(oc + 1) * 128], writes=[sk])
                    k.op('pool', lambda: G.tensor_copy(out=ob[:, half * 22:(half + 1) * 22, :], in_=st[:]), reads=[sk], writes=[obk + '_%d' % half])
                po = 5 + (oc % 2)
                for j in range(NJ):
                    k.op('pe', lambda: PE.matmul(ps[po][:], ob[:, j, :], hb[:, j, :], start=(j == 0), stop=(j == NJ - 1)),
                         reads=[obk + '_%d' % (j // 22), 'hb'], writes=[pk[po]])
                k.op('dve', lambda: V.scalar_tensor_tensor(out=x[:, oc, :], in0=ps[po][:], scalar=gtv[:, l, i, oc, ci:ci + 1], in1=x[:, oc, :],
                                                           op0=OP.mult, op1=OP.add), reads=[pk[po], 'gtv', 'x'], writes=['x'])
            k.dma('sp', xs[:, :, ts].rearrange("kc p t -> p kc t"), x[:], reads=['x'], writes=['xs'])

    def phase_out():
        x = k.sb("x", [128, KC, 128], F32)
        h = k.sb("hf", [128, KC, 128], F32)
        sq = Ring(k, "sq", [128, 128], F32, 2)
        rs = k.sb("rs", [128, 128], F32)
        yo = Ring(k, "yo", [128, D], F32, 2)
        for tt in range(NTT):
            ts = slice(tt * 128, (tt + 1) * 128)
            k.dma('sp', x[:], xs[:, :, ts].rearrange("kc p t -> p kc t"), reads=['xs'], writes=['x'])
            ada_norm(x, 'x', h, 'hf', 128, None, None, None, sq, rs, 0)
            y_, yk = yo.next()
            for g4 in range(4):
                pb = 1 + g4
                for q in range(4):
                    kc = g4 * 4 + q
                    k.op('pe', lambda: PE.transpose(out=ps[pb][:, q * 128:(q + 1) * 128], in_=h[:, kc, :], identity=ident[:]),
                         reads=['hf', 'ident'], writes=[pk[pb]])
                if g4 % 2 == 0:
                    k.op('dve', lambda: V.tensor_copy(out=y_[:, g4 * 512:(g4 + 1) * 512], in_=ps[pb][:]), reads=[pk[pb]], writes=[yk])
                else:
                    k.op('act', lambda: A.copy(out=y_[:, g4 * 512:(g4 + 1) * 512], in_=ps[pb][:]), reads=[pk[pb]], writes=[yk])
            k.dma('sp', y_out[ts, :], y_[:], reads=[yk], semkey=('yout', tt % 2))


    def bcast_load(dst, src_row, key):
        k.dma('sp', dst, src_row.partition_broadcast(128), writes=[key])

    def evac(i, out, in_, reads, writes):
        if i % 2 == 0:
            k.op('dve', lambda: V.tensor_copy(out=out, in_=in_), reads=reads, writes=writes)
        else:
            k.op('act', lambda: A.copy(out=out, in_=in_), reads=reads, writes=writes)

    def phase_win(l):
        x = k.sb("x", [128, KC, 512], F32)
        h = k.sb("h", [128, KC, 512], BF16)
        sq = Ring(k, "sq", [128, 512], F32, 2)
        rs = k.sb("rs", [128, 512], F32)
        wst = Ring(k, "wst", [128, KC, 512], F32, 2)
        wbf = Ring(k, "wbf", [128, KC, 512], BF16, 2)
        ob = Ring(k, "ob", [128, 512], BF16, 3)
        of = Ring(k, "of", [128, 512], F32, 3)
        lng = k.sb("lng", [128, 512], F32)
        lnb = k.sb("lnb", [128, 512], F32)
        st1 = k.sb("st1", [128, 4], F32)
        zt = k.sb("zt", [64, CIN], F32)
        bcast_load(lng[:], W["sgu_ln_g"][l], 'lng')
        bcast_load(lnb[:], W["sgu_ln_b"][l], 'lnb')
        k.op('pool', lambda: G.memset(zt[:], 0.0), writes=['zt'])
        k.dma('sp', zc[0:64, :], zt[:], reads=['zt'], writes=['zc'])
        k.dma('sp', zc[64 + NT:128 + NT, :], zt[:], reads=['zt'], writes=['zc'])
        wi = W["w_in"][l].rearrange("(kc p) n -> p kc n", p=128)
        blocks = [(0, 512), (512, 512), (1024, 512)] + [(1536 + 512 * b, 512 if b < 6 else 416) for b in range(7)]
        nev = 0
        for tg in range(3):
            ci = 0 if tg < 2 else 1
            ts = slice(tg * 512, (tg + 1) * 512)
            k.dma('sp', x[:], xs[:, :, ts].rearrange("kc p t -> p kc t"), reads=['xs'], writes=['x'])
            ada_norm(x, 'x', h, 'h', 512, l, 1, ci, sq, rs, 0)
            for bi, (c0, ncol) in enumerate(blocks):
                st, sk = wst.next()
                wb, wk = wbf.next()
                k.dma('sp', st[:, :, 0:ncol], wi[:, :, c0:c0 + ncol], writes=[sk])
                k.op('pool', lambda: G.tensor_copy(out=wb[:, :, 0:ncol], in_=st[:, :, 0:ncol]), reads=[sk], writes=[wk])
                for q in range(4):
                    pb = 1 + (nev % 6)
                    nev += 1
                    if bi in (0, 2):
                        for kc in range(KC):
                            k.op('pe', lambda: PE.matmul(ps[pb][:], wb[:, kc, q * 128:(q + 1) * 128], h[:, kc, :], start=(kc == 0), stop=(kc == KC - 1)),
                                 reads=[wk, 'h'], writes=[pk[pb]])
                        o_, okey = ob.next()
                        if bi == 0:
                            k.op('act', lambda: A.activation(out=o_[:], in_=ps[pb][:], func=AF.Gelu_apprx_tanh), reads=[pk[pb]], writes=[okey])
                            k.dma('sp', uT[q, :, ts], o_[:], reads=[okey], writes=['uT'], semkey=(okey, 'st'))
                        else:
                            evac(nev, o_[:], ps[pb][:], [pk[pb]], [okey])
                            k.dma('sp', zbT[q, :, ts], o_[:], reads=[okey], writes=['zbT'], semkey=(okey, 'st'))
                    else:
                        trow = tg * 512 + q * 128
                        for kc in range(KC):
                            k.op('pe', lambda: PE.matmul(ps[pb][:, 0:ncol], h[:, kc, q * 128:(q + 1) * 128], wb[:, kc, 0:ncol], start=(kc == 0), stop=(kc == KC - 1)),
                                 reads=[wk, 'h'], writes=[pk[pb]])
                        f_, fkey = of.next()
                        if bi == 1:
                            k.op('act', lambda: A.activation(out=f_[:], in_=ps[pb][:], func=AF.Gelu_apprx_tanh), reads=[pk[pb]], writes=[fkey])
                            k.op('dve', lambda: V.tensor_reduce(out=st1[:, 0:1], in_=f_[:], axis=AX.X, op=OP.add), reads=[fkey], writes=['st1'])
                            k.op('dve', lambda: V.tensor_scalar(out=st1[:, 1:2], in0=st1[:, 0:1], scalar1=1.0 / 512, scalar2=None, op0=OP.mult), reads=['st1'], writes=['st1'])
                            k.op('dve', lambda: V.tensor_scalar(out=f_[:], in0=f_[:], scalar1=st1[:, 1:2], scalar2=None, op0=OP.subtract), reads=[fkey, 'st1'], writes=[fkey])
                            s_, sk2 = sq.next()
                            k.op('dve', lambda: V.tensor_tensor(out=s_[:], in0=f_[:], in1=f_[:], op=OP.mult), reads=[fkey], writes=[sk2])
                            k.op('dve', lambda: V.tensor_reduce(out=st1[:, 2:3], in_=s_[:], axis=AX.X, op=OP.add), reads=[sk2], writes=['st1'])
                            k.op('act', lambda: A.activation(out=st1[:, 3:4], in_=st1[:, 2:3], func=AF.Sqrt, bias=epsl[:, 0:1], scale=1.0 / 512), reads=['st1', 'epsl'], writes=['st1'])
                            k.op('dve', lambda: V.reciprocal(out=st1[:, 3:4], in_=st1[:, 3:4]), reads=['st1'], writes=['st1'])
                            k.op('dve', lambda: V.scalar_tensor_tensor(out=f_[:], in0=f_[:], scalar=st1[:, 3:4], in1=lng[:], op0=OP.mult, op1=OP.mult),
                                 reads=[fkey, 'st1', 'lng'], writes=[fkey])
                            o_, okey = ob.next()
                            k.op('dve', lambda: V.tensor_tensor(out=o_[:], in0=f_[:], in1=lnb[:], op=OP.add), reads=[fkey, 'lnb'], writes=[okey])
                            k.dma('sp', vn[trow:trow + 128, :], o_[:], reads=[okey], writes=['vn'], semkey=(okey, 'st'))
                        else:
                            evac(nev, f_[:, 0:ncol], ps[pb][:, 0:ncol], [pk[pb]], [fkey])
                            cz = c0 - 1536
                            k.dma('sp', zc[64 + trow:64 + trow + 128, cz:cz + ncol], f_[:, 0:ncol], reads=[fkey], writes=['zc'], semkey=(fkey, 'st'))

    def phase_gmlp(l):
        wsf = k.sb("wsf", [128, 4, 128], F32)
        wsT = k.sb("wsT", [128, 4, 128], BF16)
        bsb = k.sb("bsb", [128, 512], F32)
        vt = Ring(k, "vt", [128, 512], BF16, 2)
        ut = Ring(k, "ut", [128, 4, 128], BF16, 2)
        tm = Ring(k, "tm", [128, 512], F32, 2)
        yo = Ring(k, "yo", [128, 4, 128], BF16, 2)
        k.dma('sp', wsf[:], W["sgu_w"][l].rearrange("h t s -> t h s"), writes=['wsf'])
        bcast_load(bsb[:], W["sgu_b"][l].rearrange("h t -> (h t)"), 'bsb')
        for hh in range(4):
            k.op('pe', lambda: PE.transpose(out=ps[0][:, hh * 128:(hh + 1) * 128], in_=wsf[:, hh, :], identity=ident[:]), reads=['wsf', 'ident'], writes=['ps0'])
        k.op('dve', lambda: V.tensor_copy(out=wsT[:].rearrange("p a b -> p (a b)"), in_=ps[0][:]), reads=['ps0'], writes=['wsT'])
        for tt in range(NTT):
            ts = slice(tt * 128, (tt + 1) * 128)
            v_, vk = vt.next()
            u_, uk = ut.next()
            k.dma('sp', v_[:], vn[ts, :], reads=['vn'], writes=[vk])
            k.dma('sp', u_[:], uT[:, :, ts].rearrange("q p t -> p q t"), reads=['uT'], writes=[uk])
            pb = 1 + tt % 2
            for hh in range(4):
                k.op('pe', lambda: PE.matmul(ps[pb][:, hh * 128:(hh + 1) * 128], v_[:, hh * 128:(hh + 1) * 128], wsT[:, hh, :], start=True, stop=True),
                     reads=[vk, 'wsT'], writes=[pk[pb]])
            t_, tk = tm.next()
            k.op('dve', lambda: V.tensor_tensor(out=t_[:], in0=ps[pb][:], in1=bsb[:], op=OP.add), reads=[pk[pb], 'bsb'], writes=[tk])
            y_, yk = yo.next()
            k.op('dve', lambda: V.tensor_tensor(out=y_[:].rearrange("p a b -> p (a b)"), in0=t_[:], in1=u_[:].rearrange("p a b -> p (a b)"), op=OP.mult),
                 reads=[tk, uk], writes=[yk])
            k.dma('sp', ymixT[0:4, :, ts].rearrange("q p t -> p q t"), y_[:], reads=[yk], writes=['ymixT'], semkey=(yk, 'st'))

    def phase_fnet(l):
        stg = k.sb("stg", [128, 8, 1024], F32)
        CLb = k.sb("CLb", [128, 8, 1024], BF16)
        SLb = k.sb("SLb", [128, 8, 1024], BF16)
        CBb = k.sb("CBb", [128, 2, 256], BF16)
        SBb = k.sb("SBb", [128, 2, 256], BF16)
        CSb = k.sb("CSb", [128, 256], BF16)
        Atm = k.sb("Atm", [128, NTT, 4, 256], BF16)
        zb = Ring(k, "zb", [128, 4, 128], BF16, 2)
        yo = Ring(k, "yo", [128, 512], BF16, 2)
        for src, dst, key, shp in [(CL_in, CLb, 'CLb', (8, 1024)), (nSL_in, SLb, 'SLb', (8, 1024)), (CB_in, CBb, 'CBb', (2, 256)), (nSB_in, SBb, 'SBb', (2, 256))]:
            a, b = shp
            k.dma('sp', stg[:, 0:a, 0:b], src.rearrange("(sc p) t -> p sc t", p=128), writes=['stg'])
            k.op('pool', lambda: G.tensor_copy(out=dst[:], in_=stg[:, 0:a, 0:b]), reads=['stg'], writes=[key])
        k.dma('sp', stg[:, 0, 0:256], CSd_in, writes=['stg'])
        k.op('pool', lambda: G.tensor_copy(out=CSb[:], in_=stg[:, 0, 0:256]), reads=['stg'], writes=['CSb'])
        for tt in range(NTT):
            ts = slice(tt * 128, (tt + 1) * 128)
            z_, zk = zb.next()
            k.dma('sp', z_[:], zbT[:, :, ts].rearrange("q p t -> p q t"), reads=['zbT'], writes=[zk])
            for half in range(2):
                pb = 1 + (2 * tt + half) % 4
                for gg in range(2):
                    g_ = half * 2 + gg
                    k.op('pe', lambda: PE.matmul(ps[pb][:, gg * 256:(gg + 1) * 256], z_[:, g_, :], CSb[:], start=True, stop=True), reads=[zk, 'CSb'], writes=[pk[pb]])
                evac(tt + half, Atm[:, tt, half * 2:half * 2 + 2, :].rearrange("p a b -> p (a b)"), ps[pb][:], [pk[pb]], ['Atm%d' % tt])
        n = 0
        for g_ in range(4):
            for half in range(2):
                pb = 5 + n % 2
                n += 1
                cs_ = slice(half * 512, (half + 1) * 512)
                for sc in range(8):
                    k.op('pe', lambda: PE.matmul(ps[pb][:], Atm[:, sc, g_, 0:128], CLb[:, sc, cs_], start=(sc == 0), stop=False), reads=['Atm%d' % sc, 'CLb'], writes=[pk[pb]])
                    k.op('pe', lambda: PE.matmul(ps[pb][:], Atm[:, sc, g_, 128:256], SLb[:, sc, cs_], start=False, stop=(sc == 7)), reads=['Atm%d' % sc, 'SLb'], writes=[pk[pb]])
                y_, yk = yo.next()
                evac(n, y_[:], ps[pb][:], [pk[pb]], [yk])
                k.dma('sp', ymixT[4 + g_, :, cs_], y_[:], reads=[yk], writes=['ymixT'], semkey=(yk, 'st'))
            for seg in range(2):
                pb = 5 + n % 2
                n += 1
                for sc in range(2):
                    tt = 8 + seg * 2 + sc
                    k.op('pe', lambda: PE.matmul(ps[pb][:, 0:256], Atm[:, tt, g_, 0:128], CBb[:, sc, :], start=(sc == 0), stop=False), reads=['Atm%d' % tt, 'CBb'], writes=[pk[pb]])
                    k.op('pe', lambda: PE.matmul(ps[pb][:, 0:256], Atm[:, tt, g_, 128:256], SBb[:, sc, :], start=False, stop=(sc == 1)), reads=['Atm%d' % tt, 'SBb'], writes=[pk[pb]])
                y_, yk = yo.next()
                evac(n, y_[:, 0:256], ps[pb][:, 0:256], [pk[pb]], [yk])
                k.dma('sp', ymixT[4 + g_, :, 1024 + seg * 256:1280 + seg * 256], y_[:, 0:256], reads=[yk], writes=['ymixT'], semkey=(yk, 'st'))

    def phase_shift(l):
        HC = CIN // 2
        mub = k.sb("mub", [128, HC], F32)
        omm = k.sb("omm", [128, HC], F32)
        cm = k.sb("cm", [128, 2, 4, HC], F32)
        zl = [Ring(k, "zl%d" % o, [128, HC], F32, 2) for o in range(5)]
        acc = Ring(k, "acc", [128, HC], F32, 2)
        tmp = Ring(k, "tmp", [128, HC], F32, 2)
        offs = [-1, 1, -64, 64]
        for hc in range(2):
            cs_ = slice(hc * HC, (hc + 1) * HC)
            bcast_load(mub[:], W["shift_mu"][l][cs_], 'mub')
            for ty in range(2):
                for o in range(4):
                    bcast_load(cm[:, ty, o, :], ind_in[ty, o, cs_], 'cm')
            k.op('dve', lambda: V.tensor_scalar(out=omm[:], in0=mub[:], scalar1=-1.0, scalar2=1.0, op0=OP.mult, op1=OP.add), reads=['mub'], writes=['omm'])
            k.op('dve', lambda: V.tensor_tensor(out=cm[:].rearrange("p a b c -> p (a b) c"), in0=cm[:].rearrange("p a b c -> p (a b) c"),
                                                in1=mub[:].unsqueeze(1).to_broadcast([128, 8, HC]), op=OP.mult), reads=['cm', 'mub'], writes=['cm'])
            for tt in range(NTT):
                ty = 0 if tt < 8 else 1
                r0 = 64 + tt * 128
                a_, ak = acc.next()
                z0, z0k = zl[4].next()
                k.dma('sp', z0[:], zc[r0:r0 + 128, cs_], reads=['zc'], writes=[z0k])
                k.op('dve', lambda: V.tensor_tensor(out=a_[:], in0=z0[:], in1=omm[:], op=OP.mult), reads=[z0k, 'omm'], writes=[ak])
                for o in range(4 if ty == 0 else 2):
                    zo, zok = zl[o].next()
                    k.dma('sp', zo[:], zc[r0 + offs[o]:r0 + offs[o] + 128, cs_], reads=['zc'], writes=[zok])
                    t_, tk = tmp.next()
                    k.op('pool', lambda: G.tensor_tensor(out=t_[:], in0=zo[:], in1=cm[:, ty, o, :], op=OP.mult), reads=[zok, 'cm'], writes=[tk])
                    k.op('dve', lambda: V.scalar_tensor_tensor(out=a_[:], in0=t_[:], scalar=valid[:, tt, o:o + 1], in1=a_[:], op0=OP.mult, op1=OP.add),
                         reads=[ak, tk, 'valid'], writes=[ak])
                k.dma('sp', ztm[tt * 128:(tt + 1) * 128, cs_], a_[:], reads=[ak], writes=['ztm'], semkey=(ak, 'st'))


    CDEC = 0.6065306597126334

    def phase_scan(l):
        import os
        SS = int(os.environ.get('SCAN_STOP', '99'))
        NTS = int(os.environ.get('SCAN_TILES', str(NTT)))
        bc = {}
        for nm, src in [('w0f', W["decay_w0"][l, 0]), ('w0b', W["decay_w0"][l, 1]), ('a0f', W["iclr_a0"][l, 0]), ('a0b', W["iclr_a0"][l, 1]),
                        ('kk', W["k_k"][l]), ('ka', W["k_a"][l]), ('rk', W["r_k"][l].rearrange("h n -> (h n)"))]:
            bc[nm] = k.sb("bc_" + nm, [128, 1024], F32)
            bcast_load(bc[nm][:], src, 'bc_' + nm)
        f1 = k.sb("f1", [128, 1024], F32)
        f2 = k.sb("f2", [128, 1024], F32)
        stg = f1
        w2b = k.sb("w2b", [128, 1024], BF16)
        a2b = k.sb("a2b", [128, 1024], BF16)
        g2b = k.sb("g2b", [128, 2, 1024], BF16)
        idb = k.sb("idb", [128, 128], BF16)
        onec = k.sb("onec", [128, 1], F32)
        k.op('dve', lambda: V.memset(onec[:], 1.0), writes=['onec'])
        k.op('dve', lambda: V.tensor_copy(out=idb[:], in_=ident[:]), reads=['ident'], writes=['idb'])
        for src, dst, key in [(W["decay_w2"][l].rearrange("d r c -> (d r) c"), w2b[:], 'w2b'), (W["iclr_a2"][l].rearrange("d r c -> (d r) c"), a2b[:], 'a2b'),
                              (W["gate_w2"][l][0:128, :], g2b[:, 0, :], 'g2b')]:
            k.dma('sp', stg[:], src, writes=['f1'])
            k.op('pool', lambda: G.tensor_copy(out=dst, in_=stg[:]), reads=['f1'], writes=[key])
        k.dma('sp', stg[0:32, :], W["gate_w2"][l][128:160, :], writes=['f1'])
        k.op('pool', lambda: G.tensor_copy(out=g2b[0:32, 1, :], in_=stg[0:32, :]), reads=['f1'], writes=['g2b'])
        z = k.sb("z", [128, CIN], F32)
        t128 = k.sb("t128", [128, 416], F32)
        trT = k.sb("trT", [128, 4, 128], BF16)
        sig = k.sb("sig", [128, 2, 1024], F32)
        av = k.sb("av", [128, 2, 1024], F32)
        kkv = k.sb("kkv", [128, 1024], F32)
        kmv = k.sb("kmv", [128, 2, 1024], F32)
        bv = k.sb("bv", [128, 2, 1024], F32)
        sm = k.sb("sm", [128, 64], F32)
        Vb = k.sb("Vb", [128, 1024], BF16)
        At = k.sb("At", [128, 1024], F32)
        Bt = k.sb("Bt", [128, 1024], F32)
        Kt = k.sb("Kt", [128, 1024], F32)
        Rt = k.sb("Rt", [128, 1024], F32)
        Btb = k.sb("Btb", [128, 1024], BF16)
        Ktb = k.sb("Ktb", [128, 1024], BF16)
        Rtb = k.sb("Rtb", [128, 1024], BF16)
        XT4 = k.sb("XT4", [128, 4, 8, 128], BF16)
        M5 = k.sb("M5", [128, 5, 16, 128], BF16)
        Xp = k.sb("Xp", [128, 2, 2, 16, 128], BF16)
        Zf = k.sb("Zf", [128, 16, 128], F32)
        Zb = k.sb("Zb", [128, 16, 128], BF16)
        nZb = k.sb("nZb", [128, 16, 128], BF16)
        o64 = Ring(k, "o64", [64, 16, 128], F32, 1)
        oy = Ring(k, "oy", [128, 1024], F32, 1)
        od = Ring(k, "od", [64, 16], F32, 2)
        pbn = [0]

        def nb():
            pbn[0] += 1
            return pbn[0] % 8

        for tt in range(NTS):
            ts = slice(tt * 128, (tt + 1) * 128)
            k.dma('sp', z[:], ztm[ts, :], reads=['ztm'], writes=['z'])
            r_ = z[:, 0:1024]
            k_ = z[:, 1024:2048]
            v_ = z[:, 2048:3072]
            k.op('act', lambda: A.activation(out=t128[:, 0:128], in_=z[:, 3072:3200], func=AF.Tanh), reads=['z'], writes=['t128'])
            k.op('act', lambda: A.activation(out=t128[:, 256:416], in_=z[:, 3328:3488], func=AF.Sigmoid), reads=['z'], writes=['t128'])
            k.op('dve', lambda: V.tensor_copy(out=t128[:, 128:256], in_=z[:, 3200:3328]), reads=['z'], writes=['t128'])
            p0 = nb()
            for q, (c0, n) in enumerate([(0, 128), (128, 128), (256, 128), (384, 32)]):
                k.op('pe', lambda: PE.transpose(out=ps[p0][0:n, q * 128:(q + 1) * 128], in_=t128[:, c0:c0 + n], identity=ident[:]), reads=['t128', 'ident'], writes=[pk[p0]])
            k.op('dve', lambda: V.tensor_copy(out=trT[:, 0:3, :].rearrange("p a b -> p (a b)"), in_=ps[p0][:, 0:384]), reads=[pk[p0]], writes=['trT'])
            k.op('dve', lambda: V.tensor_copy(out=trT[0:32, 3, :], in_=ps[p0][0:32, 384:512]), reads=[pk[p0]], writes=['trT'])
            for d in range(2):
                for (wsrc, wkey, bsrc, dst, q) in [(w2b, 'w2b', bc['w0f' if d == 0 else 'w0b'], sig, 0), (a2b, 'a2b', bc['a0f' if d == 0 else 'a0b'], av, 1)]:
                    for hf in range(2):
                        p1 = nb()
                        cs_ = slice(hf * 512, (hf + 1) * 512)
                        k.op('pe', lambda: PE.matmul(ps[p1][:], trT[d * 64:(d + 1) * 64, q, :], wsrc[d * 64:(d + 1) * 64, cs_], start=True, stop=True),
                             reads=['trT', wkey], writes=[pk[p1]])
                        k.op('dve', lambda: V.tensor_tensor(out=f1[:, cs_], in0=ps[p1][:], in1=bsrc[:, cs_], op=OP.add), reads=[pk[p1], 'bc_w0f', 'bc_w0b', 'bc_a0f', 'bc_a0b'], writes=['f1'])
                    k.op('act', lambda: A.activation(out=dst[:, d, :], in_=f1[:], func=AF.Sigmoid), reads=['f1'], writes=['sig' if q == 0 else 'av'])
            g_, gk = oy.next()
            for hf in range(2):
                p1 = nb()
                cs_ = slice(hf * 512, (hf + 1) * 512)
                k.op('pe', lambda: PE.matmul(ps[p1][:], trT[:, 2, :], g2b[:, 0, cs_], start=True, stop=False), reads=['trT', 'g2b'], writes=[pk[p1]])
                k.op('pe', lambda: PE.matmul(ps[p1][:], trT[0:32, 3, :], g2b[0:32, 1, cs_], start=False, stop=True), reads=['trT', 'g2b'], writes=[pk[p1]])
                evac(hf, g_[:, cs_], ps[p1][:], [pk[p1]], [gk])
            k.dma('sp', scGt[ts, :], g_[:], reads=[gk], writes=['scGt'], semkey=(gk, 'st'))
            k.op('dve', lambda: V.tensor_tensor(out=kkv[:], in0=k_, in1=bc['kk'][:], op=OP.mult), reads=['z', 'bc_kk'], writes=['kkv'])
            k.op('pool', lambda: G.tensor_tensor(out=f2[:], in0=kkv[:], in1=kkv[:], op=OP.mult), reads=['kkv'], writes=['f2'])
            k.op('dve', lambda: V.tensor_reduce(out=sm[:, 0:16], in_=f2[:].rearrange("p (h n) -> p h n", n=64), axis=AX.X, op=OP.add), reads=['f2'], writes=['sm'])
            k.op('act', lambda: A.activation(out=sm[:, 0:16], in_=sm[:, 0:16], func=AF.Sqrt), reads=['sm'], writes=['sm'])
            k.op('dve', lambda: V.tensor_scalar(out=sm[:, 0:16], in0=sm[:, 0:16], scalar1=1e-12, scalar2=None, op0=OP.max), reads=['sm'], writes=['sm'])
            k.op('dve', lambda: V.reciprocal(out=sm[:, 0:16], in_=sm[:, 0:16]), reads=['sm'], writes=['sm'])
            k.op('dve', lambda: V.tensor_tensor(out=kkv[:].rearrange("p (h n) -> p h n", n=64), in0=kkv[:].rearrange("p (h n) -> p h n", n=64),
                                                in1=sm[:, 0:16].unsqueeze(2).to_broadcast([128, 16, 64]), op=OP.mult), reads=['kkv', 'sm'], writes=['kkv'])
            k.op('pool', lambda: G.tensor_tensor(out=f2[:], in0=r_, in1=bc['rk'][:], op=OP.mult), reads=['z', 'bc_rk'], writes=['f2'])
            for d in range(2):
                k.op('dve', lambda: V.scalar_tensor_tensor(out=f1[:], in0=av[:, d, :], scalar=-1.0, in1=bc['ka'][:], op0=OP.add, op1=OP.mult), reads=['av', 'bc_ka'], writes=['f1'])
                k.op('dve', lambda: V.scalar_tensor_tensor(out=kmv[:, d, :], in0=f1[:], scalar=1.0, in1=k_, op0=OP.add, op1=OP.mult), reads=['f1', 'z'], writes=['kmv'])
                k.op('pool', lambda: G.tensor_tensor(out=bv[:, d, :], in0=kkv[:], in1=av[:, d, :], op=OP.mult), reads=['kkv', 'av'], writes=['bv'])
                k.op('dve', lambda: V.tensor_tensor(out=f1[:], in0=kmv[:, d, :], in1=f2[:], op=OP.mult), reads=['kmv', 'f2'], writes=['f1'])
                k.op('dve', lambda: V.tensor_reduce(out=sm[:, 16 + 16 * d:32 + 16 * d], in_=f1[:].rearrange("p (h n) -> p h n", n=64), axis=AX.X, op=OP.add), reads=['f1'], writes=['sm'])
            k.op('dve', lambda: V.tensor_tensor(out=sm[:, 48:64], in0=sm[:, 16:32], in1=sm[:, 32:48], op=OP.add), reads=['sm'], writes=['sm'])
            k.dma('sp', scB[ts, :], sm[:, 48:64], reads=['sm'], writes=['scB'], semkey=('scB',))
            k.op('pool', lambda: G.tensor_copy(out=Vb[:], in_=v_), reads=['z'], writes=['Vb'])
            for d in range(2 if SS > 1 else 0):
                tri_i = tri[:, 0 if d == 0 else 2, :]
                tri_e = tri[:, 1 if d == 0 else 3, :]
                m_st = tri[:, 1 if d == 0 else 3, :]
                m_ts = tri[:, 3 if d == 0 else 1, :]
                m_in = tri[:, 0 if d == 0 else 2, :]
                pe_ = [nb(), nb()]
                pi_ = [nb(), nb()]
                for hf in range(2):
                    cs_ = slice(hf * 512, (hf + 1) * 512)
                    k.op('pe', lambda: PE.matmul(ps[pe_[hf]][:], tri_e, sig[:, d, cs_], start=True, stop=True), reads=['tri', 'sig'], writes=[pk[pe_[hf]]])
                    k.op('pe', lambda: PE.matmul(ps[pi_[hf]][:], tri_i, sig[:, d, cs_], start=True, stop=True), reads=['tri', 'sig'], writes=[pk[pi_[hf]]])
                for hf in range(2):
                    cs_ = slice(hf * 512, (hf + 1) * 512)
                    k.op('act', lambda: A.activation(out=f1[:, cs_], in_=ps[pe_[hf]][:], func=AF.Exp, scale=-CDEC), reads=[pk[pe_[hf]]], writes=['f1'])
                    k.op('dve', lambda: V.tensor_tensor(out=At[:, cs_], in0=kkv[:, cs_], in1=f1[:, cs_], op=OP.mult), reads=['kkv', 'f1'], writes=['At'])
                    k.op('act', lambda: A.activation(out=f2[:, cs_], in_=ps[pi_[hf]][:], func=AF.Exp, scale=CDEC), reads=[pk[pi_[hf]]], writes=['f2'])
                    k.op('dve', lambda: V.tensor_tensor(out=Bt[:, cs_], in0=bv[:, d, cs_], in1=f2[:, cs_], op=OP.mult), reads=['bv', 'f2'], writes=['Bt'])
                    k.op('pool', lambda: G.tensor_tensor(out=Kt[:, cs_], in0=kmv[:, d, cs_], in1=f2[:, cs_], op=OP.mult), reads=['kmv', 'f2'], writes=['Kt'])
                    k.op('act', lambda: A.activation(out=f1[:, cs_], in_=ps[pi_[hf]][:], func=AF.Exp, scale=-CDEC), reads=[pk[pi_[hf]]], writes=['f1'])
                    k.op('dve', lambda: V.tensor_tensor(out=Rt[:, cs_], in0=r_[:, cs_], in1=f1[:, cs_], op=OP.mult), reads=['z', 'f1'], writes=['Rt'])
                k.op('pool', lambda: G.tensor_copy(out=Btb[:], in_=Bt[:]), reads=['Bt'], writes=['Btb'])
                k.op('pool', lambda: G.tensor_copy(out=Ktb[:], in_=Kt[:]), reads=['Kt'], writes=['Ktb'])
                k.op('pool', lambda: G.tensor_copy(out=Rtb[:], in_=Rt[:]), reads=['Rt'], writes=['Rtb'])
                if SS <= 2:
                    continue
                p1 = nb()
                for hh in range(16):
                    k.op('pe', lambda: PE.matmul(ps[p1][0:64, hh:hh + 1], sig[:, d, hh * 64:(hh + 1) * 64], onec[:, 0:1], start=True, stop=True), reads=['sig', 'onec'], writes=[pk[p1]])
                d_, dk = od.next()
                k.op('act', lambda: A.activation(out=d_[:], in_=ps[p1][0:64, 0:16], func=AF.Exp, scale=-CDEC), reads=[pk[p1]], writes=[dk])
                k.dma('sp', scD[tt, d], d_[:], reads=[dk], writes=['scD'], semkey=(dk, 'st'))
                if SS <= 3:
                    continue
                for qi, (src, skey) in enumerate([(At, 'At'), (Bt, 'Bt'), (Kt, 'Kt'), (Rt, 'Rt')]):
                    for hf in range(2):
                        p1 = nb()
                        for q in range(4):
                            pr = hf * 4 + q
                            k.op('pe', lambda: PE.transpose(out=ps[p1][:, q * 128:(q + 1) * 128], in_=src[:, pr * 128:(pr + 1) * 128], identity=ident[:]),
                                 reads=[skey, 'ident'], writes=[pk[p1]])
                        evac(qi + hf, XT4[:, qi, hf * 4:hf * 4 + 4, :].rearrange("p a b -> p (a b)"), ps[p1][:], [pk[p1]], ['XT4'])

                def hpos(hh):
                    return (hh % 2) * 8 + hh // 2

                def fm(qi, hh):
                    return XT4[(hh % 2) * 64:(hh % 2) * 64 + 64, qi, hh // 2, :]
                if SS <= 4:
                    continue
                for mi, (lq, rq, msk) in enumerate([(1, 0, m_st), (0, 1, m_ts), (2, 0, m_st), (1, 3, m_in), (2, 3, m_in)]):
                    for hg in range(4):
                        p1 = nb()
                        for q in range(4):
                            hh = 2 * ((hg % 2) * 4 + q) + hg // 2
                            k.op('pe', lambda: PE.matmul(ps[p1][:, q * 128:(q + 1) * 128], fm(lq, hh), fm(rq, hh), start=True, stop=True), reads=['XT4'], writes=[pk[p1]])
                        k.op('dve', lambda: V.tensor_tensor(out=M5[:, mi, hg * 4:hg * 4 + 4, :], in0=ps[p1][:].rearrange("p (a b) -> p a b", b=128),
                                                            in1=msk.unsqueeze(1).to_broadcast([128, 4, 128]), op=OP.mult), reads=[pk[p1], 'tri'], writes=['M5_%d' % mi])
                if SS <= 5:
                    continue
                k.op('pool', lambda: G.tensor_copy(out=Zf[:, :, 0:64], in_=At[:].rearrange("p (h n) -> p h n", n=64)), reads=['At'], writes=['Zf'])
                for hf in range(2):
                    p1 = nb()
                    for q in range(8):
                        hh = hf * 8 + q
                        k.op('pe', lambda: PE.matmul(ps[p1][:, q * 64:(q + 1) * 64], M5[:, 2, hpos(hh), :], Vb[:, hh * 64:(hh + 1) * 64], start=True, stop=True), reads=['M5_2', 'Vb'], writes=[pk[p1]])
                    k.op('dve', lambda: V.tensor_copy(out=Zf[:, hf * 8:hf * 8 + 8, 64:128], in_=ps[p1][:].rearrange("p (a b) -> p a b", b=64)), reads=[pk[p1]], writes=['Zf'])
                k.op('act', lambda: A.copy(out=Zb[:], in_=Zf[:]), reads=['Zf'], writes=['Zb'])
                if SS <= 6:
                    continue
                cur = None
                for it in range(7):
                    if it == 0:
                        Xc = lambda hh: M5[:, 1, hpos(hh), :]
                        XTc = lambda hh: M5[:, 0, hpos(hh), :]
                        xkeys = ['M5_0', 'M5_1']
                    else:
                        src_i = (it - 1) % 2
                        dst_i = it % 2
                        if it == 1:
                            Xs, XTs, skeys = (lambda hh: M5[:, 1, hpos(hh), :]), (lambda hh: M5[:, 0, hpos(hh), :]), ['M5_0', 'M5_1']
                        else:
                            Xs, XTs, skeys = (lambda hh, si=src_i: Xp[:, si, 0, hh, :]), (lambda hh, si=src_i: Xp[:, si, 1, hh, :]), ['Xp%d' % src_i]
                        for which in range(2):
                            for hg in range(4):
                                p1 = nb()
                                for q in range(4):
                                    hh = hg * 4 + q
                                    if which == 0:
                                        k.op('pe', lambda: PE.matmul(ps[p1][:, q * 128:(q + 1) * 128], XTs(hh), Xs(hh), start=True, stop=True), reads=skeys, writes=[pk[p1]])
                                    else:
                                        k.op('pe', lambda: PE.matmul(ps[p1][:, q * 128:(q + 1) * 128], Xs(hh), XTs(hh), start=True, stop=True), reads=skeys, writes=[pk[p1]])
                                evac(hg, Xp[:, dst_i, which, hg * 4:hg * 4 + 4, :].rearrange("p a b -> p (a b)"), ps[p1][:], [pk[p1]], ['Xp%d' % dst_i])
                        XTc = lambda hh, di=dst_i: Xp[:, di, 1, hh, :]
                        xkeys = ['Xp%d' % dst_i]
                    for hg in range(4):
                        p1 = nb()
                        for q in range(4):
                            hh = hg * 4 + q
                            k.op('pe', lambda: PE.matmul(ps[p1][:, q * 128:(q + 1) * 128], XTc(hh), Zb[:, hh, :], start=True, stop=True), reads=xkeys + ['Zb'], writes=[pk[p1]])
                        k.op('dve', lambda: V.tensor_tensor(out=Zf[:, hg * 4:hg * 4 + 4, :], in0=Zf[:, hg * 4:hg * 4 + 4, :], in1=ps[p1][:].rearrange("p (a b) -> p a b", b=128),
                                                            op=(OP.subtract if it == 0 else OP.add)), reads=[pk[p1], 'Zf'], writes=['Zf'])
                    k.op('act', lambda: A.copy(out=Zb[:], in_=Zf[:]), reads=['Zf'], writes=['Zb'])
                k.op('act', lambda: A.mul(out=nZb[:], in_=Zf[:], mul=-1.0), reads=['Zf'], writes=['nZb'])
                if SS <= 7:
                    continue
                o_, okey = o64.next()
                for hf in range(2):
                    p1 = nb()
                    for q in range(8):
                        hh = hf * 8 + q
                        k.op('pe', lambda: PE.matmul(ps[p1][0:64, q * 64:(q + 1) * 64], Zb[:, hh, 0:64], Btb[:, hh * 64:(hh + 1) * 64], start=True, stop=True), reads=['Zb', 'Btb'], writes=[pk[p1]])
                    k.op('dve', lambda: V.tensor_tensor(out=o_[:, hf * 8:hf * 8 + 8, 0:64], in0=ident[0:64, 0:64].unsqueeze(1).to_broadcast([64, 8, 64]),
                                                        in1=ps[p1][0:64, :].rearrange("p (a b) -> p a b", b=64), op=OP.subtract), reads=[pk[p1], 'ident'], writes=[okey])
                k.dma('sp', scP[tt, d], o_[:, :, 0:64], reads=[okey], writes=['scP'], semkey=(okey, 'st'))
                o_, okey = o64.next()
                for hf in range(2):
                    p1 = nb()
                    for q in range(8):
                        hh = hf * 8 + q
                        k.op('pe', lambda: PE.matmul(ps[p1][0:64, q * 64:(q + 1) * 64], Ktb[:, hh * 64:(hh + 1) * 64], Vb[:, hh * 64:(hh + 1) * 64], start=True, stop=False), reads=['Ktb', 'Vb'], writes=[pk[p1]])
                        k.op('pe', lambda: PE.matmul(ps[p1][0:64, q * 64:(q + 1) * 64], Btb[:, hh * 64:(hh + 1) * 64], nZb[:, hh, 64:128], start=False, stop=True), reads=['Btb', 'nZb'], writes=[pk[p1]])
                    evac(hf, o_[:, hf * 8:hf * 8 + 8, 0:64], ps[p1][0:64, :].rearrange("p (a b) -> p a b", b=64), [pk[p1]], [okey])
                k.dma('sp', scQ[tt, d], o_[:, :, 0:64], reads=[okey], writes=['scQ'], semkey=(okey, 'st'))
                o_, okey = o64.next()
                for hg in range(4):
                    p1 = nb()
                    for q in range(4):
                        hh = hg * 4 + q
                        k.op('pe', lambda: PE.matmul(ps[p1][0:64, q * 128:(q + 1) * 128], Rtb[:, hh * 64:(hh + 1) * 64], idb[:], start=True, stop=False), reads=['Rtb', 'idb'], writes=[pk[p1]])
                        k.op('pe', lambda: PE.matmul(ps[p1][0:64, q * 128:(q + 1) * 128], nZb[:, hh, 0:64], M5[:, 3, hpos(hh), :], start=False, stop=True), reads=['nZb', 'M5_3'], writes=[pk[p1]])
                    evac(hg, o_[:, hg * 4:hg * 4 + 4, :].rearrange("p a b -> p (a b)"), ps[p1][0:64, :], [pk[p1]], [okey])
                k.dma('sp', scG[tt, d], o_[:], reads=[okey], writes=['scG'], semkey=(okey, 'st'))
                y_, yk = oy.next()
                for hf in range(2):
                    p1 = nb()
                    for q in range(8):
                        hh = hf * 8 + q
                        k.op('pe', lambda: PE.matmul(ps[p1][:, q * 64:(q + 1) * 64], M5[:, 4, hpos(hh), :], Vb[:, hh * 64:(hh + 1) * 64], start=True, stop=False), reads=['M5_4', 'Vb'], writes=[pk[p1]])
                        k.op('pe', lambda: PE.matmul(ps[p1][:, q * 64:(q + 1) * 64], M5[:, 3, hpos(hh), :], nZb[:, hh, 64:128], start=False, stop=True), reads=['M5_3', 'nZb'], writes=[pk[p1]])
                    evac(hf, y_[:, hf * 512:(hf + 1) * 512], ps[p1][:], [pk[p1]], [yk])
                k.dma('sp', scY[tt, d], y_[:], reads=[yk], writes=['scY'], semkey=(yk, 'st'))


    def phase_chain(l):
        S = k.sb("S", [64, 2, 16, 64], F32)
        Ys = k.sb("Ys", [128, NTT, 1024], F32)
        Pt = Ring(k, "Pt", [64, 16, 64], F32, 2)
        Qt = Ring(k, "Qt", [64, 16, 64], F32, 2)
        Gt = Ring(k, "Gt", [64, 16, 128], F32, 2)
        Yt = Ring(k, "Yt", [128, 1024], F32, 2)
        Dt = Ring(k, "Dt", [64, 16], F32, 2)
        tq = k.sb("tq", [64, 16, 64], F32)
        s0t = k.sb("s0t", [128, 8, 64], F32)
        so = Ring(k, "so", [128, 8, 64], F32, 2)
        pbn = [0]

        def nb():
            pbn[0] += 1
            return pbn[0] % 8
        first = {}

        def out_state(seg, d):
            p1 = nb()
            for hp in range(8):
                k.op('pe', lambda: PE.transpose(out=ps[p1][:, hp * 64:(hp + 1) * 64], in_=S[:, d, 2 * hp:2 * hp + 2, :].rearrange("p a b -> p (a b)"), identity=ident[0:64, 0:64]),
                     reads=['S%d' % d, 'ident'], writes=[pk[p1]])
            o_, okey = so.next()
            k.op('dve', lambda: V.tensor_copy(out=o_[:].rearrange("p a b -> p (a b)"), in_=ps[p1][:]), reads=[pk[p1]], writes=[okey])
            k.dma('sp', st_out[l, seg, d].rearrange("(hp h2) v kk -> (h2 v) hp kk", h2=2), o_[:], reads=[okey], semkey=(okey, 'st'))

        def step(c, d):
            P_, Pk = Pt.next()
            Q_, Qk = Qt.next()
            G_, Gk = Gt.next()
            Y_, Yk = Yt.next()
            D_, Dk = Dt.next()
            k.dma('sp', P_[:], scP[c, d], reads=['scP'], writes=[Pk])
            k.dma('sp', Q_[:], scQ[c, d], reads=['scQ'], writes=[Qk])
            k.dma('sp', G_[:], scG[c, d], reads=['scG'], writes=[Gk])
            k.dma('sp', Y_[:], scY[c, d], reads=['scY'], writes=[Yk])
            k.dma('sp', D_[:], scD[c, d], reads=['scD'], writes=[Dk])
            sk_ = 'S%d' % d
            for hf in range(2):
                p1 = nb()
                cs_ = slice(hf * 512, (hf + 1) * 512)
                for q in range(8):
                    hh = hf * 8 + q
                    k.op('pe', lambda: PE.matmul(ps[p1][:, q * 64:(q + 1) * 64], G_[:, hh, :], S[:, d, hh, :], start=True, stop=True), reads=[Gk, sk_], writes=[pk[p1]])
                if c not in first:
                    k.op('dve', lambda: V.tensor_tensor(out=Ys[:, c, cs_], in0=ps[p1][:], in1=Y_[:, cs_], op=OP.add), reads=[pk[p1], Yk], writes=['Ys%d' % c])
                else:
                    k.op('dve', lambda: V.tensor_tensor(out=Y_[:, cs_], in0=ps[p1][:], in1=Y_[:, cs_], op=OP.add), reads=[pk[p1], Yk], writes=[Yk])
                    k.op('pool', lambda: G.tensor_tensor(out=Ys[:, c, cs_], in0=Ys[:, c, cs_], in1=Y_[:, cs_], op=OP.add), reads=[Yk, 'Ys%d' % c], writes=['Ys%d' % c])
            first[c] = True
            pp = [nb(), nb()]
            for hf in range(2):
                for q in range(8):
                    hh = hf * 8 + q
                    k.op('pe', lambda: PE.matmul(ps[pp[hf]][0:64, q * 64:(q + 1) * 64], P_[:, hh, :], S[:, d, hh, :], start=True, stop=True), reads=[Pk, sk_], writes=[pk[pp[hf]]])
            for hf in range(2):
                hs = slice(hf * 8, hf * 8 + 8)
                k.op('dve', lambda: V.tensor_tensor(out=tq[:, hs, :], in0=ps[pp[hf]][0:64, :].rearrange("p (a b) -> p a b", b=64), in1=Q_[:, hs, :], op=OP.add),
                     reads=[pk[pp[hf]], Qk], writes=['tq'])
            k.op('dve', lambda: V.tensor_tensor(out=S[:, d], in0=tq[:], in1=D_[:].unsqueeze(2).to_broadcast([64, 16, 64]), op=OP.mult), reads=['tq', Dk], writes=[sk_])

        for d in range(2):
            k.dma('sp', s0t[:], s0_in[l, d].rearrange("(hp h2) v kk -> (h2 v) hp kk", h2=2), writes=['s0t'])
            for hf in range(2):
                p1 = nb()
                for q in range(4):
                    hp = hf * 4 + q
                    k.op('pe', lambda: PE.transpose(out=ps[p1][0:64, q * 128:(q + 1) * 128], in_=s0t[:, hp, :], identity=ident[:]), reads=['s0t', 'ident'], writes=[pk[p1]])
                k.op('dve', lambda: V.tensor_copy(out=S[:, d, hf * 8:hf * 8 + 8, :].rearrange("p a b -> p (a b)"), in_=ps[p1][0:64, :]), reads=[pk[p1]], writes=['S%d' % d])
            order = list(range(8)) if d == 0 else list(range(7, -1, -1))
            for n_, c in enumerate(order):
                step(c, d)
                if n_ % 2 == 1:
                    out_state(c // 2, d)
                    if n_ < 7:
                        k.op('dve', lambda: V.tensor_scalar(out=S[:, d], in0=S[:, d], scalar1=carry[0:64, 0:1], scalar2=None, op0=OP.mult), reads=['S%d' % d, 'carry'], writes=['S%d' % d])
        for seg in range(2):
            for d in range(2):
                k.op('dve', lambda: V.memset(S[:, d], 0.0), writes=['S%d' % d])
                cc = [8 + 2 * seg, 9 + 2 * seg]
                for c in (cc if d == 0 else cc[::-1]):
                    step(c, d)
                out_state(4 + seg, d)
        lg = k.sb("lg", [128, 1024], F32)
        lb = k.sb("lb", [128, 1024], F32)
        bcast_load(lg[:], W["lnx_g"][l], 'lg')
        bcast_load(lb[:], W["lnx_b"][l], 'lb')
        vt = Ring(k, "vt", [128, 1024], F32, 2)
        gt = Ring(k, "gt", [128, 1024], F32, 2)
        bt = Ring(k, "bt", [128, 16], F32, 2)
        sm = k.sb("sm", [128, 48], F32)
        f1 = k.sb("f1", [128, 1024], F32)
        yo = Ring(k, "yo", [128, 8, 128], BF16, 2)
        h3 = lambda ap: ap.rearrange("p (h n) -> p h n", n=64)
        for tt in range(NTT):
            ts = slice(tt * 128, (tt + 1) * 128)
            v_, vk = vt.next()
            g_, gk = gt.next()
            b_, bk = bt.next()
            k.dma('sp', v_[:], ztm[ts, 2048:3072], reads=['ztm'], writes=[vk])
            k.dma('sp', g_[:], scGt[ts, :], reads=['scGt'], writes=[gk])
            k.dma('sp', b_[:], scB[ts, :], reads=['scB'], writes=[bk])
            y = Ys[:, tt, :]
            yk_ = 'Ys%d' % tt
            k.op('dve', lambda: V.tensor_reduce(out=sm[:, 0:16], in_=h3(y), axis=AX.X, op=OP.add), reads=[yk_], writes=['sm'])
            k.op('dve', lambda: V.tensor_scalar(out=sm[:, 0:16], in0=sm[:, 0:16], scalar1=1.0 / 64, scalar2=None, op0=OP.mult), reads=['sm'], writes=['sm'])
            k.op('dve', lambda: V.tensor_tensor(out=h3(y), in0=h3(y), in1=sm[:, 0:16].unsqueeze(2).to_broadcast([128, 16, 64]), op=OP.subtract), reads=[yk_, 'sm'], writes=[yk_])
            k.op('pool', lambda: G.tensor_tensor(out=f1[:], in0=y, in1=y, op=OP.mult), reads=[yk_], writes=['f1'])
            k.op('dve', lambda: V.tensor_reduce(out=sm[:, 16:32], in_=h3(f1[:]), axis=AX.X, op=OP.add), reads=['f1'], writes=['sm'])
            k.op('act', lambda: A.activation(out=sm[:, 16:32], in_=sm[:, 16:32], func=AF.Sqrt, bias=epsx[:, 0:1], scale=1.0 / 64), reads=['sm', 'epsx'], writes=['sm'])
            k.op('dve', lambda: V.reciprocal(out=sm[:, 16:32], in_=sm[:, 16:32]), reads=['sm'], writes=['sm'])
            k.op('dve', lambda: V.tensor_tensor(out=h3(y), in0=h3(y), in1=sm[:, 16:32].unsqueeze(2).to_broadcast([128, 16, 64]), op=OP.mult), reads=[yk_, 'sm'], writes=[yk_])
            k.op('dve', lambda: V.tensor_tensor(out=y, in0=y, in1=lg[:], op=OP.mult), reads=[yk_, 'lg'], writes=[yk_])
            k.op('dve', lambda: V.tensor_tensor(out=h3(v_[:]), in0=h3(v_[:]), in1=b_[:].unsqueeze(2).to_broadcast([128, 16, 64]), op=OP.mult), reads=[vk, bk], writes=[vk])
            k.op('dve', lambda: V.tensor_tensor(out=y, in0=y, in1=lb[:], op=OP.add), reads=[yk_, 'lb'], writes=[yk_])
            k.op('dve', lambda: V.tensor_tensor(out=y, in0=y, in1=v_[:], op=OP.add), reads=[yk_, vk], writes=[yk_])
            k.op('dve', lambda: V.tensor_tensor(out=y, in0=y, in1=g_[:], op=OP.mult), reads=[yk_, gk], writes=[yk_])
            o_, okey = yo.next()
            for hf in range(2):
                p1 = nb()
                for q in range(4):
                    pr = hf * 4 + q
                    k.op('pe', lambda: PE.transpose(out=ps[p1][:, q * 128:(q + 1) * 128], in_=Ys[:, tt, pr * 128:(pr + 1) * 128], identity=ident[:]), reads=[yk_, 'ident'], writes=[pk[p1]])
                evac(hf, o_[:, hf * 4:hf * 4 + 4, :].rearrange("p a b -> p (a b)"), ps[p1][:], [pk[p1]], [okey])
            k.dma('sp', ymixT[8:16, :, ts].rearrange("q p t -> p q t"), o_[:], reads=[okey], writes=['ymixT'], semkey=(okey, 'st'))

    def phase_wout(l):
        x = k.sb("x", [128, KC, 512], F32)
        ym = k.sb("ym", [128, KC, 512], BF16)
        wo = k.sb("wo", [128, KC, D], BF16)
        wst = Ring(k, "wst", [128, KC, 256], F32, 2)
        wv = W["w_out"][l].rearrange("(kc p) n -> p kc n", p=128)
        for b in range(8):
            st, sk = wst.next()
            k.dma('sp', st[:], wv[:, :, b * 256:(b + 1) * 256], writes=[sk])
            k.op('pool', lambda: G.tensor_copy(out=wo[:, :, b * 256:(b + 1) * 256], in_=st[:]), reads=[sk], writes=['wo%d' % b])
        for tg in range(3):
            ci = 0 if tg < 2 else 1
            ts = slice(tg * 512, (tg + 1) * 512)
            k.dma('sp', x[:], xs[:, :, ts].rearrange("kc p t -> p kc t"), reads=['xs'], writes=['x'])
            k.dma('sp', ym[:], ymixT[:, :, ts].rearrange("kc p t -> p kc t"), reads=['ymixT'], writes=['ym'])
            for oc in range(KC):
                po = oc % 4
                for kc in range(KC):
                    k.op('pe', lambda: PE.matmul(ps[po][:], wo[:, kc, oc * 128:(oc + 1) * 128], ym[:, kc, :], start=(kc == 0), stop=(kc == KC - 1)),
                         reads=['wo%d' % (oc // 2), 'ym'], writes=[pk[po]])
                k.op('dve', lambda: V.scalar_tensor_tensor(out=x[:, oc, :], in0=ps[po][:], scalar=gtv[:, l, 1, oc, ci:ci + 1], in1=x[:, oc, :],
                                                           op0=OP.mult, op1=OP.add), reads=[pk[po], 'gtv', 'x'], writes=['x'])
            k.dma('sp', xs[:, :, ts].rearrange("kc p t -> p kc t"), x[:], reads=['x'], writes=['xs'])

    if phases is None or 'mod' in phases:
        phase_mod()
        phase_reset()
    if phases is None or 'in' in phases:
        phase_in()
        phase_reset()
    stages = []
    for l in range(DEPTH):
        stages.append(('ffn', l, 0))
        stages.append(('mix', l, 1))
        stages.append(('ffn', l, 2))
    nst = 0
    for kind, l, i in stages:
        if stop_after is not None and nst >= stop_after:
            break
        nst += 1
        if kind == 'ffn':
            if phases is None or 'ffn' in phases:
                phase_ffn(l, i)
                phase_reset()
        else:
            for nm, fn in [('win', phase_win), ('gmlp', phase_gmlp), ('fnet', phase_fnet), ('shift', phase_shift), ('scan', phase_scan), ('chain', phase_chain), ('wout', phase_wout)]:
                if mix_parts is None or nm in mix_parts:
                    fn(l)
                    phase_reset()
    if phases is None or 'out' in phases:
        phase_out()
    k.finish()
    print("program built: ninst", k.ninst, "dma sems", len(k.dpool))
    nc.used_weight_inputs = list(used_inputs)
    return nc


def host_consts(kind):
    o = {}
    ch = np.arange(CIN)
    ind = np.zeros((2, 4, CIN), np.float32)
    q = CIN // 4
    if kind == 'grid':
        for j in range(4):
            ind[0, j, j * q:(j + 1) * q] = 1
    else:
        ind[0, 0, :CIN // 2] = 1
        ind[0, 1, CIN // 2:] = 1
    ind[1, 0, :CIN // 2] = 1
    ind[1, 1, CIN // 2:] = 1
    o['ind'] = ind
    valid = np.zeros((NT, 4), np.float32)
    t = np.arange(1024)
    if kind == 'grid':
        valid[:1024, 0] = (t % 64 != 0)
        valid[:1024, 1] = (t % 64 != 63)
        valid[:1024, 2] = (t >= 64)
        valid[:1024, 3] = (t < 960)
    else:
        valid[:1024, 0] = (t % 256 != 0)
        valid[:1024, 1] = (t % 256 != 255)
    t2 = np.arange(512)
    valid[1024:, 0] = (t2 % 256 != 0)
    valid[1024:, 1] = (t2 % 256 != 255)
    o['valid'] = valid

    def dft(n):
        a = 2 * np.pi * np.outer(np.arange(n), np.arange(n)) / n
        return (np.cos(a) / np.sqrt(n)), (np.sin(a) / np.sqrt(n))
    if kind == 'grid':
        c, s = dft(1024)
    else:
        c256, s256 = dft(256)
        c = np.zeros((1024, 1024))
        s = np.zeros((1024, 1024))
        for j in range(4):
            c[j * 256:(j + 1) * 256, j * 256:(j + 1) * 256] = c256
            s[j * 256:(j + 1) * 256, j * 256:(j + 1) * 256] = s256
    o['CL'] = c.astype(np.float32)
    o['nSL'] = (-s).astype(np.float32)
    c256, s256 = dft(256)
    o['CB'] = c256.astype(np.float32)
    o['nSB'] = (-s256).astype(np.float32)
    cd, sd = dft(128)
    o['CSd'] = np.concatenate([cd, sd], 1).astype(np.float32)
    i = np.arange(128)
    ui = (i[:, None] <= i[None, :]).astype(np.float32)
    ue = (i[:, None] < i[None, :]).astype(np.float32)
    o['tri'] = np.stack([ui, ue, ui.T.copy(), ue.T.copy()])
    o['ident'] = np.eye(128, dtype=np.float32)
    o['onesm'] = np.full((128, 128), 1.0 / D, np.float32)
    o['carry'] = np.full((128, 1), 1.0 if kind == 'grid' else 0.0, np.float32)
    return o


WNAMES = ["norm_g", "w_mod", "b_mod", "ffn_w_in", "ffn_w_out", "w_in", "w_out", "sgu_ln_g", "sgu_ln_b", "sgu_w", "sgu_b",
          "shift_mu", "decay_w0", "decay_w2", "iclr_a0", "iclr_a2", "k_k", "k_a", "r_k", "gate_w2", "lnx_g", "lnx_b", "final_g"]


def core_seqs(c):
    if c < 4:
        return [2 * c, 2 * c + 1]
    return [8 + (c - 4) * 6 + j for j in range(6)]


def make_in_maps(inputs):
    f = lambda a: np.ascontiguousarray(np.asarray(a, dtype=np.float32))
    xp = f(inputs['x_prompt'])
    xsm = f(inputs['x_sample'])
    stw = f(inputs['state_wkv'])
    cc = f(inputs['c'])
    cctx = f(inputs['c_ctx'])
    wts = {n: f(inputs[n]) for n in WNAMES}
    cg = host_consts('grid')
    cs = host_consts('seq')
    maps = []
    for c in range(8):
        m = dict(wts)
        seqs = core_seqs(c)
        if c < 4:
            m.update(cg)
            m['xin'] = np.concatenate([xsm[c]] + [xp[s] for s in seqs], 0)
            m['cond'] = np.stack([cc[c], cctx])
            m['s0'] = np.ascontiguousarray(stw[c])
        else:
            m.update(cs)
            m['xin'] = np.concatenate([xp[s] for s in seqs], 0)
            m['cond'] = np.stack([cctx, cctx])
            m['s0'] = np.zeros((DEPTH, 2, 16, 64, 64), np.float32)
        maps.append(m)
    return maps


def filter_maps(nc, maps):
    drop = set(WNAMES) - set(nc.used_weight_inputs)
    return [{n: v for n, v in m.items() if n not in drop} for m in maps]


def kernel(**inputs):
    maps = make_in_maps(inputs)
    nc = build_program()
    maps = filter_maps(nc, maps)
    res = run_bass_kernel_spmd(nc, maps, core_ids=list(range(8)))
    B, S = inputs['x_prompt'].shape[0], inputs['x_prompt'].shape[1]
    y_prompt = np.zeros((B, S, D), np.float32)
    y_sample = np.zeros((4, 1024, D), np.float32)
    new_state = np.zeros((B, DEPTH, 2, 16, 64, 64), np.float32)
    for c in range(8):
        r = res.results[c]
        y = r['y']
        st = r['st']
        seqs = core_seqs(c)
        if c < 4:
            y_sample[c] = y[:1024]
            for j, s in enumerate(seqs):
                y_prompt[s] = y[1024 + j * 256:1024 + (j + 1) * 256]
                new_state[s] = st[:, 4 + j]
        else:
            for j, s in enumerate(seqs):
                y_prompt[s] = y[j * 256:(j + 1) * 256]
                new_state[s] = st[:, j]
    return (y_prompt, y_sample, new_state)
```

```python
import numpy as np
import concourse.bass as bass
import concourse.mybir as mybir
from concourse.bass_utils import run_bass_kernel_spmd

F32 = mybir.dt.float32
BF16 = mybir.dt.bfloat16
AF = mybir.ActivationFunctionType
OP = mybir.AluOpType
AX = mybir.AxisListType

D = 2048
KC = 16
NT = 1536
NTT = 12
DFF = 5632
NJ = 44
DEPTH = 2
CIN = 3488
INC = 5024
SB_BASE = 17408
SB_END = 229376


class K:
    def __init__(s, nc):
        s.nc = nc
        s.E = {'pe': nc.tensor, 'dve': nc.vector, 'act': nc.scalar, 'pool': nc.gpsimd, 'sp': nc.sync}
        s.csem = {e: nc.alloc_semaphore('c_' + e) for e in s.E}
        s.cnt = {e: 0 for e in s.E}
        s.seen = {}
        s.track = {}
        s.dpool = []
        s.dmap = {}
        s.dnext = 0
        s.sbuf_off = SB_BASE
        s.ninst = 0
        s.uid = 0

    def _wait(s, eng, tok):
        if tok is None:
            return
        sem, val, owner = tok
        if owner == eng and eng == 'pe':
            return
        kk = (eng, id(sem))
        if s.seen.get(kk, -1) >= val:
            return
        s.seen[kk] = val
        s.E[eng].wait_ge(sem, val)
        s.ninst += 1

    def _deps(s, eng, reads, writes):
        for key in reads:
            t = s.track.get(key)
            if t is not None:
                s._wait(eng, t['w'])
        for key in writes:
            t = s.track.get(key)
            if t is not None:
                s._wait(eng, t['w'])
                for r in t['r']:
                    s._wait(eng, r)

    def _commit(s, tok, reads, writes):
        for key in reads:
            t = s.track.setdefault(key, {'w': None, 'r': []})
            t['r'].append(tok)
            if len(t['r']) > 16:
                best = {}
                for r in t['r']:
                    q = id(r[0])
                    if q not in best or best[q][1] < r[1]:
                        best[q] = r
                t['r'] = list(best.values())
        for key in writes:
            s.track[key] = {'w': tok, 'r': []}

    def op(s, eng, fn, reads=(), writes=()):
        s._deps(eng, reads, writes)
        ins = fn()
        s.cnt[eng] += 1
        ins.then_inc(s.csem[eng], 1)
        s._commit((s.csem[eng], s.cnt[eng], eng), reads, writes)
        s.ninst += 1
        return ins

    def dma(s, q, out, in_, reads=(), writes=(), semkey=None):
        if q == 'sp' and 'DRam' in type(out.tensor).__name__ and 'DRam' not in type(in_.tensor).__name__:
            q = 'act'
        s._deps(q, reads, writes)
        if semkey is None:
            semkey = tuple(writes) if writes else tuple(reads)
        if semkey not in s.dmap:
            if s.dnext >= len(s.dpool):
                s.dpool.append([s.nc.alloc_semaphore('d%d' % len(s.dpool)), 0])
            s.dmap[semkey] = s.dnext
            s.dnext += 1
        ent = s.dpool[s.dmap[semkey]]
        ent[1] += 16
        ins = s.E[q].dma_start(out=out, in_=in_)
        ins.then_inc(ent[0], 16)
        s._commit((ent[0], ent[1], 'dma'), reads, writes)
        s.ninst += 1
        return ins

    def all_tokens(s):
        toks = [(s.csem[e], s.cnt[e], e) for e in s.E if s.cnt[e] > 0]
        toks += [(e[0], e[1], 'dma') for e in s.dpool if e[1] > 0]
        return toks

    def barrier(s):
        toks = s.all_tokens()
        for e in s.E:
            for t in toks:
                s._wait(e, t)
        s.track = {}
        s.dmap = {}
        s.dnext = 0

    def finish(s):
        for t in s.all_tokens():
            s._wait('sp', t)

    def sb(s, name, shape, dtype):
        nbytes = int(np.prod(shape[1:])) * (2 if dtype == BF16 else 4)
        off = (s.sbuf_off + 63) // 64 * 64
        s.sbuf_off = off + nbytes
        assert s.sbuf_off <= SB_END, (name, s.sbuf_off)
        s.uid += 1
        return s.nc.alloc_sbuf_tensor_at('%s_%d' % (name, s.uid), list(shape), dtype, offset=off)


class Ring:
    def __init__(s, k, name, shape, dtype, n):
        s.t = [k.sb('%s%d' % (name, i), shape, dtype) for i in range(n)]
        s.keys = ['%s_%d_%d' % (name, k.uid, i) for i in range(n)]
        s.i = 0
        s.n = n

    def next(s):
        j = s.i % s.n
        s.i += 1
        return s.t[j], s.keys[j]


def build_program(debug=(), stop_after=None, phases=None, mix_parts=None):
    nc = bass.Bass("TRN2", target_bir_lowering=False)
    V = nc.vector
    A = nc.scalar
    G = nc.gpsimd
    PE = nc.tensor

    def din(name, shape):
        return nc.dram_tensor(name, list(shape), F32, kind="ExternalInput").ap()

    def dout(name, shape):
        return nc.dram_tensor(name, list(shape), F32, kind="ExternalOutput").ap()

    def dscr(name, shape, dt=F32):
        kind = "ExternalOutput" if name in debug else "Internal"
        return nc.dram_tensor(name, list(shape), dt, kind=kind).ap()

    xin = din("xin", [NT, D])
    cond = din("cond", [2, D])
    s0_in = din("s0", [DEPTH, 2, 16, 64, 64])
    carry_in = din("carry", [128, 1])
    ind_in = din("ind", [2, 4, CIN])
    valid_in = din("valid", [NT, 4])
    CL_in = din("CL", [1024, 1024])
    nSL_in = din("nSL", [1024, 1024])
    CB_in = din("CB", [256, 256])
    nSB_in = din("nSB", [256, 256])
    CSd_in = din("CSd", [128, 256])
    tri_in = din("tri", [4, 128, 128])
    ident_in = din("ident", [128, 128])
    onesm_in = din("onesm", [128, 128])
    WSHAPES = dict([("norm_g", [DEPTH, 3, D]), ("w_mod", [DEPTH, D, 9 * D]), ("b_mod", [DEPTH, 9 * D]),
                        ("ffn_w_in", [DEPTH, 2, D, 2 * DFF]), ("ffn_w_out", [DEPTH, 2, DFF, D]),
                        ("w_in", [DEPTH, D, INC]), ("w_out", [DEPTH, D, D]),
                        ("sgu_ln_g", [DEPTH, 512]), ("sgu_ln_b", [DEPTH, 512]), ("sgu_w", [DEPTH, 4, 128, 128]),
                        ("sgu_b", [DEPTH, 4, 128]), ("shift_mu", [DEPTH, CIN]), ("decay_w0", [DEPTH, 2, 1024]),
                        ("decay_w2", [DEPTH, 2, 64, 1024]), ("iclr_a0", [DEPTH, 2, 1024]),
                        ("iclr_a2", [DEPTH, 2, 64, 1024]), ("k_k", [DEPTH, 1024]), ("k_a", [DEPTH, 1024]),
                        ("r_k", [DEPTH, 16, 64]), ("gate_w2", [DEPTH, 160, 1024]), ("lnx_g", [DEPTH, 1024]),
                        ("lnx_b", [DEPTH, 1024]), ("final_g", [D])])
    used_inputs = []

    class LazyW(dict):
        def __missing__(s, name):
            s[name] = din(name, WSHAPES[name])
            used_inputs.append(name)
            return s[name]
    W = LazyW()
    y_out = dout("y", [NT, D])
    st_out = dout("st", [DEPTH, 6, 2, 16, 64, 64])
    xs = dscr("xs", [KC, 128, NT])
    uT = dscr("uT", [4, 128, NT], BF16)
    vn = dscr("vn", [NT, 512], BF16)
    zbT = dscr("zbT", [4, 128, NT], BF16)
    zc = dscr("zc", [NT + 128, CIN])
    ztm = dscr("ztm", [NT, CIN])
    ymixT = dscr("ymixT", [KC, 128, NT], BF16)
    scP = dscr("scP", [NTT, 2, 64, 16, 64])
    scQ = dscr("scQ", [NTT, 2, 64, 16, 64])
    scG = dscr("scG", [NTT, 2, 64, 16, 128])
    scY = dscr("scY", [NTT, 2, 128, 1024])
    scD = dscr("scD", [NTT, 2, 64, 16])
    scV = dscr("scV", [NT, 1024])
    scGt = dscr("scGt", [NT, 1024])
    scB = dscr("scB", [NT, 16])

    k = K(nc)
    ps = [nc.alloc_psum_tensor("ps%d" % i, [128, 512], F32) for i in range(8)]
    pk = ['ps%d' % i for i in range(8)]

    ident = k.sb("ident", [128, 128], F32)
    onesm = k.sb("onesm", [128, 128], F32)
    tri = k.sb("tri", [128, 4, 128], F32)
    carry = k.sb("carry", [128, 1], F32)
    valid = k.sb("valid", [128, NTT, 4], F32)
    epsr = k.sb("epsr", [128, 1], F32)
    epsx = k.sb("epsx", [128, 1], F32)
    epsl = k.sb("epsl", [128, 1], F32)
    modv = k.sb("modv", [128, DEPTH, 144, 2], F32)
    gsv = k.sb("gsv", [128, DEPTH, 3, KC, 2], F32)
    gtv = k.sb("gtv", [128, DEPTH, 3, KC, 2], F32)
    ngT = k.sb("ngT", [128, DEPTH, KC, 3], F32)
    fgT = k.sb("fgT", [128, KC, 1], F32)
    PERSIST_END = k.sbuf_off

    def phase_reset():
        k.barrier()
        k.sbuf_off = PERSIST_END

    k.dma('sp', ident[:], ident_in, writes=['ident'])
    k.dma('sp', onesm[:], onesm_in, writes=['onesm'])
    k.dma('sp', tri[:], tri_in.rearrange("f s t -> s f t"), writes=['tri'])
    k.dma('sp', carry[:], carry_in, writes=['carry'])
    k.dma('sp', valid[:], valid_in.rearrange("(tt p) o -> p tt o", p=128), writes=['valid'])
    k.op('dve', lambda: V.memset(epsr[:], 1e-6), writes=['epsr'])
    k.op('dve', lambda: V.memset(epsx[:], 64e-5), writes=['epsx'])
    k.op('dve', lambda: V.memset(epsl[:], 1e-5), writes=['epsl'])

    def rows_to_fm(src, R, C, dst_fn, tmp, tmpkey, pbank):
        k.dma('sp', tmp[0:R, 0:C], src, writes=[tmpkey])
        nj = C // 128
        for j in range(nj):
            k.op('pe', lambda: PE.transpose(out=ps[pbank][:, j * R:(j + 1) * R], in_=tmp[0:R, j * 128:(j + 1) * 128],
                                            identity=ident[0:R, 0:R]), reads=[tmpkey, 'ident'], writes=[pk[pbank]])
        for j in range(nj):
            k.op('dve', lambda: V.tensor_copy(out=dst_fn(j), in_=ps[pbank][:, j * R:(j + 1) * R]),
                 reads=[pk[pbank]], writes=['fm_dst'])

    def phase_mod():
        tmp = k.sb("rtmp", [128, D], F32)
        scT = k.sb("scT", [128, KC, 2], BF16)
        scf = k.sb("scf", [128, KC, 2], F32)
        bmT = k.sb("bmT", [128, DEPTH, 144], F32)
        c2 = k.sb("c2", [2, D], F32)
        k.dma('sp', c2[:], cond, writes=['c2'])
        k.op('act', lambda: A.activation(out=tmp[0:2, :], in_=c2[0:2, :], func=AF.Silu), reads=['c2'], writes=['rtmp'])
        for j in range(KC):
            k.op('pe', lambda: PE.transpose(out=ps[0][:, j * 2:(j + 1) * 2], in_=tmp[0:2, j * 128:(j + 1) * 128],
                                            identity=ident[0:2, 0:2]), reads=['rtmp', 'ident'], writes=['ps0'])
        k.op('dve', lambda: V.tensor_copy(out=scT[:].rearrange("p a b -> p (a b)"), in_=ps[0][:, 0:32]), reads=['ps0'], writes=['scT'])
        rows_to_fm(W["final_g"].rearrange("(o n) -> o n", o=1), 1, D, lambda j: fgT[:, j, :], tmp, 'rtmp', 1)
        for l in range(DEPTH):
            rows_to_fm(W["norm_g"][l], 3, D, lambda j: ngT[:, l, j, :], tmp, 'rtmp', 2 + l)
            bm = W["b_mod"][l].rearrange("(c p) -> c p", p=128)
            k.dma('sp', tmp[0:128, 0:128], bm[0:128, :], writes=['rtmp'])
            k.dma('sp', tmp[0:16, 128:256], bm[128:144, :], writes=['rtmp'])
            k.op('pe', lambda: PE.transpose(out=ps[4][:, 0:128], in_=tmp[0:128, 0:128], identity=ident[:]), reads=['rtmp', 'ident'], writes=['ps4'])
            k.op('pe', lambda: PE.transpose(out=ps[4][:, 128:144], in_=tmp[0:16, 128:256], identity=ident[0:16, 0:16]), reads=['rtmp', 'ident'], writes=['ps4'])
            k.op('dve', lambda: V.tensor_copy(out=bmT[:, l, :], in_=ps[4][:, 0:144]), reads=['ps4'], writes=['bmT'])
        stg = Ring(k, "wmst", [128, KC, 384], F32, 2)
        wbf = Ring(k, "wmbf", [128, KC, 384], BF16, 2)
        nb = 0
        for l in range(DEPTH):
            wm = W["w_mod"][l].rearrange("(kc p) n -> p kc n", p=128)
            for blk in range(48):
                st, sk = stg.next()
                wb, wk = wbf.next()
                k.dma('sp', st[:], wm[:, :, blk * 384:(blk + 1) * 384], writes=[sk])
                k.op('pool', lambda: G.tensor_copy(out=wb[:], in_=st[:]), reads=[sk], writes=[wk])
                pb = 5 + (nb % 2)
                nb += 1
                for q in range(3):
                    for kc in range(KC):
                        k.op('pe', lambda: PE.matmul(ps[pb][:, q * 2:(q + 1) * 2], wb[:, kc, q * 128:(q + 1) * 128], scT[:, kc, :],
                                                     start=(kc == 0), stop=(kc == KC - 1)), reads=[wk, 'scT'], writes=[pk[pb]])
                k.op('dve', lambda: V.tensor_tensor(out=modv[:, l, blk * 3:(blk + 1) * 3, :],
                                                    in0=ps[pb][:, 0:6].rearrange("p (a b) -> p a b", b=2),
                                                    in1=bmT[:, l, blk * 3:(blk + 1) * 3].unsqueeze(2).to_broadcast([128, 3, 2]), op=OP.add),
                     reads=[pk[pb], 'bmT'], writes=['modv'])
        for l in range(DEPTH):
            for i in range(3):
                sc = modv[:, l, (3 * i + 1) * 16:(3 * i + 2) * 16, :]
                gt = modv[:, l, (3 * i + 2) * 16:(3 * i + 3) * 16, :]
                k.op('dve', lambda: V.tensor_scalar(out=scf[:], in0=sc, scalar1=1.0, scalar2=None, op0=OP.add), reads=['modv'], writes=['scf'])
                k.op('dve', lambda: V.tensor_tensor(out=gsv[:, l, i], in0=scf[:], in1=ngT[:, l, :, i:i + 1].to_broadcast([128, KC, 2]), op=OP.mult),
                     reads=['scf', 'fm_dst'], writes=['gsv'])
                k.op('dve', lambda: V.tensor_scalar(out=gtv[:, l, i], in0=gt, scalar1=(1.0 if i == 1 else 0.5), scalar2=None, op0=OP.mult),
                     reads=['modv'], writes=['gtv'])

    def shiftv(l, i):
        return modv[:, l, (3 * i) * 16:(3 * i + 1) * 16, :]

    def phase_in():
        xt = Ring(k, "xt", [128, D], F32, 2)
        xo = Ring(k, "xo", [128, KC, 128], F32, 2)
        for tt in range(NTT):
            t_, tk = xt.next()
            o_, ok = xo.next()
            k.dma('sp', t_[:], xin[tt * 128:(tt + 1) * 128, :], writes=[tk])
            for g4 in range(4):
                pb = g4 % 4
                for q in range(4):
                    kc = g4 * 4 + q
                    k.op('pe', lambda: PE.transpose(out=ps[pb][:, q * 128:(q + 1) * 128], in_=t_[:, kc * 128:(kc + 1) * 128], identity=ident[:]),
                         reads=[tk, 'ident'], writes=[pk[pb]])
                eng = 'dve' if g4 % 2 == 0 else 'act'
                if eng == 'dve':
                    k.op('dve', lambda: V.tensor_copy(out=o_[:, g4 * 4:(g4 + 1) * 4, :].rearrange("p a b -> p (a b)"), in_=ps[pb][:]), reads=[pk[pb]], writes=[ok])
                else:
                    k.op('act', lambda: A.copy(out=o_[:, g4 * 4:(g4 + 1) * 4, :].rearrange("p a b -> p (a b)"), in_=ps[pb][:]), reads=[pk[pb]], writes=[ok])
            k.dma('sp', xs[:, :, tt * 128:(tt + 1) * 128].rearrange("kc p t -> p kc t"), o_[:], reads=[ok], writes=['xs'])

    def ada_norm(x, xkey, h, hkey, n, l, i, ci, sq, rs, pbank):
        for kc in range(KC):
            s_, sk = sq.next()
            k.op('act', lambda: A.activation(out=s_[:, 0:n], in_=x[:, kc, 0:n], func=AF.Square), reads=[xkey], writes=[sk])
            k.op('pe', lambda: PE.matmul(ps[pbank][:, 0:n], onesm[:], s_[:, 0:n], start=(kc == 0), stop=(kc == KC - 1)),
                 reads=[sk, 'onesm'], writes=[pk[pbank]])
        k.op('act', lambda: A.activation(out=rs[:, 0:n], in_=ps[pbank][:, 0:n], func=AF.Sqrt, bias=epsr[:, 0:1], scale=1.0), reads=[pk[pbank], 'epsr'], writes=['rs'])
        k.op('dve', lambda: V.reciprocal(out=rs[:, 0:n], in_=rs[:, 0:n]), reads=['rs'], writes=['rs'])
        for kc in range(KC):
            s_, sk = sq.next()
            k.op('dve', lambda: V.tensor_tensor(out=s_[:, 0:n], in0=x[:, kc, 0:n], in1=rs[:, 0:n], op=OP.mult), reads=[xkey, 'rs'], writes=[sk])
            if l is None:
                k.op('act', lambda: A.activation(out=h[:, kc, 0:n], in_=s_[:, 0:n], func=AF.Identity, scale=fgT[:, kc, 0:1], bias=0.0),
                     reads=[sk, 'fm_dst'], writes=[hkey])
            else:
                k.op('act', lambda: A.activation(out=h[:, kc, 0:n], in_=s_[:, 0:n], func=AF.Identity, scale=gsv[:, l, i, kc, ci:ci + 1],
                                                 bias=shiftv(l, i)[:, kc, ci:ci + 1]), reads=[sk, 'gsv', 'modv'], writes=[hkey])

    def phase_ffn(l, i):
        fi = 0 if i == 0 else 1
        x = k.sb("x", [128, KC, 512], F32)
        h = k.sb("h", [128, KC, 512], BF16)
        hb = k.sb("hb", [128, NJ, 512], BF16)
        sq = Ring(k, "sq", [128, 512], F32, 2)
        rs = k.sb("rs", [128, 512], F32)
        sg = Ring(k, "sg", [128, 512], F32, 2)
        wst = Ring(k, "wst", [128, KC, 256], F32, 2)
        wbf = Ring(k, "wbf", [128, KC, 256], BF16, 2)
        ost = Ring(k, "ost", [128, 22, 128], F32, 2)
        obf = Ring(k, "obf", [128, NJ, 128], BF16, 2)
        wi = W["ffn_w_in"][l, fi].rearrange("(kc p) n -> p kc n", p=128)
        wo = W["ffn_w_out"][l, fi].rearrange("(j p) n -> p j n", p=128)
        for tg in range(3):
            ci = 0 if tg < 2 else 1
            ts = slice(tg * 512, (tg + 1) * 512)
            k.dma('sp', x[:], xs[:, :, ts].rearrange("kc p t -> p kc t"), reads=['xs'], writes=['x'])
            ada_norm(x, 'x', h, 'h', 512, l, i, ci, sq, rs, 0)
            for j in range(NJ):
                st, sk = wst.next()
                wb, wk = wbf.next()
                k.dma('sp', st[:, :, 0:128], wi[:, :, j * 128:(j + 1) * 128], writes=[sk + 'a'])
                k.dma('sp', st[:, :, 128:256], wi[:, :, DFF + j * 128:DFF + (j + 1) * 128], writes=[sk + 'b'])
                k.op('pool', lambda: G.tensor_copy(out=wb[:], in_=st[:]), reads=[sk + 'a', sk + 'b'], writes=[wk])
                pg = 1 + 2 * (j % 2)
                pu = pg + 1
                for kc in range(KC):
                    k.op('pe', lambda: PE.matmul(ps[pg][:], wb[:, kc, 0:128], h[:, kc, :], start=(kc == 0), stop=(kc == KC - 1)),
                         reads=[wk, 'h'], writes=[pk[pg]])
                for kc in range(KC):
                    k.op('pe', lambda: PE.matmul(ps[pu][:], wb[:, kc, 128:256], h[:, kc, :], start=(kc == 0), stop=(kc == KC - 1)),
                         reads=[wk, 'h'], writes=[pk[pu]])
                s_, sgk = sg.next()
                k.op('act', lambda: A.activation(out=s_[:], in_=ps[pg][:], func=AF.Silu), reads=[pk[pg]], writes=[sgk])
                k.op('dve', lambda: V.tensor_tensor(out=hb[:, j, :], in0=s_[:], in1=ps[pu][:], op=OP.mult), reads=[sgk, pk[pu]], writes=['hb%d' % j])
            for oc in range(KC):
                ob, obk = obf.next()
                for half in range(2):
                    st, sk = ost.next()
                    k.dma('sp', st[:], wo[:, half * 22:(half + 1) * 22, oc * 128:(oc + 1) * 128], writes=[sk])
                    k.op('pool', lambda: G.tensor_copy(out=ob[:, half * 22:(half + 1) * 22, :], in_=st[:]), reads=[sk], writes=[obk + '_%d' % half])
                po = 5 + (oc % 2)
                for j in range(NJ):
                    k.op('pe', lambda: PE.matmul(ps[po][:], ob[:, j, :], hb[:, j, :], start=(j == 0), stop=(j == NJ - 1)),
                         reads=[obk + '_%d' % (j // 22), 'hb%d' % j], writes=[pk[po]])
                k.op('dve', lambda: V.scalar_tensor_tensor(out=x[:, oc, :], in0=ps[po][:], scalar=gtv[:, l, i, oc, ci:ci + 1], in1=x[:, oc, :],
                                                           op0=OP.mult, op1=OP.add), reads=[pk[po], 'gtv', 'x'], writes=['x'])
            k.dma('sp', xs[:, :, ts].rearrange("kc p t -> p kc t"), x[:], reads=['x'], writes=['xs'])

    def phase_out():
        x = k.sb("x", [128, KC, 128], F32)
        h = k.sb("hf", [128, KC, 128], F32)
        sq = Ring(k, "sq", [128, 128], F32, 2)
        rs = k.sb("rs", [128, 128], F32)
        yo = Ring(k, "yo", [128, D], F32, 2)
        for tt in range(NTT):
            ts = slice(tt * 128, (tt + 1) * 128)
            k.dma('sp', x[:], xs[:, :, ts].rearrange("kc p t -> p kc t"), reads=['xs'], writes=['x'])
            ada_norm(x, 'x', h, 'hf', 128, None, None, None, sq, rs, 0)
            y_, yk = yo.next()
            for g4 in range(4):
                pb = 1 + g4
                for q in range(4):
                    kc = g4 * 4 + q
                    k.op('pe', lambda: PE.transpose(out=ps[pb][:, q * 128:(q + 1) * 128], in_=h[:, kc, :], identity=ident[:]),
                         reads=['hf', 'ident'], writes=[pk[pb]])
                if g4 % 2 == 0:
                    k.op('dve', lambda: V.tensor_copy(out=y_[:, g4 * 512:(g4 + 1) * 512], in_=ps[pb][:]), reads=[pk[pb]], writes=[yk])
                else:
                    k.op('act', lambda: A.copy(out=y_[:, g4 * 512:(g4 + 1) * 512], in_=ps[pb][:]), reads=[pk[pb]], writes=[yk])
            k.dma('sp', y_out[ts, :], y_[:], reads=[yk], semkey=('yout', tt % 2))


    def bcast_load(dst, src_row, key):
        k.dma('sp', dst, src_row.partition_broadcast(128), writes=[key])

    def evac(i, out, in_, reads, writes):
        if i % 2 == 0:
            k.op('dve', lambda: V.tensor_copy(out=out, in_=in_), reads=reads, writes=writes)
        else:
            k.op('act', lambda: A.copy(out=out, in_=in_), reads=reads, writes=writes)

    def phase_win(l):
        x = k.sb("x", [128, KC, 512], F32)
        h = k.sb("h", [128, KC, 512], BF16)
        sq = Ring(k, "sq", [128, 512], F32, 2)
        rs = k.sb("rs", [128, 512], F32)
        wst = Ring(k, "wst", [128, KC, 512], F32, 2)
        wbf = Ring(k, "wbf", [128, KC, 512], BF16, 2)
        ob = Ring(k, "ob", [128, 512], BF16, 3)
        of = Ring(k, "of", [128, 512], F32, 3)
        lng = k.sb("lng", [128, 512], F32)
        lnb = k.sb("lnb", [128, 512], F32)
        st1 = k.sb("st1", [128, 4], F32)
        zt = k.sb("zt", [64, CIN], F32)
        bcast_load(lng[:], W["sgu_ln_g"][l], 'lng')
        bcast_load(lnb[:], W["sgu_ln_b"][l], 'lnb')
        k.op('pool', lambda: G.memset(zt[:], 0.0), writes=['zt'])
        k.dma('sp', zc[0:64, :], zt[:], reads=['zt'], writes=['zc'])
        k.dma('sp', zc[64 + NT:128 + NT, :], zt[:], reads=['zt'], writes=['zc'])
        wi = W["w_in"][l].rearrange("(kc p) n -> p kc n", p=128)
        blocks = [(0, 512), (512, 512), (1024, 512)] + [(1536 + 512 * b, 512 if b < 6 else 416) for b in range(7)]
        nev = 0
        for tg in range(3):
            ci = 0 if tg < 2 else 1
            ts = slice(tg * 512, (tg + 1) * 512)
            k.dma('sp', x[:], xs[:, :, ts].rearrange("kc p t -> p kc t"), reads=['xs'], writes=['x'])
            ada_norm(x, 'x', h, 'h', 512, l, 1, ci, sq, rs, 0)
            for bi, (c0, ncol) in enumerate(blocks):
                st, sk = wst.next()
                wb, wk = wbf.next()
                k.dma('sp', st[:, :, 0:ncol], wi[:, :, c0:c0 + ncol], writes=[sk])
                k.op('pool', lambda: G.tensor_copy(out=wb[:, :, 0:ncol], in_=st[:, :, 0:ncol]), reads=[sk], writes=[wk])
                for q in range(4):
                    pb = 1 + (nev % 6)
                    nev += 1
                    if bi in (0, 2):
                        for kc in range(KC):
                            k.op('pe', lambda: PE.matmul(ps[pb][:], wb[:, kc, q * 128:(q + 1) * 128], h[:, kc, :], start=(kc == 0), stop=(kc == KC - 1)),
                                 reads=[wk, 'h'], writes=[pk[pb]])
                        o_, okey = ob.next()
                        if bi == 0:
                            k.op('act', lambda: A.activation(out=o_[:], in_=ps[pb][:], func=AF.Gelu_apprx_tanh), reads=[pk[pb]], writes=[okey])
                            k.dma('sp', uT[q, :, ts], o_[:], reads=[okey], writes=['uT'], semkey=(okey, 'st'))
                        else:
                            evac(nev, o_[:], ps[pb][:], [pk[pb]], [okey])
                            k.dma('sp', zbT[q, :, ts], o_[:], reads=[okey], writes=['zbT'], semkey=(okey, 'st'))
                    else:
                        trow = tg * 512 + q * 128
                        for kc in range(KC):
                            k.op('pe', lambda: PE.matmul(ps[pb][:, 0:ncol], h[:, kc, q * 128:(q + 1) * 128], wb[:, kc, 0:ncol], start=(kc == 0), stop=(kc == KC - 1)),
                                 reads=[wk, 'h'], writes=[pk[pb]])
                        f_, fkey = of.next()
                        if bi == 1:
                            k.op('act', lambda: A.activation(out=f_[:], in_=ps[pb][:], func=AF.Gelu_apprx_tanh), reads=[pk[pb]], writes=[fkey])
                            k.op('dve', lambda: V.tensor_reduce(out=st1[:, 0:1], in_=f_[:], axis=AX.X, op=OP.add), reads=[fkey], writes=['st1'])
                            k.op('dve', lambda: V.tensor_scalar(out=st1[:, 1:2], in0=st1[:, 0:1], scalar1=1.0 / 512, scalar2=None, op0=OP.mult), reads=['st1'], writes=['st1'])
                            k.op('dve', lambda: V.tensor_scalar(out=f_[:], in0=f_[:], scalar1=st1[:, 1:2], scalar2=None, op0=OP.subtract), reads=[fkey, 'st1'], writes=[fkey])
                            s_, sk2 = sq.next()
                            k.op('dve', lambda: V.tensor_tensor(out=s_[:], in0=f_[:], in1=f_[:], op=OP.mult), reads=[fkey], writes=[sk2])
                            k.op('dve', lambda: V.tensor_reduce(out=st1[:, 2:3], in_=s_[:], axis=AX.X, op=OP.add), reads=[sk2], writes=['st1'])
                            k.op('act', lambda: A.activation(out=st1[:, 3:4], in_=st1[:, 2:3], func=AF.Sqrt, bias=epsl[:, 0:1], scale=1.0 / 512), reads=['st1', 'epsl'], writes=['st1'])
                            k.op('dve', lambda: V.reciprocal(out=st1[:, 3:4], in_=st1[:, 3:4]), reads=['st1'], writes=['st1'])
                            k.op('dve', lambda: V.scalar_tensor_tensor(out=f_[:], in0=f_[:], scalar=st1[:, 3:4], in1=lng[:], op0=OP.mult, op1=OP.mult),
                                 reads=[fkey, 'st1', 'lng'], writes=[fkey])
                            o_, okey = ob.next()
                            k.op('dve', lambda: V.tensor_tensor(out=o_[:], in0=f_[:], in1=lnb[:], op=OP.add), reads=[fkey, 'lnb'], writes=[okey])
                            k.dma('sp', vn[trow:trow + 128, :], o_[:], reads=[okey], writes=['vn'], semkey=(okey, 'st'))
                        else:
                            evac(nev, f_[:, 0:ncol], ps[pb][:, 0:ncol], [pk[pb]], [fkey])
                            cz = c0 - 1536
                            k.dma('sp', zc[64 + trow:64 + trow + 128, cz:cz + ncol], f_[:, 0:ncol], reads=[fkey], writes=['zc'], semkey=(fkey, 'st'))

    def phase_gmlp(l):
        wsf = k.sb("wsf", [128, 4, 128], F32)
        wsT = k.sb("wsT", [128, 4, 128], BF16)
        bsb = k.sb("bsb", [128, 512], F32)
        vt = Ring(k, "vt", [128, 512], BF16, 2)
        ut = Ring(k, "ut", [128, 4, 128], BF16, 2)
        tm = Ring(k, "tm", [128, 512], F32, 2)
        yo = Ring(k, "yo", [128, 4, 128], BF16, 2)
        k.dma('sp', wsf[:], W["sgu_w"][l].rearrange("h t s -> t h s"), writes=['wsf'])
        bcast_load(bsb[:], W["sgu_b"][l].rearrange("h t -> (h t)"), 'bsb')
        for hh in range(4):
            k.op('pe', lambda: PE.transpose(out=ps[0][:, hh * 128:(hh + 1) * 128], in_=wsf[:, hh, :], identity=ident[:]), reads=['wsf', 'ident'], writes=['ps0'])
        k.op('dve', lambda: V.tensor_copy(out=wsT[:].rearrange("p a b -> p (a b)"), in_=ps[0][:]), reads=['ps0'], writes=['wsT'])
        for tt in range(NTT):
            ts = slice(tt * 128, (tt + 1) * 128)
            v_, vk = vt.next()
            u_, uk = ut.next()
            k.dma('sp', v_[:], vn[ts, :], reads=['vn'], writes=[vk])
            k.dma('sp', u_[:], uT[:, :, ts].rearrange("q p t -> p q t"), reads=['uT'], writes=[uk])
            pb = 1 + tt % 2
            for hh in range(4):
                k.op('pe', lambda: PE.matmul(ps[pb][:, hh * 128:(hh + 1) * 128], v_[:, hh * 128:(hh + 1) * 128], wsT[:, hh, :], start=True, stop=True),
                     reads=[vk, 'wsT'], writes=[pk[pb]])
            t_, tk = tm.next()
            k.op('dve', lambda: V.tensor_tensor(out=t_[:], in0=ps[pb][:], in1=bsb[:], op=OP.add), reads=[pk[pb], 'bsb'], writes=[tk])
            y_, yk = yo.next()
            k.op('dve', lambda: V.tensor_tensor(out=y_[:].rearrange("p a b -> p (a b)"), in0=t_[:], in1=u_[:].rearrange("p a b -> p (a b)"), op=OP.mult),
                 reads=[tk, uk], writes=[yk])
            k.dma('sp', ymixT[0:4, :, ts].rearrange("q p t -> p q t"), y_[:], reads=[yk], writes=['ymixT'], semkey=(yk, 'st'))

    def phase_fnet(l):
        stg = k.sb("stg", [128, 8, 1024], F32)
        CLb = k.sb("CLb", [128, 8, 1024], BF16)
        SLb = k.sb("SLb", [128, 8, 1024], BF16)
        CBb = k.sb("CBb", [128, 2, 256], BF16)
        SBb = k.sb("SBb", [128, 2, 256], BF16)
        CSb = k.sb("CSb", [128, 256], BF16)
        Atm = k.sb("Atm", [128, NTT, 4, 256], BF16)
        zb = Ring(k, "zb", [128, 4, 128], BF16, 2)
        yo = Ring(k, "yo", [128, 512], BF16, 2)
        for src, dst, key, shp in [(CL_in, CLb, 'CLb', (8, 1024)), (nSL_in, SLb, 'SLb', (8, 1024)), (CB_in, CBb, 'CBb', (2, 256)), (nSB_in, SBb, 'SBb', (2, 256))]:
            a, b = shp
            k.dma('sp', stg[:, 0:a, 0:b], src.rearrange("(sc p) t -> p sc t", p=128), writes=['stg'])
            k.op('pool', lambda: G.tensor_copy(out=dst[:], in_=stg[:, 0:a, 0:b]), reads=['stg'], writes=[key])
        k.dma('sp', stg[:, 0, 0:256], CSd_in, writes=['stg'])
        k.op('pool', lambda: G.tensor_copy(out=CSb[:], in_=stg[:, 0, 0:256]), reads=['stg'], writes=['CSb'])
        for tt in range(NTT):
            ts = slice(tt * 128, (tt + 1) * 128)
            z_, zk = zb.next()
            k.dma('sp', z_[:], zbT[:, :, ts].rearrange("q p t -> p q t"), reads=['zbT'], writes=[zk])
            for half in range(2):
                pb = 1 + (2 * tt + half) % 4
                for gg in range(2):
                    g_ = half * 2 + gg
                    k.op('pe', lambda: PE.matmul(ps[pb][:, gg * 256:(gg + 1) * 256], z_[:, g_, :], CSb[:], start=True, stop=True), reads=[zk, 'CSb'], writes=[pk[pb]])
                evac(tt + half, Atm[:, tt, half * 2:half * 2 + 2, :].rearrange("p a b -> p (a b)"), ps[pb][:], [pk[pb]], ['Atm%d' % tt])
        n = 0
        for g_ in range(4):
            for half in range(2):
                pb = 5 + n % 2
                n += 1
                cs_ = slice(half * 512, (half + 1) * 512)
                for sc in range(8):
                    k.op('pe', lambda: PE.matmul(ps[pb][:], Atm[:, sc, g_, 0:128], CLb[:, sc, cs_], start=(sc == 0), stop=False), reads=['Atm%d' % sc, 'CLb'], writes=[pk[pb]])
                    k.op('pe', lambda: PE.matmul(ps[pb][:], Atm[:, sc, g_, 128:256], SLb[:, sc, cs_], start=False, stop=(sc == 7)), reads=['Atm%d' % sc, 'SLb'], writes=[pk[pb]])
                y_, yk = yo.next()
                evac(n, y_[:], ps[pb][:], [pk[pb]], [yk])
                k.dma('sp', ymixT[4 + g_, :, cs_], y_[:], reads=[yk], writes=['ymixT'], semkey=(yk, 'st'))
            for seg in range(2):
                pb = 5 + n % 2
                n += 1
                for sc in range(2):
                    tt = 8 + seg * 2 + sc
                    k.op('pe', lambda: PE.matmul(ps[pb][:, 0:256], Atm[:, tt, g_, 0:128], CBb[:, sc, :], start=(sc == 0), stop=False), reads=['Atm%d' % tt, 'CBb'], writes=[pk[pb]])
                    k.op('pe', lambda: PE.matmul(ps[pb][:, 0:256], Atm[:, tt, g_, 128:256], SBb[:, sc, :], start=False, stop=(sc == 1)), reads=['Atm%d' % tt, 'SBb'], writes=[pk[pb]])
                y_, yk = yo.next()
                evac(n, y_[:, 0:256], ps[pb][:, 0:256], [pk[pb]], [yk])
                k.dma('sp', ymixT[4 + g_, :, 1024 + seg * 256:1280 + seg * 256], y_[:, 0:256], reads=[yk], writes=['ymixT'], semkey=(yk, 'st'))

    def phase_shift(l):
        HC = CIN // 2
        mub = k.sb("mub", [128, HC], F32)
        omm = k.sb("omm", [128, HC], F32)
        cm = k.sb("cm", [128, 2, 4, HC], F32)
        zl = [Ring(k, "zl%d" % o, [128, HC], F32, 2) for o in range(5)]
        acc = Ring(k, "acc", [128, HC], F32, 2)
        tmp = Ring(k, "tmp", [128, HC], F32, 2)
        offs = [-1, 1, -64, 64]
        for hc in range(2):
            cs_ = slice(hc * HC, (hc + 1) * HC)
            bcast_load(mub[:], W["shift_mu"][l][cs_], 'mub')
            for ty in range(2):
                for o in range(4):
                    bcast_load(cm[:, ty, o, :], ind_in[ty, o, cs_], 'cm')
            k.op('dve', lambda: V.tensor_scalar(out=omm[:], in0=mub[:], scalar1=-1.0, scalar2=1.0, op0=OP.mult, op1=OP.add), reads=['mub'], writes=['omm'])
            k.op('dve', lambda: V.tensor_tensor(out=cm[:].rearrange("p a b c -> p (a b) c"), in0=cm[:].rearrange("p a b c -> p (a b) c"),
                                                in1=mub[:].unsqueeze(1).to_broadcast([128, 8, HC]), op=OP.mult), reads=['cm', 'mub'], writes=['cm'])
            for tt in range(NTT):
                ty = 0 if tt < 8 else 1
                r0 = 64 + tt * 128
                a_, ak = acc.next()
                z0, z0k = zl[4].next()
                k.dma('sp', z0[:], zc[r0:r0 + 128, cs_], reads=['zc'], writes=[z0k])
                k.op('dve', lambda: V.tensor_tensor(out=a_[:], in0=z0[:], in1=omm[:], op=OP.mult), reads=[z0k, 'omm'], writes=[ak])
                for o in range(4 if ty == 0 else 2):
                    zo, zok = zl[o].next()
                    k.dma('sp', zo[:], zc[r0 + offs[o]:r0 + offs[o] + 128, cs_], reads=['zc'], writes=[zok])
                    t_, tk = tmp.next()
                    k.op('pool', lambda: G.tensor_tensor(out=t_[:], in0=zo[:], in1=cm[:, ty, o, :], op=OP.mult), reads=[zok, 'cm'], writes=[tk])
                    k.op('dve', lambda: V.scalar_tensor_tensor(out=a_[:], in0=t_[:], scalar=valid[:, tt, o:o + 1], in1=a_[:], op0=OP.mult, op1=OP.add),
                         reads=[ak, tk, 'valid'], writes=[ak])
                k.dma('sp', ztm[tt * 128:(tt + 1) * 128, cs_], a_[:], reads=[ak], writes=['ztm'], semkey=(ak, 'st'))


    CDEC = 0.6065306597126334

    def phase_scan(l):
        import os
        SS = int(os.environ.get('SCAN_STOP', '99'))
        NTS = int(os.environ.get('SCAN_TILES', str(NTT)))
        bc = {}
        for nm, src in [('w0f', W["decay_w0"][l, 0]), ('w0b', W["decay_w0"][l, 1]), ('a0f', W["iclr_a0"][l, 0]), ('a0b', W["iclr_a0"][l, 1]),
                        ('kk', W["k_k"][l]), ('ka', W["k_a"][l]), ('rk', W["r_k"][l].rearrange("h n -> (h n)"))]:
            bc[nm] = k.sb("bc_" + nm, [128, 1024], F32)
            bcast_load(bc[nm][:], src, 'bc_' + nm)
        f1 = k.sb("f1", [128, 1024], F32)
        f2 = k.sb("f2", [128, 1024], F32)
        stg = f1
        w2b = k.sb("w2b", [128, 1024], BF16)
        a2b = k.sb("a2b", [128, 1024], BF16)
        g2b = k.sb("g2b", [128, 2, 1024], BF16)
        idb = k.sb("idb", [128, 128], BF16)
        onec = k.sb("onec", [128, 1], F32)
        k.op('dve', lambda: V.memset(onec[:], 1.0), writes=['onec'])
        k.op('dve', lambda: V.tensor_copy(out=idb[:], in_=ident[:]), reads=['ident'], writes=['idb'])
        for src, dst, key in [(W["decay_w2"][l].rearrange("d r c -> (d r) c"), w2b[:], 'w2b'), (W["iclr_a2"][l].rearrange("d r c -> (d r) c"), a2b[:], 'a2b'),
                              (W["gate_w2"][l][0:128, :], g2b[:, 0, :], 'g2b')]:
            k.dma('sp', stg[:], src, writes=['f1'])
            k.op('pool', lambda: G.tensor_copy(out=dst, in_=stg[:]), reads=['f1'], writes=[key])
        k.dma('sp', stg[0:32, :], W["gate_w2"][l][128:160, :], writes=['f1'])
        k.op('pool', lambda: G.tensor_copy(out=g2b[0:32, 1, :], in_=stg[0:32, :]), reads=['f1'], writes=['g2b'])
        z = k.sb("z", [128, CIN], F32)
        t128 = k.sb("t128", [128, 416], F32)
        trT = k.sb("trT", [128, 4, 128], BF16)
        sig = k.sb("sig", [128, 2, 1024], F32)
        av = k.sb("av", [128, 2, 1024], F32)
        kkv = k.sb("kkv", [128, 1024], F32)
        kmv = k.sb("kmv", [128, 2, 1024], F32)
        bv = k.sb("bv", [128, 2, 1024], F32)
        sm = k.sb("sm", [128, 64], F32)
        Vb = k.sb("Vb", [128, 1024], BF16)
        At = k.sb("At", [128, 1024], F32)
        Bt = k.sb("Bt", [128, 1024], F32)
        Kt = k.sb("Kt", [128, 1024], F32)
        Rt = k.sb("Rt", [128, 1024], F32)
        Btb = k.sb("Btb", [128, 1024], BF16)
        Ktb = k.sb("Ktb", [128, 1024], BF16)
        Rtb = k.sb("Rtb", [128, 1024], BF16)
        XT4 = k.sb("XT4", [128, 4, 8, 128], BF16)
        M5 = k.sb("M5", [128, 5, 16, 128], BF16)
        Xp = k.sb("Xp", [128, 2, 2, 16, 128], BF16)
        Zf = k.sb("Zf", [128, 16, 128], F32)
        Zb = k.sb("Zb", [128, 16, 128], BF16)
        nZb = k.sb("nZb", [128, 16, 128], BF16)
        o64 = Ring(k, "o64", [64, 16, 128], F32, 1)
        oy = Ring(k, "oy", [128, 1024], F32, 1)
        od = Ring(k, "od", [64, 16], F32, 2)
        pbn = [0]

        def nb():
            pbn[0] += 1
            return pbn[0] % 8

        for tt in range(NTS):
            ts = slice(tt * 128, (tt + 1) * 128)
            k.dma('sp', z[:], ztm[ts, :], reads=['ztm'], writes=['z'])
            r_ = z[:, 0:1024]
            k_ = z[:, 1024:2048]
            v_ = z[:, 2048:3072]
            k.op('act', lambda: A.activation(out=t128[:, 0:128], in_=z[:, 3072:3200], func=AF.Tanh), reads=['z'], writes=['t128'])
            k.op('act', lambda: A.activation(out=t128[:, 256:416], in_=z[:, 3328:3488], func=AF.Sigmoid), reads=['z'], writes=['t128'])
            k.op('dve', lambda: V.tensor_copy(out=t128[:, 128:256], in_=z[:, 3200:3328]), reads=['z'], writes=['t128'])
            p0 = nb()
            for q, (c0, n) in enumerate([(0, 128), (128, 128), (256, 128), (384, 32)]):
                k.op('pe', lambda: PE.transpose(out=ps[p0][0:n, q * 128:(q + 1) * 128], in_=t128[:, c0:c0 + n], identity=ident[:]), reads=['t128', 'ident'], writes=[pk[p0]])
            k.op('dve', lambda: V.tensor_copy(out=trT[:, 0:3, :].rearrange("p a b -> p (a b)"), in_=ps[p0][:, 0:384]), reads=[pk[p0]], writes=['trT'])
            k.op('dve', lambda: V.tensor_copy(out=trT[0:32, 3, :], in_=ps[p0][0:32, 384:512]), reads=[pk[p0]], writes=['trT'])
            for d in range(2):
                for (wsrc, wkey, bsrc, dst, q) in [(w2b, 'w2b', bc['w0f' if d == 0 else 'w0b'], sig, 0), (a2b, 'a2b', bc['a0f' if d == 0 else 'a0b'], av, 1)]:
                    for hf in range(2):
                        p1 = nb()
                        cs_ = slice(hf * 512, (hf + 1) * 512)
                        k.op('pe', lambda: PE.matmul(ps[p1][:], trT[d * 64:(d + 1) * 64, q, :], wsrc[d * 64:(d + 1) * 64, cs_], start=True, stop=True),
                             reads=['trT', wkey], writes=[pk[p1]])
                        k.op('dve', lambda: V.tensor_tensor(out=f1[:, cs_], in0=ps[p1][:], in1=bsrc[:, cs_], op=OP.add), reads=[pk[p1], 'bc_w0f', 'bc_w0b', 'bc_a0f', 'bc_a0b'], writes=['f1'])
                    k.op('act', lambda: A.activation(out=dst[:, d, :], in_=f1[:], func=AF.Sigmoid), reads=['f1'], writes=['sig' if q == 0 else 'av'])
            g_, gk = oy.next()
            for hf in range(2):
                p1 = nb()
                cs_ = slice(hf * 512, (hf + 1) * 512)
                k.op('pe', lambda: PE.matmul(ps[p1][:], trT[:, 2, :], g2b[:, 0, cs_], start=True, stop=False), reads=['trT', 'g2b'], writes=[pk[p1]])
                k.op('pe', lambda: PE.matmul(ps[p1][:], trT[0:32, 3, :], g2b[0:32, 1, cs_], start=False, stop=True), reads=['trT', 'g2b'], writes=[pk[p1]])
                evac(hf, g_[:, cs_], ps[p1][:], [pk[p1]], [gk])
            k.dma('sp', scGt[ts, :], g_[:], reads=[gk], writes=['scGt'], semkey=(gk, 'st'))
            k.op('dve', lambda: V.tensor_tensor(out=kkv[:], in0=k_, in1=bc['kk'][:], op=OP.mult), reads=['z', 'bc_kk'], writes=['kkv'])
            k.op('pool', lambda: G.tensor_tensor(out=f2[:], in0=kkv[:], in1=kkv[:], op=OP.mult), reads=['kkv'], writes=['f2'])
            k.op('dve', lambda: V.tensor_reduce(out=sm[:, 0:16], in_=f2[:].rearrange("p (h n) -> p h n", n=64), axis=AX.X, op=OP.add), reads=['f2'], writes=['sm'])
            k.op('act', lambda: A.activation(out=sm[:, 0:16], in_=sm[:, 0:16], func=AF.Sqrt), reads=['sm'], writes=['sm'])
            k.op('dve', lambda: V.tensor_scalar(out=sm[:, 0:16], in0=sm[:, 0:16], scalar1=1e-12, scalar2=None, op0=OP.max), reads=['sm'], writes=['sm'])
            k.op('dve', lambda: V.reciprocal(out=sm[:, 0:16], in_=sm[:, 0:16]), reads=['sm'], writes=['sm'])
            k.op('dve', lambda: V.tensor_tensor(out=kkv[:].rearrange("p (h n) -> p h n", n=64), in0=kkv[:].rearrange("p (h n) -> p h n", n=64),
                                                in1=sm[:, 0:16].unsqueeze(2).to_broadcast([128, 16, 64]), op=OP.mult), reads=['kkv', 'sm'], writes=['kkv'])
            k.op('pool', lambda: G.tensor_tensor(out=f2[:], in0=r_, in1=bc['rk'][:], op=OP.mult), reads=['z', 'bc_rk'], writes=['f2'])
            for d in range(2):
                k.op('dve', lambda: V.scalar_tensor_tensor(out=f1[:], in0=av[:, d, :], scalar=-1.0, in1=bc['ka'][:], op0=OP.add, op1=OP.mult), reads=['av', 'bc_ka'], writes=['f1'])
                k.op('dve', lambda: V.scalar_tensor_tensor(out=kmv[:, d, :], in0=f1[:], scalar=1.0, in1=k_, op0=OP.add, op1=OP.mult), reads=['f1', 'z'], writes=['kmv'])
                k.op('pool', lambda: G.tensor_tensor(out=bv[:, d, :], in0=kkv[:], in1=av[:, d, :], op=OP.mult), reads=['kkv', 'av'], writes=['bv'])
                k.op('dve', lambda: V.tensor_tensor(out=f1[:], in0=kmv[:, d, :], in1=f2[:], op=OP.mult), reads=['kmv', 'f2'], writes=['f1'])
                k.op('dve', lambda: V.tensor_reduce(out=sm[:, 16 + 16 * d:32 + 16 * d], in_=f1[:].rearrange("p (h n) -> p h n", n=64), axis=AX.X, op=OP.add), reads=['f1'], writes=['sm'])
            k.op('dve', lambda: V.tensor_tensor(out=sm[:, 48:64], in0=sm[:, 16:32], in1=sm[:, 32:48], op=OP.add), reads=['sm'], writes=['sm'])
            k.dma('sp', scB[ts, :], sm[:, 48:64], reads=['sm'], writes=['scB'], semkey=('scB',))
            k.op('pool', lambda: G.tensor_copy(out=Vb[:], in_=v_), reads=['z'], writes=['Vb'])
            for d in range(2 if SS > 1 else 0):
                tri_i = tri[:, 0 if d == 0 else 2, :]
                tri_e = tri[:, 1 if d == 0 else 3, :]
                m_st = tri[:, 1 if d == 0 else 3, :]
                m_ts = tri[:, 3 if d == 0 else 1, :]
                m_in = tri[:, 0 if d == 0 else 2, :]
                pe_ = [nb(), nb()]
                pi_ = [nb(), nb()]
                for hf in range(2):
                    cs_ = slice(hf * 512, (hf + 1) * 512)
                    k.op('pe', lambda: PE.matmul(ps[pe_[hf]][:], tri_e, sig[:, d, cs_], start=True, stop=True), reads=['tri', 'sig'], writes=[pk[pe_[hf]]])
                    k.op('pe', lambda: PE.matmul(ps[pi_[hf]][:], tri_i, sig[:, d, cs_], start=True, stop=True), reads=['tri', 'sig'], writes=[pk[pi_[hf]]])
                for hf in range(2):
                    cs_ = slice(hf * 512, (hf + 1) * 512)
                    k.op('act', lambda: A.activation(out=f1[:, cs_], in_=ps[pe_[hf]][:], func=AF.Exp, scale=-CDEC), reads=[pk[pe_[hf]]], writes=['f1'])
                    k.op('dve', lambda: V.tensor_tensor(out=At[:, cs_], in0=kkv[:, cs_], in1=f1[:, cs_], op=OP.mult), reads=['kkv', 'f1'], writes=['At'])
                    k.op('act', lambda: A.activation(out=f2[:, cs_], in_=ps[pi_[hf]][:], func=AF.Exp, scale=CDEC), reads=[pk[pi_[hf]]], writes=['f2'])
                    k.op('dve', lambda: V.tensor_tensor(out=Bt[:, cs_], in0=bv[:, d, cs_], in1=f2[:, cs_], op=OP.mult), reads=['bv', 'f2'], writes=['Bt'])
                    k.op('pool', lambda: G.tensor_tensor(out=Kt[:, cs_], in0=kmv[:, d, cs_], in1=f2[:, cs_], op=OP.mult), reads=['kmv', 'f2'], writes=['Kt'])
                    k.op('act', lambda: A.activation(out=f1[:, cs_], in_=ps[pi_[hf]][:], func=AF.Exp, scale=-CDEC), reads=[pk[pi_[hf]]], writes=['f1'])
                    k.op('dve', lambda: V.tensor_tensor(out=Rt[:, cs_], in0=r_[:, cs_], in1=f1[:, cs_], op=OP.mult), reads=['z', 'f1'], writes=['Rt'])
                k.op('pool', lambda: G.tensor_copy(out=Btb[:], in_=Bt[:]), reads=['Bt'], writes=['Btb'])
                k.op('pool', lambda: G.tensor_copy(out=Ktb[:], in_=Kt[:]), reads=['Kt'], writes=['Ktb'])
                k.op('pool', lambda: G.tensor_copy(out=Rtb[:], in_=Rt[:]), reads=['Rt'], writes=['Rtb'])
                if SS <= 2:
                    continue
                p1 = nb()
                for hh in range(16):
                    k.op('pe', lambda: PE.matmul(ps[p1][0:64, hh:hh + 1], sig[:, d, hh * 64:(hh + 1) * 64], onec[:, 0:1], start=True, stop=True), reads=['sig', 'onec'], writes=[pk[p1]])
                d_, dk = od.next()
                k.op('act', lambda: A.activation(out=d_[:], in_=ps[p1][0:64, 0:16], func=AF.Exp, scale=-CDEC), reads=[pk[p1]], writes=[dk])
                k.dma('sp', scD[tt, d], d_[:], reads=[dk], writes=['scD'], semkey=(dk, 'st'))
                if SS <= 3:
                    continue
                for qi, (src, skey) in enumerate([(At, 'At'), (Bt, 'Bt'), (Kt, 'Kt'), (Rt, 'Rt')]):
                    for hf in range(2):
                        p1 = nb()
                        for q in range(4):
                            pr = hf * 4 + q
                            k.op('pe', lambda: PE.transpose(out=ps[p1][:, q * 128:(q + 1) * 128], in_=src[:, pr * 128:(pr + 1) * 128], identity=ident[:]),
                                 reads=[skey, 'ident'], writes=[pk[p1]])
                        evac(qi + hf, XT4[:, qi, hf * 4:hf * 4 + 4, :].rearrange("p a b -> p (a b)"), ps[p1][:], [pk[p1]], ['XT4'])

                def hpos(hh):
                    return (hh % 2) * 8 + hh // 2

                def fm(qi, hh):
                    return XT4[(hh % 2) * 64:(hh % 2) * 64 + 64, qi, hh // 2, :]
                if SS <= 4:
                    continue
                for mi, (lq, rq, msk) in enumerate([(1, 0, m_st), (0, 1, m_ts), (2, 0, m_st), (1, 3, m_in), (2, 3, m_in)]):
                    for hg in range(4):
                        p1 = nb()
                        for q in range(4):
                            hh = 2 * ((hg % 2) * 4 + q) + hg // 2
                            k.op('pe', lambda: PE.matmul(ps[p1][:, q * 128:(q + 1) * 128], fm(lq, hh), fm(rq, hh), start=True, stop=True), reads=['XT4'], writes=[pk[p1]])
                        k.op('dve', lambda: V.tensor_tensor(out=M5[:, mi, hg * 4:hg * 4 + 4, :], in0=ps[p1][:].rearrange("p (a b) -> p a b", b=128),
                                                            in1=msk.unsqueeze(1).to_broadcast([128, 4, 128]), op=OP.mult), reads=[pk[p1], 'tri'], writes=['M5_%d' % mi])
                if SS <= 5:
                    continue
                k.op('pool', lambda: G.tensor_copy(out=Zf[:, :, 0:64], in_=At[:].rearrange("p (h n) -> p h n", n=64)), reads=['At'], writes=['Zf'])
                for hf in range(2):
                    p1 = nb()
                    for q in range(8):
                        hh = hf * 8 + q
                        k.op('pe', lambda: PE.matmul(ps[p1][:, q * 64:(q + 1) * 64], M5[:, 2, hpos(hh), :], Vb[:, hh * 64:(hh + 1) * 64], start=True, stop=True), reads=['M5_2', 'Vb'], writes=[pk[p1]])
                    k.op('dve', lambda: V.tensor_copy(out=Zf[:, hf * 8:hf * 8 + 8, 64:128], in_=ps[p1][:].rearrange("p (a b) -> p a b", b=64)), reads=[pk[p1]], writes=['Zf'])
                k.op('act', lambda: A.copy(out=Zb[:], in_=Zf[:]), reads=['Zf'], writes=['Zb'])
                if SS <= 6:
                    continue
                cur = None
                for it in range(7):
                    if it == 0:
                        Xc = lambda hh: M5[:, 1, hpos(hh), :]
                        XTc = lambda hh: M5[:, 0, hpos(hh), :]
                        xkeys = ['M5_0', 'M5_1']
                    else:
                        src_i = (it - 1) % 2
                        dst_i = it % 2
                        if it == 1:
                            Xs, XTs, skeys = (lambda hh: M5[:, 1, hpos(hh), :]), (lambda hh: M5[:, 0, hpos(hh), :]), ['M5_0', 'M5_1']
                        else:
                            Xs, XTs, skeys = (lambda hh, si=src_i: Xp[:, si, 0, hh, :]), (lambda hh, si=src_i: Xp[:, si, 1, hh, :]), ['Xp%d' % src_i]
                        for which in range(2):
                            for hg in range(4):
                                p1 = nb()
                                for q in range(4):
                                    hh = hg * 4 + q
                                    if which == 0:
                                        k.op('pe', lambda: PE.matmul(ps[p1][:, q * 128:(q + 1) * 128], XTs(hh), Xs(hh), start=True, stop=True), reads=skeys, writes=[pk[p1]])
                                    else:
                                        k.op('pe', lambda: PE.matmul(ps[p1][:, q * 128:(q + 1) * 128], Xs(hh), XTs(hh), start=True, stop=True), reads=skeys, writes=[pk[p1]])
                                evac(hg, Xp[:, dst_i, which, hg * 4:hg * 4 + 4, :].rearrange("p a b -> p (a b)"), ps[p1][:], [pk[p1]], ['Xp%d' % dst_i])
                        XTc = lambda hh, di=dst_i: Xp[:, di, 1, hh, :]
                        xkeys = ['Xp%d' % dst_i]
                    for hg in range(4):
                        p1 = nb()
                        for q in range(4):
                            hh = hg * 4 + q
                            k.op('pe', lambda: PE.matmul(ps[p1][:, q * 128:(q + 1) * 128], XTc(hh), Zb[:, hh, :], start=True, stop=True), reads=xkeys + ['Zb'], writes=[pk[p1]])
                        k.op('dve', lambda: V.tensor_tensor(out=Zf[:, hg * 4:hg * 4 + 4, :], in0=Zf[:, hg * 4:hg * 4 + 4, :], in1=ps[p1][:].rearrange("p (a b) -> p a b", b=128),
                                                            op=(OP.subtract if it == 0 else OP.add)), reads=[pk[p1], 'Zf'], writes=['Zf'])
                    k.op('act', lambda: A.copy(out=Zb[:], in_=Zf[:]), reads=['Zf'], writes=['Zb'])
                k.op('act', lambda: A.mul(out=nZb[:], in_=Zf[:], mul=-1.0), reads=['Zf'], writes=['nZb'])
                if SS <= 7:
                    continue
                o_, okey = o64.next()
                for hf in range(2):
                    p1 = nb()
                    for q in range(8):
                        hh = hf * 8 + q
                        k.op('pe', lambda: PE.matmul(ps[p1][0:64, q * 64:(q + 1) * 64], Zb[:, hh, 0:64], Btb[:, hh * 64:(hh + 1) * 64], start=True, stop=True), reads=['Zb', 'Btb'], writes=[pk[p1]])
                    k.op('dve', lambda: V.tensor_tensor(out=o_[:, hf * 8:hf * 8 + 8, 0:64], in0=ident[0:64, 0:64].unsqueeze(1).to_broadcast([64, 8, 64]),
                                                        in1=ps[p1][0:64, :].rearrange("p (a b) -> p a b", b=64), op=OP.subtract), reads=[pk[p1], 'ident'], writes=[okey])
                k.dma('sp', scP[tt, d], o_[:, :, 0:64], reads=[okey], writes=['scP'], semkey=(okey, 'st'))
                o_, okey = o64.next()
                for hf in range(2):
                    p1 = nb()
                    for q in range(8):
                        hh = hf * 8 + q
                        k.op('pe', lambda: PE.matmul(ps[p1][0:64, q * 64:(q + 1) * 64], Ktb[:, hh * 64:(hh + 1) * 64], Vb[:, hh * 64:(hh + 1) * 64], start=True, stop=False), reads=['Ktb', 'Vb'], writes=[pk[p1]])
                        k.op('pe', lambda: PE.matmul(ps[p1][0:64, q * 64:(q + 1) * 64], Btb[:, hh * 64:(hh + 1) * 64], nZb[:, hh, 64:128], start=False, stop=True), reads=['Btb', 'nZb'], writes=[pk[p1]])
                    evac(hf, o_[:, hf * 8:hf * 8 + 8, 0:64], ps[p1][0:64, :].rearrange("p (a b) -> p a b", b=64), [pk[p1]], [okey])
                k.dma('sp', scQ[tt, d], o_[:, :, 0:64], reads=[okey], writes=['scQ'], semkey=(okey, 'st'))
                o_, okey = o64.next()
                for hg in range(4):
                    p1 = nb()
                    for q in range(4):
                        hh = hg * 4 + q
                        k.op('pe', lambda: PE.matmul(ps[p1][0:64, q * 128:(q + 1) * 128], Rtb[:, hh * 64:(hh + 1) * 64], idb[:], start=True, stop=False), reads=['Rtb', 'idb'], writes=[pk[p1]])
                        k.op('pe', lambda: PE.matmul(ps[p1][0:64, q * 128:(q + 1) * 128], nZb[:, hh, 0:64], M5[:, 3, hpos(hh), :], start=False, stop=True), reads=['nZb', 'M5_3'], writes=[pk[p1]])
                    evac(hg, o_[:, hg * 4:hg * 4 + 4, :].rearrange("p a b -> p (a b)"), ps[p1][0:64, :], [pk[p1]], [okey])
                k.dma('sp', scG[tt, d], o_[:], reads=[okey], writes=['scG'], semkey=(okey, 'st'))
                y_, yk = oy.next()
                for hf in range(2):
                    p1 = nb()
                    for q in range(8):
                        hh = hf * 8 + q
                        k.op('pe', lambda: PE.matmul(ps[p1][:, q * 64:(q + 1) * 64], M5[:, 4, hpos(hh), :], Vb[:, hh * 64:(hh + 1) * 64], start=True, stop=False), reads=['M5_4', 'Vb'], writes=[pk[p1]])
                        k.op('pe', lambda: PE.matmul(ps[p1][:, q * 64:(q + 1) * 64], M5[:, 3, hpos(hh), :], nZb[:, hh, 64:128], start=False, stop=True), reads=['M5_3', 'nZb'], writes=[pk[p1]])
                    evac(hf, y_[:, hf * 512:(hf + 1) * 512], ps[p1][:], [pk[p1]], [yk])
                k.dma('sp', scY[tt, d], y_[:], reads=[yk], writes=['scY'], semkey=(yk, 'st'))


    def phase_chain(l):
        S = k.sb("S", [64, 2, 16, 64], F32)
        Ys = k.sb("Ys", [128, NTT, 1024], F32)
        Pt = Ring(k, "Pt", [64, 16, 64], F32, 2)
        Qt = Ring(k, "Qt", [64, 16, 64], F32, 2)
        Gt = Ring(k, "Gt", [64, 16, 128], F32, 2)
        Yt = Ring(k, "Yt", [128, 1024], F32, 2)
        Dt = Ring(k, "Dt", [64, 16], F32, 2)
        tq = k.sb("tq", [64, 16, 64], F32)
        s0t = k.sb("s0t", [128, 8, 64], F32)
        so = Ring(k, "so", [128, 8, 64], F32, 2)
        pbn = [0]

        def nb():
            pbn[0] += 1
            return pbn[0] % 8
        first = {}

        def out_state(seg, d):
            p1 = nb()
            for hp in range(8):
                k.op('pe', lambda: PE.transpose(out=ps[p1][:, hp * 64:(hp + 1) * 64], in_=S[:, d, 2 * hp:2 * hp + 2, :].rearrange("p a b -> p (a b)"), identity=ident[0:64, 0:64]),
                     reads=['S%d' % d, 'ident'], writes=[pk[p1]])
            o_, okey = so.next()
            k.op('dve', lambda: V.tensor_copy(out=o_[:].rearrange("p a b -> p (a b)"), in_=ps[p1][:]), reads=[pk[p1]], writes=[okey])
            k.dma('sp', st_out[l, seg, d].rearrange("(hp h2) v kk -> (h2 v) hp kk", h2=2), o_[:], reads=[okey], semkey=(okey, 'st'))

        def step(c, d):
            P_, Pk = Pt.next()
            Q_, Qk = Qt.next()
            G_, Gk = Gt.next()
            Y_, Yk = Yt.next()
            D_, Dk = Dt.next()
            k.dma('sp', P_[:], scP[c, d], reads=['scP'], writes=[Pk])
            k.dma('sp', Q_[:], scQ[c, d], reads=['scQ'], writes=[Qk])
            k.dma('sp', G_[:], scG[c, d], reads=['scG'], writes=[Gk])
            k.dma('sp', Y_[:], scY[c, d], reads=['scY'], writes=[Yk])
            k.dma('sp', D_[:], scD[c, d], reads=['scD'], writes=[Dk])
            sk_ = 'S%d' % d
            for hf in range(2):
                p1 = nb()
                cs_ = slice(hf * 512, (hf + 1) * 512)
                for q in range(8):
                    hh = hf * 8 + q
                    k.op('pe', lambda: PE.matmul(ps[p1][:, q * 64:(q + 1) * 64], G_[:, hh, :], S[:, d, hh, :], start=True, stop=True), reads=[Gk, sk_], writes=[pk[p1]])
                if c not in first:
                    k.op('dve', lambda: V.tensor_tensor(out=Ys[:, c, cs_], in0=ps[p1][:], in1=Y_[:, cs_], op=OP.add), reads=[pk[p1], Yk], writes=['Ys%d' % c])
                else:
                    k.op('dve', lambda: V.tensor_tensor(out=Y_[:, cs_], in0=ps[p1][:], in1=Y_[:, cs_], op=OP.add), reads=[pk[p1], Yk], writes=[Yk])
                    k.op('pool', lambda: G.tensor_tensor(out=Ys[:, c, cs_], in0=Ys[:, c, cs_], in1=Y_[:, cs_], op=OP.add), reads=[Yk, 'Ys%d' % c], writes=['Ys%d' % c])
            first[c] = True
            pp = [nb(), nb()]
            for hf in range(2):
                for q in range(8):
                    hh = hf * 8 + q
                    k.op('pe', lambda: PE.matmul(ps[pp[hf]][0:64, q * 64:(q + 1) * 64], P_[:, hh, :], S[:, d, hh, :], start=True, stop=True), reads=[Pk, sk_], writes=[pk[pp[hf]]])
            for hf in range(2):
                hs = slice(hf * 8, hf * 8 + 8)
                k.op('dve', lambda: V.tensor_tensor(out=tq[:, hs, :], in0=ps[pp[hf]][0:64, :].rearrange("p (a b) -> p a b", b=64), in1=Q_[:, hs, :], op=OP.add),
                     reads=[pk[pp[hf]], Qk], writes=['tq'])
            k.op('dve', lambda: V.tensor_tensor(out=S[:, d], in0=tq[:], in1=D_[:].unsqueeze(2).to_broadcast([64, 16, 64]), op=OP.mult), reads=['tq', Dk], writes=[sk_])

        for d in range(2):
            k.dma('sp', s0t[:], s0_in[l, d].rearrange("(hp h2) v kk -> (h2 v) hp kk", h2=2), writes=['s0t'])
            for hf in range(2):
                p1 = nb()
                for q in range(4):
                    hp = hf * 4 + q
                    k.op('pe', lambda: PE.transpose(out=ps[p1][0:64, q * 128:(q + 1) * 128], in_=s0t[:, hp, :], identity=ident[:]), reads=['s0t', 'ident'], writes=[pk[p1]])
                k.op('dve', lambda: V.tensor_copy(out=S[:, d, hf * 8:hf * 8 + 8, :].rearrange("p a b -> p (a b)"), in_=ps[p1][0:64, :]), reads=[pk[p1]], writes=['S%d' % d])
            order = list(range(8)) if d == 0 else list(range(7, -1, -1))
            for n_, c in enumerate(order):
                step(c, d)
                if n_ % 2 == 1:
                    out_state(c // 2, d)
                    if n_ < 7:
                        k.op('dve', lambda: V.tensor_scalar(out=S[:, d], in0=S[:, d], scalar1=carry[0:64, 0:1], scalar2=None, op0=OP.mult), reads=['S%d' % d, 'carry'], writes=['S%d' % d])
        for seg in range(2):
            for d in range(2):
                k.op('dve', lambda: V.memset(S[:, d], 0.0), writes=['S%d' % d])
                cc = [8 + 2 * seg, 9 + 2 * seg]
                for c in (cc if d == 0 else cc[::-1]):
                    step(c, d)
                out_state(4 + seg, d)
        lg = k.sb("lg", [128, 1024], F32)
        lb = k.sb("lb", [128, 1024], F32)
        bcast_load(lg[:], W["lnx_g"][l], 'lg')
        bcast_load(lb[:], W["lnx_b"][l], 'lb')
        vt = Ring(k, "vt", [128, 1024], F32, 2)
        gt = Ring(k, "gt", [128, 1024], F32, 2)
        bt = Ring(k, "bt", [128, 16], F32, 2)
        sm = k.sb("sm", [128, 48], F32)
        f1 = k.sb("f1", [128, 1024], F32)
        yo = Ring(k, "yo", [128, 8, 128], BF16, 2)
        h3 = lambda ap: ap.rearrange("p (h n) -> p h n", n=64)
        for tt in range(NTT):
            ts = slice(tt * 128, (tt + 1) * 128)
            v_, vk = vt.next()
            g_, gk = gt.next()
            b_, bk = bt.next()
            k.dma('sp', v_[:], ztm[ts, 2048:3072], reads=['ztm'], writes=[vk])
            k.dma('sp', g_[:], scGt[ts, :], reads=['scGt'], writes=[gk])
            k.dma('sp', b_[:], scB[ts, :], reads=['scB'], writes=[bk])
            y = Ys[:, tt, :]
            yk_ = 'Ys%d' % tt
            k.op('dve', lambda: V.tensor_reduce(out=sm[:, 0:16], in_=h3(y), axis=AX.X, op=OP.add), reads=[yk_], writes=['sm'])
            k.op('dve', lambda: V.tensor_scalar(out=sm[:, 0:16], in0=sm[:, 0:16], scalar1=1.0 / 64, scalar2=None, op0=OP.mult), reads=['sm'], writes=['sm'])
            k.op('dve', lambda: V.tensor_tensor(out=h3(y), in0=h3(y), in1=sm[:, 0:16].unsqueeze(2).to_broadcast([128, 16, 64]), op=OP.subtract), reads=[yk_, 'sm'], writes=[yk_])
            k.op('pool', lambda: G.tensor_tensor(out=f1[:], in0=y, in1=y, op=OP.mult), reads=[yk_], writes=['f1'])
            k.op('dve', lambda: V.tensor_reduce(out=sm[:, 16:32], in_=h3(f1[:]), axis=AX.X, op=OP.add), reads=['f1'], writes=['sm'])
            k.op('act', lambda: A.activation(out=sm[:, 16:32], in_=sm[:, 16:32], func=AF.Sqrt, bias=epsx[:, 0:1], scale=1.0 / 64), reads=['sm', 'epsx'], writes=['sm'])
            k.op('dve', lambda: V.reciprocal(out=sm[:, 16:32], in_=sm[:, 16:32]), reads=['sm'], writes=['sm'])
            k.op('dve', lambda: V.tensor_tensor(out=h3(y), in0=h3(y), in1=sm[:, 16:32].unsqueeze(2).to_broadcast([128, 16, 64]), op=OP.mult), reads=[yk_, 'sm'], writes=[yk_])
            k.op('dve', lambda: V.tensor_tensor(out=y, in0=y, in1=lg[:], op=OP.mult), reads=[yk_, 'lg'], writes=[yk_])
            k.op('dve', lambda: V.tensor_tensor(out=h3(v_[:]), in0=h3(v_[:]), in1=b_[:].unsqueeze(2).to_broadcast([128, 16, 64]), op=OP.mult), reads=[vk, bk], writes=[vk])
            k.op('dve', lambda: V.tensor_tensor(out=y, in0=y, in1=lb[:], op=OP.add), reads=[yk_, 'lb'], writes=[yk_])
            k.op('dve', lambda: V.tensor_tensor(out=y, in0=y, in1=v_[:], op=OP.add), reads=[yk_, vk], writes=[yk_])
            k.op('dve', lambda: V.tensor_tensor(out=y, in0=y, in1=g_[:], op=OP.mult), reads=[yk_, gk], writes=[yk_])
            o_, okey = yo.next()
            for hf in range(2):
                p1 = nb()
                for q in range(4):
                    pr = hf * 4 + q
                    k.op('pe', lambda: PE.transpose(out=ps[p1][:, q * 128:(q + 1) * 128], in_=Ys[:, tt, pr * 128:(pr + 1) * 128], identity=ident[:]), reads=[yk_, 'ident'], writes=[pk[p1]])
                evac(hf, o_[:, hf * 4:hf * 4 + 4, :].rearrange("p a b -> p (a b)"), ps[p1][:], [pk[p1]], [okey])
            k.dma('sp', ymixT[8:16, :, ts].rearrange("q p t -> p q t"), o_[:], reads=[okey], writes=['ymixT'], semkey=(okey, 'st'))

    def phase_wout(l):
        x = k.sb("x", [128, KC, 512], F32)
        ym = k.sb("ym", [128, KC, 512], BF16)
        wo = k.sb("wo", [128, KC, D], BF16)
        wst = Ring(k, "wst", [128, KC, 256], F32, 2)
        wv = W["w_out"][l].rearrange("(kc p) n -> p kc n", p=128)
        for b in range(8):
            st, sk = wst.next()
            k.dma('sp', st[:], wv[:, :, b * 256:(b + 1) * 256], writes=[sk])
            k.op('pool', lambda: G.tensor_copy(out=wo[:, :, b * 256:(b + 1) * 256], in_=st[:]), reads=[sk], writes=['wo%d' % b])
        for tg in range(3):
            ci = 0 if tg < 2 else 1
            ts = slice(tg * 512, (tg + 1) * 512)
            k.dma('sp', x[:], xs[:, :, ts].rearrange("kc p t -> p kc t"), reads=['xs'], writes=['x'])
            k.dma('sp', ym[:], ymixT[:, :, ts].rearrange("kc p t -> p kc t"), reads=['ymixT'], writes=['ym'])
            for oc in range(KC):
                po = oc % 4
                for kc in range(KC):
                    k.op('pe', lambda: PE.matmul(ps[po][:], wo[:, kc, oc * 128:(oc + 1) * 128], ym[:, kc, :], start=(kc == 0), stop=(kc == KC - 1)),
                         reads=['wo%d' % (oc // 2), 'ym'], writes=[pk[po]])
                k.op('dve', lambda: V.scalar_tensor_tensor(out=x[:, oc, :], in0=ps[po][:], scalar=gtv[:, l, 1, oc, ci:ci + 1], in1=x[:, oc, :],
                                                           op0=OP.mult, op1=OP.add), reads=[pk[po], 'gtv', 'x'], writes=['x'])
            k.dma('sp', xs[:, :, ts].rearrange("kc p t -> p kc t"), x[:], reads=['x'], writes=['xs'])

    if phases is None or 'mod' in phases:
        phase_mod()
        phase_reset()
    if phases is None or 'in' in phases:
        phase_in()
        phase_reset()
    stages = []
    for l in range(DEPTH):
        stages.append(('ffn', l, 0))
        stages.append(('mix', l, 1))
        stages.append(('ffn', l, 2))
    nst = 0
    for kind, l, i in stages:
        if stop_after is not None and nst >= stop_after:
            break
        nst += 1
        if kind == 'ffn':
            if phases is None or 'ffn' in phases:
                phase_ffn(l, i)
                phase_reset()
        else:
            for nm, fn in [('win', phase_win), ('gmlp', phase_gmlp), ('fnet', phase_fnet), ('shift', phase_shift), ('scan', phase_scan), ('chain', phase_chain), ('wout', phase_wout)]:
                if mix_parts is None or nm in mix_parts:
                    fn(l)
                    phase_reset()
    if phases is None or 'out' in phases:
        phase_out()
    k.finish()
    print("program built: ninst", k.ninst, "dma sems", len(k.dpool))
    nc.used_weight_inputs = list(used_inputs)
    return nc


def host_consts(kind):
    o = {}
    ch = np.arange(CIN)
    ind = np.zeros((2, 4, CIN), np.float32)
    q = CIN // 4
    if kind == 'grid':
        for j in range(4):
            ind[0, j, j * q:(j + 1) * q] = 1
    else:
        ind[0, 0, :CIN // 2] = 1
        ind[0, 1, CIN // 2:] = 1
    ind[1, 0, :CIN // 2] = 1
    ind[1, 1, CIN // 2:] = 1
    o['ind'] = ind
    valid = np.zeros((NT, 4), np.float32)
    t = np.arange(1024)
    if kind == 'grid':
        valid[:1024, 0] = (t % 64 != 0)
        valid[:1024, 1] = (t % 64 != 63)
        valid[:1024, 2] = (t >= 64)
        valid[:1024, 3] = (t < 960)
    else:
        valid[:1024, 0] = (t % 256 != 0)
        valid[:1024, 1] = (t % 256 != 255)
    t2 = np.arange(512)
    valid[1024:, 0] = (t2 % 256 != 0)
    valid[1024:, 1] = (t2 % 256 != 255)
    o['valid'] = valid

    def dft(n):
        a = 2 * np.pi * np.outer(np.arange(n), np.arange(n)) / n
        return (np.cos(a) / np.sqrt(n)), (np.sin(a) / np.sqrt(n))
    if kind == 'grid':
        c, s = dft(1024)
    else:
        c256, s256 = dft(256)
        c = np.zeros((1024, 1024))
        s = np.zeros((1024, 1024))
        for j in range(4):
            c[j * 256:(j + 1) * 256, j * 256:(j + 1) * 256] = c256
            s[j * 256:(j + 1) * 256, j * 256:(j + 1) * 256] = s256
    o['CL'] = c.astype(np.float32)
    o['nSL'] = (-s).astype(np.float32)
    c256, s256 = dft(256)
    o['CB'] = c256.astype(np.float32)
    o['nSB'] = (-s256).astype(np.float32)
    cd, sd = dft(128)
    o['CSd'] = np.concatenate([cd, sd], 1).astype(np.float32)
    i = np.arange(128)
    ui = (i[:, None] <= i[None, :]).astype(np.float32)
    ue = (i[:, None] < i[None, :]).astype(np.float32)
    o['tri'] = np.stack([ui, ue, ui.T.copy(), ue.T.copy()])
    o['ident'] = np.eye(128, dtype=np.float32)
    o['onesm'] = np.full((128, 128), 1.0 / D, np.float32)
    o['carry'] = np.full((128, 1), 1.0 if kind == 'grid' else 0.0, np.float32)
    return o


WNAMES = ["norm_g", "w_mod", "b_mod", "ffn_w_in", "ffn_w_out", "w_in", "w_out", "sgu_ln_g", "sgu_ln_b", "sgu_w", "sgu_b",
          "shift_mu", "decay_w0", "decay_w2", "iclr_a0", "iclr_a2", "k_k", "k_a", "r_k", "gate_w2", "lnx_g", "lnx_b", "final_g"]


def core_seqs(c):
    if c < 4:
        return [2 * c, 2 * c + 1]
    return [8 + (c - 4) * 6 + j for j in range(6)]


def make_in_maps(inputs):
    f = lambda a: np.ascontiguousarray(np.asarray(a, dtype=np.float32))
    xp = f(inputs['x_prompt'])
    xsm = f(inputs['x_sample'])
    stw = f(inputs['state_wkv'])
    cc = f(inputs['c'])
    cctx = f(inputs['c_ctx'])
    wts = {n: f(inputs[n]) for n in WNAMES}
    cg = host_consts('grid')
    cs = host_consts('seq')
    maps = []
    for c in range(8):
        m = dict(wts)
        seqs = core_seqs(c)
        if c < 4:
            m.update(cg)
            m['xin'] = np.concatenate([xsm[c]] + [xp[s] for s in seqs], 0)
            m['cond'] = np.stack([cc[c], cctx])
            m['s0'] = np.ascontiguousarray(stw[c])
        else:
            m.update(cs)
            m['xin'] = np.concatenate([xp[s] for s in seqs], 0)
            m['cond'] = np.stack([cctx, cctx])
            m['s0'] = np.zeros((DEPTH, 2, 16, 64, 64), np.float32)
        maps.append(m)
    return maps


def filter_maps(nc, maps):
    drop = set(WNAMES) - set(nc.used_weight_inputs)
    return [{n: v for n, v in m.items() if n not in drop} for m in maps]


def kernel(**inputs):
    maps = make_in_maps(inputs)
    nc = build_program()
    maps = filter_maps(nc, maps)
    res = run_bass_kernel_spmd(nc, maps, core_ids=list(range(8)))
    B, S = inputs['x_prompt'].shape[0], inputs['x_prompt'].shape[1]
    y_prompt = np.zeros((B, S, D), np.float32)
    y_sample = np.zeros((4, 1024, D), np.float32)
    new_state = np.zeros((B, DEPTH, 2, 16, 64, 64), np.float32)
    for c in range(8):
        r = res.results[c]
        y = r['y']
        st = r['st']
        seqs = core_seqs(c)
        if c < 4:
            y_sample[c] = y[:1024]
            for j, s in enumerate(seqs):
                y_prompt[s] = y[1024 + j * 256:1024 + (j + 1) * 256]
                new_state[s] = st[:, 4 + j]
        else:
            for j, s in enumerate(seqs):
                y_prompt[s] = y[j * 256:(j + 1) * 256]
                new_state[s] = st[:, j]
    return (y_prompt, y_sample, new_state)
```

```python
import numpy as np
import concourse.bass as bass
import concourse.mybir as mybir
from concourse.bass_utils import run_bass_kernel_spmd

F32 = mybir.dt.float32
BF16 = mybir.dt.bfloat16
AF = mybir.ActivationFunctionType
OP = mybir.AluOpType
AX = mybir.AxisListType

D = 2048
KC = 16
NT = 1536
NTT = 12
DFF = 5632
NJ = 44
DEPTH = 2
CIN = 3488
INC = 5024
SB_BASE = 17408
SB_END = 229376


class K:
    def __init__(s, nc):
        s.nc = nc
        s.E = {'pe': nc.tensor, 'dve': nc.vector, 'act': nc.scalar, 'pool': nc.gpsimd, 'sp': nc.sync}
        s.csem = {e: nc.alloc_semaphore('c_' + e) for e in s.E}
        s.cnt = {e: 0 for e in s.E}
        s.seen = {}
        s.track = {}
        s.dpool = []
        s.dmap = {}
        s.dnext = 0
        s.sbuf_off = SB_BASE
        s.ninst = 0
        s.uid = 0

    def _wait(s, eng, tok):
        if tok is None:
            return
        sem, val, owner = tok
        if owner == eng and eng == 'pe':
            return
        kk = (eng, id(sem))
        if s.seen.get(kk, -1) >= val:
            return
        s.seen[kk] = val
        s.E[eng].wait_ge(sem, val)
        s.ninst += 1

    def _deps(s, eng, reads, writes):
        for key in reads:
            t = s.track.get(key)
            if t is not None:
                s._wait(eng, t['w'])
        for key in writes:
            t = s.track.get(key)
            if t is not None:
                s._wait(eng, t['w'])
                for r in t['r']:
                    s._wait(eng, r)

    def _commit(s, tok, reads, writes):
        for key in reads:
            t = s.track.setdefault(key, {'w': None, 'r': []})
            t['r'].append(tok)
            if len(t['r']) > 16:
                best = {}
                for r in t['r']:
                    q = id(r[0])
                    if q not in best or best[q][1] < r[1]:
                        best[q] = r
                t['r'] = list(best.values())
        for key in writes:
            s.track[key] = {'w': tok, 'r': []}

    def op(s, eng, fn, reads=(), writes=()):
        s._deps(eng, reads, writes)
        ins = fn()
        s.cnt[eng] += 1
        ins.then_inc(s.csem[eng], 1)
        s._commit((s.csem[eng], s.cnt[eng], eng), reads, writes)
        s.ninst += 1
        return ins

    def dma(s, q, out, in_, reads=(), writes=(), semkey=None):
        if q == 'sp' and 'DRam' in type(out.tensor).__name__ and 'DRam' not in type(in_.tensor).__name__:
            q = 'act'
        s._deps(q, reads, writes)
        if semkey is None:
            semkey = tuple(writes) if writes else tuple(reads)
        if semkey not in s.dmap:
            if s.dnext >= len(s.dpool):
                s.dpool.append([s.nc.alloc_semaphore('d%d' % len(s.dpool)), 0])
            s.dmap[semkey] = s.dnext
            s.dnext += 1
        ent = s.dpool[s.dmap[semkey]]
        ent[1] += 16
        ins = s.E[q].dma_start(out=out, in_=in_)
        ins.then_inc(ent[0], 16)
        s._commit((ent[0], ent[1], 'dma'), reads, writes)
        s.ninst += 1
        return ins

    def all_tokens(s):
        toks = [(s.csem[e], s.cnt[e], e) for e in s.E if s.cnt[e] > 0]
        toks += [(e[0], e[1], 'dma') for e in s.dpool if e[1] > 0]
        return toks

    def barrier(s):
        toks = s.all_tokens()
        for e in s.E:
            for t in toks:
                s._wait(e, t)
        s.track = {}
        s.dmap = {}
        s.dnext = 0

    def finish(s):
        for t in s.all_tokens():
            s._wait('sp', t)

    def sb(s, name, shape, dtype):
        nbytes = int(np.prod(shape[1:])) * (2 if dtype == BF16 else 4)
        off = (s.sbuf_off + 63) // 64 * 64
        s.sbuf_off = off + nbytes
        assert s.sbuf_off <= SB_END, (name, s.sbuf_off)
        s.uid += 1
        return s.nc.alloc_sbuf_tensor_at('%s_%d' % (name, s.uid), list(shape), dtype, offset=off)


class Ring:
    def __init__(s, k, name, shape, dtype, n):
        s.t = [k.sb('%s%d' % (name, i), shape, dtype) for i in range(n)]
        s.keys = ['%s_%d_%d' % (name, k.uid, i) for i in range(n)]
        s.i = 0
        s.n = n

    def next(s):
        j = s.i % s.n
        s.i += 1
        return s.t[j], s.keys[j]


def build_program(debug=(), stop_after=None, phases=None, mix_parts=None):
    nc = bass.Bass("TRN2", target_bir_lowering=False)
    V = nc.vector
    A = nc.scalar
    G = nc.gpsimd
    PE = nc.tensor

    def din(name, shape):
        return nc.dram_tensor(name, list(shape), F32, kind="ExternalInput").ap()

    def dout(name, shape):
        return nc.dram_tensor(name, list(shape), F32, kind="ExternalOutput").ap()

    def dscr(name, shape, dt=F32):
        kind = "ExternalOutput" if name in debug else "Internal"
        return nc.dram_tensor(name, list(shape), dt, kind=kind).ap()

    xin = din("xin", [NT, D])
    cond = din("cond", [2, D])
    s0_in = din("s0", [DEPTH, 2, 16, 64, 64])
    carry_in = din("carry", [128, 1])
    ind_in = din("ind", [2, 4, CIN])
    valid_in = din("valid", [NT, 4])
    CL_in = din("CL", [1024, 1024])
    nSL_in = din("nSL", [1024, 1024])
    CB_in = din("CB", [256, 256])
    nSB_in = din("nSB", [256, 256])
    CSd_in = din("CSd", [128, 256])
    tri_in = din("tri", [4, 128, 128])
    ident_in = din("ident", [128, 128])
    onesm_in = din("onesm", [128, 128])
    WSHAPES = dict([("norm_g", [DEPTH, 3, D]), ("w_mod", [DEPTH, D, 9 * D]), ("b_mod", [DEPTH, 9 * D]),
                        ("ffn_w_in", [DEPTH, 2, D, 2 * DFF]), ("ffn_w_out", [DEPTH, 2, DFF, D]),
                        ("w_in", [DEPTH, D, INC]), ("w_out", [DEPTH, D, D]),
                        ("sgu_ln_g", [DEPTH, 512]), ("sgu_ln_b", [DEPTH, 512]), ("sgu_w", [DEPTH, 4, 128, 128]),
                        ("sgu_b", [DEPTH, 4, 128]), ("shift_mu", [DEPTH, CIN]), ("decay_w0", [DEPTH, 2, 1024]),
                        ("decay_w2", [DEPTH, 2, 64, 1024]), ("iclr_a0", [DEPTH, 2, 1024]),
                        ("iclr_a2", [DEPTH, 2, 64, 1024]), ("k_k", [DEPTH, 1024]), ("k_a", [DEPTH, 1024]),
                        ("r_k", [DEPTH, 16, 64]), ("gate_w2", [DEPTH, 160, 1024]), ("lnx_g", [DEPTH, 1024]),
                        ("lnx_b", [DEPTH, 1024]), ("final_g", [D])])
    used_inputs = []

    class LazyW(dict):
        def __missing__(s, name):
            s[name] = din(name, WSHAPES[name])
            used_inputs.append(name)
            return s[name]
    W = LazyW()
    y_out = dout("y", [NT, D])
    st_out = dout("st", [DEPTH, 6, 2, 16, 64, 64])
    xs = dscr("xs", [KC, 128, NT])
    uT = dscr("uT", [4, 128, NT], BF16)
    vn = dscr("vn", [NT, 512], BF16)
    zbT = dscr("zbT", [4, 128, NT], BF16)
    zc = dscr("zc", [NT + 128, CIN])
    ztm = dscr("ztm", [NT, CIN])
    ymixT = dscr("ymixT", [KC, 128, NT], BF16)
    scP = dscr("scP", [NTT, 2, 64, 16, 64])
    scQ = dscr("scQ", [NTT, 2, 64, 16, 64])
    scG = dscr("scG", [NTT, 2, 64, 16, 128])
    scY = dscr("scY", [NTT, 2, 128, 1024])
    scD = dscr("scD", [NTT, 2, 64, 16])
    scV = dscr("scV", [NT, 1024])
    scGt = dscr("scGt", [NT, 1024])
    scB = dscr("scB", [NT, 16])
    hbT = dscr("hbT", [NJ, 128, NT], BF16)

    k = K(nc)
    ps = [nc.alloc_psum_tensor("ps%d" % i, [128, 512], F32) for i in range(8)]
    pk = ['ps%d' % i for i in range(8)]

    ident = k.sb("ident", [128, 128], F32)
    onesm = k.sb("onesm", [128, 128], F32)
    tri = k.sb("tri", [128, 4, 128], F32)
    carry = k.sb("carry", [128, 1], F32)
    valid = k.sb("valid", [128, NTT, 4], F32)
    epsr = k.sb("epsr", [128, 1], F32)
    epsx = k.sb("epsx", [128, 1], F32)
    epsl = k.sb("epsl", [128, 1], F32)
    modv = k.sb("modv", [128, DEPTH, 144, 2], F32)
    gsv = k.sb("gsv", [128, DEPTH, 3, KC, 2], F32)
    gtv = k.sb("gtv", [128, DEPTH, 3, KC, 2], F32)
    ngT = k.sb("ngT", [128, DEPTH, KC, 3], F32)
    fgT = k.sb("fgT", [128, KC, 1], F32)
    PERSIST_END = k.sbuf_off

    def phase_reset():
        k.barrier()
        k.sbuf_off = PERSIST_END

    k.dma('sp', ident[:], ident_in, writes=['ident'])
    k.dma('sp', onesm[:], onesm_in, writes=['onesm'])
    k.dma('sp', tri[:], tri_in.rearrange("f s t -> s f t"), writes=['tri'])
    k.dma('sp', carry[:], carry_in, writes=['carry'])
    k.dma('sp', valid[:], valid_in.rearrange("(tt p) o -> p tt o", p=128), writes=['valid'])
    k.op('dve', lambda: V.memset(epsr[:], 1e-6), writes=['epsr'])
    k.op('dve', lambda: V.memset(epsx[:], 64e-5), writes=['epsx'])
    k.op('dve', lambda: V.memset(epsl[:], 1e-5), writes=['epsl'])

    def rows_to_fm(src, R, C, dst_fn, tmp, tmpkey, pbank):
        k.dma('sp', tmp[0:R, 0:C], src, writes=[tmpkey])
        nj = C // 128
        for j in range(nj):
            k.op('pe', lambda: PE.transpose(out=ps[pbank][:, j * R:(j + 1) * R], in_=tmp[0:R, j * 128:(j + 1) * 128],
                                            identity=ident[0:R, 0:R]), reads=[tmpkey, 'ident'], writes=[pk[pbank]])
        for j in range(nj):
            k.op('dve', lambda: V.tensor_copy(out=dst_fn(j), in_=ps[pbank][:, j * R:(j + 1) * R]),
                 reads=[pk[pbank]], writes=['fm_dst'])

    def phase_mod():
        tmp = k.sb("rtmp", [128, D], F32)
        scT = k.sb("scT", [128, KC, 2], BF16)
        scf = k.sb("scf", [128, KC, 2], F32)
        bmT = k.sb("bmT", [128, DEPTH, 144], F32)
        c2 = k.sb("c2", [2, D], F32)
        k.dma('sp', c2[:], cond, writes=['c2'])
        k.op('act', lambda: A.activation(out=tmp[0:2, :], in_=c2[0:2, :], func=AF.Silu), reads=['c2'], writes=['rtmp'])
        for j in range(KC):
            k.op('pe', lambda: PE.transpose(out=ps[0][:, j * 2:(j + 1) * 2], in_=tmp[0:2, j * 128:(j + 1) * 128],
                                            identity=ident[0:2, 0:2]), reads=['rtmp', 'ident'], writes=['ps0'])
        k.op('dve', lambda: V.tensor_copy(out=scT[:].rearrange("p a b -> p (a b)"), in_=ps[0][:, 0:32]), reads=['ps0'], writes=['scT'])
        rows_to_fm(W["final_g"].rearrange("(o n) -> o n", o=1), 1, D, lambda j: fgT[:, j, :], tmp, 'rtmp', 1)
        for l in range(DEPTH):
            rows_to_fm(W["norm_g"][l], 3, D, lambda j: ngT[:, l, j, :], tmp, 'rtmp', 2 + l)
            bm = W["b_mod"][l].rearrange("(c p) -> c p", p=128)
            k.dma('sp', tmp[0:128, 0:128], bm[0:128, :], writes=['rtmp'])
            k.dma('sp', tmp[0:16, 128:256], bm[128:144, :], writes=['rtmp'])
            k.op('pe', lambda: PE.transpose(out=ps[4][:, 0:128], in_=tmp[0:128, 0:128], identity=ident[:]), reads=['rtmp', 'ident'], writes=['ps4'])
            k.op('pe', lambda: PE.transpose(out=ps[4][:, 128:144], in_=tmp[0:16, 128:256], identity=ident[0:16, 0:16]), reads=['rtmp', 'ident'], writes=['ps4'])
            k.op('dve', lambda: V.tensor_copy(out=bmT[:, l, :], in_=ps[4][:, 0:144]), reads=['ps4'], writes=['bmT'])
        stg = Ring(k, "wmst", [128, KC, 384], F32, 2)
        wbf = Ring(k, "wmbf", [128, KC, 384], BF16, 2)
        nb = 0
        for l in range(DEPTH):
            wm = W["w_mod"][l].rearrange("(kc p) n -> p kc n", p=128)
            for blk in range(48):
                st, sk = stg.next()
                wb, wk = wbf.next()
                k.dma('sp', st[:], wm[:, :, blk * 384:(blk + 1) * 384], writes=[sk])
                k.op('pool', lambda: G.tensor_copy(out=wb[:], in_=st[:]), reads=[sk], writes=[wk])
                pb = 5 + (nb % 2)
                nb += 1
                for q in range(3):
                    for kc in range(KC):
                        k.op('pe', lambda: PE.matmul(ps[pb][:, q * 2:(q + 1) * 2], wb[:, kc, q * 128:(q + 1) * 128], scT[:, kc, :],
                                                     start=(kc == 0), stop=(kc == KC - 1)), reads=[wk, 'scT'], writes=[pk[pb]])
                k.op('dve', lambda: V.tensor_tensor(out=modv[:, l, blk * 3:(blk + 1) * 3, :],
                                                    in0=ps[pb][:, 0:6].rearrange("p (a b) -> p a b", b=2),
                                                    in1=bmT[:, l, blk * 3:(blk + 1) * 3].unsqueeze(2).to_broadcast([128, 3, 2]), op=OP.add),
                     reads=[pk[pb], 'bmT'], writes=['modv'])
        for l in range(DEPTH):
            for i in range(3):
                sc = modv[:, l, (3 * i + 1) * 16:(3 * i + 2) * 16, :]
                gt = modv[:, l, (3 * i + 2) * 16:(3 * i + 3) * 16, :]
                k.op('dve', lambda: V.tensor_scalar(out=scf[:], in0=sc, scalar1=1.0, scalar2=None, op0=OP.add), reads=['modv'], writes=['scf'])
                k.op('dve', lambda: V.tensor_tensor(out=gsv[:, l, i], in0=scf[:], in1=ngT[:, l, :, i:i + 1].to_broadcast([128, KC, 2]), op=OP.mult),
                     reads=['scf', 'fm_dst'], writes=['gsv'])
                k.op('dve', lambda: V.tensor_scalar(out=gtv[:, l, i], in0=gt, scalar1=(1.0 if i == 1 else 0.5), scalar2=None, op0=OP.mult),
                     reads=['modv'], writes=['gtv'])

    def shiftv(l, i):
        return modv[:, l, (3 * i) * 16:(3 * i + 1) * 16, :]

    def phase_in():
        xt = Ring(k, "xt", [128, D], F32, 2)
        xo = Ring(k, "xo", [128, KC, 128], F32, 2)
        for tt in range(NTT):
            t_, tk = xt.next()
            o_, ok = xo.next()
            k.dma('sp', t_[:], xin[tt * 128:(tt + 1) * 128, :], writes=[tk])
            for g4 in range(4):
                pb = g4 % 4
                for q in range(4):
                    kc = g4 * 4 + q
                    k.op('pe', lambda: PE.transpose(out=ps[pb][:, q * 128:(q + 1) * 128], in_=t_[:, kc * 128:(kc + 1) * 128], identity=ident[:]),
                         reads=[tk, 'ident'], writes=[pk[pb]])
                eng = 'dve' if g4 % 2 == 0 else 'act'
                if eng == 'dve':
                    k.op('dve', lambda: V.tensor_copy(out=o_[:, g4 * 4:(g4 + 1) * 4, :].rearrange("p a b -> p (a b)"), in_=ps[pb][:]), reads=[pk[pb]], writes=[ok])
                else:
                    k.op('act', lambda: A.copy(out=o_[:, g4 * 4:(g4 + 1) * 4, :].rearrange("p a b -> p (a b)"), in_=ps[pb][:]), reads=[pk[pb]], writes=[ok])
            k.dma('sp', xs[:, :, tt * 128:(tt + 1) * 128].rearrange("kc p t -> p kc t"), o_[:], reads=[ok], writes=['xs'])

    def ada_norm(x, xkey, h, hkey, n, l, i, ci, sq, rs, pbank):
        for kc in range(KC):
            s_, sk = sq.next()
            k.op('act', lambda: A.activation(out=s_[:, 0:n], in_=x[:, kc, 0:n], func=AF.Square), reads=[xkey], writes=[sk])
            k.op('pe', lambda: PE.matmul(ps[pbank][:, 0:n], onesm[:], s_[:, 0:n], start=(kc == 0), stop=(kc == KC - 1)),
                 reads=[sk, 'onesm'], writes=[pk[pbank]])
        k.op('act', lambda: A.activation(out=rs[:, 0:n], in_=ps[pbank][:, 0:n], func=AF.Sqrt, bias=epsr[:, 0:1], scale=1.0), reads=[pk[pbank], 'epsr'], writes=['rs'])
        k.op('dve', lambda: V.reciprocal(out=rs[:, 0:n], in_=rs[:, 0:n]), reads=['rs'], writes=['rs'])
        for kc in range(KC):
            s_, sk = sq.next()
            k.op('dve', lambda: V.tensor_tensor(out=s_[:, 0:n], in0=x[:, kc, 0:n], in1=rs[:, 0:n], op=OP.mult), reads=[xkey, 'rs'], writes=[sk])
            if l is None:
                k.op('act', lambda: A.activation(out=h[:, kc, 0:n], in_=s_[:, 0:n], func=AF.Identity, scale=fgT[:, kc, 0:1], bias=0.0),
                     reads=[sk, 'fm_dst'], writes=[hkey])
            else:
                k.op('act', lambda: A.activation(out=h[:, kc, 0:n], in_=s_[:, 0:n], func=AF.Identity, scale=gsv[:, l, i, kc, ci:ci + 1],
                                                 bias=shiftv(l, i)[:, kc, ci:ci + 1]), reads=[sk, 'gsv', 'modv'], writes=[hkey])

    def phase_ffn(l, i):
        fi = 0 if i == 0 else 1
        wi = W["ffn_w_in"][l, fi].rearrange("(kc p) n -> p kc n", p=128)
        wo = W["ffn_w_out"][l, fi].rearrange("(j p) n -> p j n", p=128)
        x = k.sb("x", [128, KC, 512], F32)
        h = k.sb("h", [128, KC, NT], BF16)
        sq = Ring(k, "sq", [128, 512], F32, 2)
        rs = k.sb("rs", [128, 512], F32)
        sg = Ring(k, "sg", [128, 512], F32, 2)
        wst = Ring(k, "wst", [128, KC, 256], F32, 2)
        wbf = Ring(k, "wbf", [128, KC, 256], BF16, 2)
        hbo = Ring(k, "hbo", [128, 512], BF16, 3)
        for tg in range(3):
            ci = 0 if tg < 2 else 1
            ts = slice(tg * 512, (tg + 1) * 512)
            k.dma('sp', x[:], xs[:, :, ts].rearrange("kc p t -> p kc t"), reads=['xs'], writes=['x'])
            ada_norm(x, 'x', h[:, :, ts], 'h%d' % tg, 512, l, i, ci, sq, rs, 0)
        n = 0
        for j in range(NJ):
            st, sk = wst.next()
            wb, wk = wbf.next()
            k.dma('sp', st[:, :, 0:128], wi[:, :, j * 128:(j + 1) * 128], writes=[sk + 'a'])
            k.dma('sp', st[:, :, 128:256], wi[:, :, DFF + j * 128:DFF + (j + 1) * 128], writes=[sk + 'b'])
            k.op('pool', lambda: G.tensor_copy(out=wb[:], in_=st[:]), reads=[sk + 'a', sk + 'b'], writes=[wk])
            for tg in range(3):
                ts = slice(tg * 512, (tg + 1) * 512)
                pg = 1 + 2 * (n % 3)
                pu = pg + 1
                n += 1
                for kc in range(KC):
                    k.op('pe', lambda: PE.matmul(ps[pg][:], wb[:, kc, 0:128], h[:, kc, ts], start=(kc == 0), stop=(kc == KC - 1)),
                         reads=[wk, 'h%d' % tg], writes=[pk[pg]])
                for kc in range(KC):
                    k.op('pe', lambda: PE.matmul(ps[pu][:], wb[:, kc, 128:256], h[:, kc, ts], start=(kc == 0), stop=(kc == KC - 1)),
                         reads=[wk, 'h%d' % tg], writes=[pk[pu]])
                s_, sgk = sg.next()
                o_, okey = hbo.next()
                k.op('act', lambda: A.activation(out=s_[:], in_=ps[pg][:], func=AF.Silu), reads=[pk[pg]], writes=[sgk])
                k.op('dve', lambda: V.tensor_tensor(out=o_[:], in0=s_[:], in1=ps[pu][:], op=OP.mult), reads=[sgk, pk[pu]], writes=[okey])
                k.dma('sp', hbT[j, :, ts], o_[:], reads=[okey], writes=['hbT'], semkey=(okey, 'st'))
        phase_reset()
        x = k.sb("x", [128, KC, 512], F32)
        hb = k.sb("hb", [128, NJ, 512], BF16)
        ost = Ring(k, "ost", [128, 22, 128], F32, 2)
        obf = Ring(k, "obf", [128, NJ, 128], BF16, 2)
        for tg in range(3):
            ci = 0 if tg < 2 else 1
            ts = slice(tg * 512, (tg + 1) * 512)
            k.dma('sp', x[:], xs[:, :, ts].rearrange("kc p t -> p kc t"), reads=['xs'], writes=['x'])
            k.dma('sp', hb[:], hbT[:, :, ts].rearrange("j p t -> p j t"), reads=['hbT'], writes=['hb'])
            for oc in range(KC):
                ob, obk = obf.next()
                for half in range(2):
                    st, sk = ost.next()
                    k.dma('sp', st[:], wo[:, half * 22:(half + 1) * 22, oc * 128:(oc + 1) * 128], writes=[sk])
                    k.op('pool', lambda: G.tensor_copy(out=ob[:, half * 22:(half + 1) * 22, :], in_=st[:]), reads=[sk], writes=[obk + '_%d' % half])
                po = 5 + (oc % 2)
                for j in range(NJ):
                    k.op('pe', lambda: PE.matmul(ps[po][:], ob[:, j, :], hb[:, j, :], start=(j == 0), stop=(j == NJ - 1)),
                         reads=[obk + '_%d' % (j // 22), 'hb'], writes=[pk[po]])
                k.op('dve', lambda: V.scalar_tensor_tensor(out=x[:, oc, :], in0=ps[po][:], scalar=gtv[:, l, i, oc, ci:ci + 1], in1=x[:, oc, :],
                                                           op0=OP.mult, op1=OP.add), reads=[pk[po], 'gtv', 'x'], writes=['x'])
            k.dma('sp', xs[:, :, ts].rearrange("kc p t -> p kc t"), x[:], reads=['x'], writes=['xs'])

    def phase_out():
        x = k.sb("x", [128, KC, 128], F32)
        h = k.sb("hf", [128, KC, 128], F32)
        sq = Ring(k, "sq", [128, 128], F32, 2)
        rs = k.sb("rs", [128, 128], F32)
        yo = Ring(k, "yo", [128, D], F32, 2)
        for tt in range(NTT):
            ts = slice(tt * 128, (tt + 1) * 128)
            k.dma('sp', x[:], xs[:, :, ts].rearrange("kc p t -> p kc t"), reads=['xs'], writes=['x'])
            ada_norm(x, 'x', h, 'hf', 128, None, None, None, sq, rs, 0)
            y_, yk = yo.next()
            for g4 in range(4):
                pb = 1 + g4
                for q in range(4):
                    kc = g4 * 4 + q
                    k.op('pe', lambda: PE.transpose(out=ps[pb][:, q * 128:(q + 1) * 128], in_=h[:, kc, :], identity=ident[:]),
                         reads=['hf', 'ident'], writes=[pk[pb]])
                if g4 % 2 == 0:
                    k.op('dve', lambda: V.tensor_copy(out=y_[:, g4 * 512:(g4 + 1) * 512], in_=ps[pb][:]), reads=[pk[pb]], writes=[yk])
                else:
                    k.op('act', lambda: A.copy(out=y_[:, g4 * 512:(g4 + 1) * 512], in_=ps[pb][:]), reads=[pk[pb]], writes=[yk])
            k.dma('sp', y_out[ts, :], y_[:], reads=[yk], semkey=('yout', tt % 2))


    def bcast_load(dst, src_row, key):
        k.dma('sp', dst, src_row.partition_broadcast(128), writes=[key])

    def evac(i, out, in_, reads, writes):
        if i % 2 == 0:
            k.op('dve', lambda: V.tensor_copy(out=out, in_=in_), reads=reads, writes=writes)
        else:
            k.op('act', lambda: A.copy(out=out, in_=in_), reads=reads, writes=writes)

    def phase_win(l):
        x = k.sb("x", [128, KC, 512], F32)
        h = k.sb("h", [128, KC, 512], BF16)
        sq = Ring(k, "sq", [128, 512], F32, 2)
        rs = k.sb("rs", [128, 512], F32)
        wst = Ring(k, "wst", [128, KC, 512], F32, 2)
        wbf = Ring(k, "wbf", [128, KC, 512], BF16, 2)
        ob = Ring(k, "ob", [128, 512], BF16, 3)
        of = Ring(k, "of", [128, 512], F32, 3)
        lng = k.sb("lng", [128, 512], F32)
        lnb = k.sb("lnb", [128, 512], F32)
        st1 = k.sb("st1", [128, 4], F32)
        zt = k.sb("zt", [64, CIN], F32)
        bcast_load(lng[:], W["sgu_ln_g"][l], 'lng')
        bcast_load(lnb[:], W["sgu_ln_b"][l], 'lnb')
        k.op('pool', lambda: G.memset(zt[:], 0.0), writes=['zt'])
        k.dma('sp', zc[0:64, :], zt[:], reads=['zt'], writes=['zc'])
        k.dma('sp', zc[64 + NT:128 + NT, :], zt[:], reads=['zt'], writes=['zc'])
        wi = W["w_in"][l].rearrange("(kc p) n -> p kc n", p=128)
        blocks = [(0, 512), (512, 512), (1024, 512)] + [(1536 + 512 * b, 512 if b < 6 else 416) for b in range(7)]
        nev = 0
        for tg in range(3):
            ci = 0 if tg < 2 else 1
            ts = slice(tg * 512, (tg + 1) * 512)
            k.dma('sp', x[:], xs[:, :, ts].rearrange("kc p t -> p kc t"), reads=['xs'], writes=['x'])
            ada_norm(x, 'x', h, 'h', 512, l, 1, ci, sq, rs, 0)
            for bi, (c0, ncol) in enumerate(blocks):
                st, sk = wst.next()
                wb, wk = wbf.next()
                k.dma('sp', st[:, :, 0:ncol], wi[:, :, c0:c0 + ncol], writes=[sk])
                k.op('pool', lambda: G.tensor_copy(out=wb[:, :, 0:ncol], in_=st[:, :, 0:ncol]), reads=[sk], writes=[wk])
                for q in range(4):
                    pb = 1 + (nev % 6)
                    nev += 1
                    if bi in (0, 2):
                        for kc in range(KC):
                            k.op('pe', lambda: PE.matmul(ps[pb][:], wb[:, kc, q * 128:(q + 1) * 128], h[:, kc, :], start=(kc == 0), stop=(kc == KC - 1)),
                                 reads=[wk, 'h'], writes=[pk[pb]])
                        o_, okey = ob.next()
                        if bi == 0:
                            k.op('act', lambda: A.activation(out=o_[:], in_=ps[pb][:], func=AF.Gelu_apprx_tanh), reads=[pk[pb]], writes=[okey])
                            k.dma('sp', uT[q, :, ts], o_[:], reads=[okey], writes=['uT'], semkey=(okey, 'st'))
                        else:
                            evac(nev, o_[:], ps[pb][:], [pk[pb]], [okey])
                            k.dma('sp', zbT[q, :, ts], o_[:], reads=[okey], writes=['zbT'], semkey=(okey, 'st'))
                    else:
                        trow = tg * 512 + q * 128
                        for kc in range(KC):
                            k.op('pe', lambda: PE.matmul(ps[pb][:, 0:ncol], h[:, kc, q * 128:(q + 1) * 128], wb[:, kc, 0:ncol], start=(kc == 0), stop=(kc == KC - 1)),
                                 reads=[wk, 'h'], writes=[pk[pb]])
                        f_, fkey = of.next()
                        if bi == 1:
                            k.op('act', lambda: A.activation(out=f_[:], in_=ps[pb][:], func=AF.Gelu_apprx_tanh), reads=[pk[pb]], writes=[fkey])
                            k.op('dve', lambda: V.tensor_reduce(out=st1[:, 0:1], in_=f_[:], axis=AX.X, op=OP.add), reads=[fkey], writes=['st1'])
                            k.op('dve', lambda: V.tensor_scalar(out=st1[:, 1:2], in0=st1[:, 0:1], scalar1=1.0 / 512, scalar2=None, op0=OP.mult), reads=['st1'], writes=['st1'])
                            k.op('dve', lambda: V.tensor_scalar(out=f_[:], in0=f_[:], scalar1=st1[:, 1:2], scalar2=None, op0=OP.subtract), reads=[fkey, 'st1'], writes=[fkey])
                            s_, sk2 = sq.next()
                            k.op('dve', lambda: V.tensor_tensor(out=s_[:], in0=f_[:], in1=f_[:], op=OP.mult), reads=[fkey], writes=[sk2])
                            k.op('dve', lambda: V.tensor_reduce(out=st1[:, 2:3], in_=s_[:], axis=AX.X, op=OP.add), reads=[sk2], writes=['st1'])
                            k.op('act', lambda: A.activation(out=st1[:, 3:4], in_=st1[:, 2:3], func=AF.Sqrt, bias=epsl[:, 0:1], scale=1.0 / 512), reads=['st1', 'epsl'], writes=['st1'])
                            k.op('dve', lambda: V.reciprocal(out=st1[:, 3:4], in_=st1[:, 3:4]), reads=['st1'], writes=['st1'])
                            k.op('dve', lambda: V.scalar_tensor_tensor(out=f_[:], in0=f_[:], scalar=st1[:, 3:4], in1=lng[:], op0=OP.mult, op1=OP.mult),
                                 reads=[fkey, 'st1', 'lng'], writes=[fkey])
                            o_, okey = ob.next()
                            k.op('dve', lambda: V.tensor_tensor(out=o_[:], in0=f_[:], in1=lnb[:], op=OP.add), reads=[fkey, 'lnb'], writes=[okey])
                            k.dma('sp', vn[trow:trow + 128, :], o_[:], reads=[okey], writes=['vn'], semkey=(okey, 'st'))
                        else:
                            evac(nev, f_[:, 0:ncol], ps[pb][:, 0:ncol], [pk[pb]], [fkey])
                            cz = c0 - 1536
                            k.dma('sp', zc[64 + trow:64 + trow + 128, cz:cz + ncol], f_[:, 0:ncol], reads=[fkey], writes=['zc'], semkey=(fkey, 'st'))

    def phase_gmlp(l):
        wsf = k.sb("wsf", [128, 4, 128], F32)
        wsT = k.sb("wsT", [128, 4, 128], BF16)
        bsb = k.sb("bsb", [128, 512], F32)
        vt = Ring(k, "vt", [128, 512], BF16, 2)
        ut = Ring(k, "ut", [128, 4, 128], BF16, 2)
        tm = Ring(k, "tm", [128, 512], F32, 2)
        yo = Ring(k, "yo", [128, 4, 128], BF16, 2)
        k.dma('sp', wsf[:], W["sgu_w"][l].rearrange("h t s -> t h s"), writes=['wsf'])
        bcast_load(bsb[:], W["sgu_b"][l].rearrange("h t -> (h t)"), 'bsb')
        for hh in range(4):
            k.op('pe', lambda: PE.transpose(out=ps[0][:, hh * 128:(hh + 1) * 128], in_=wsf[:, hh, :], identity=ident[:]), reads=['wsf', 'ident'], writes=['ps0'])
        k.op('dve', lambda: V.tensor_copy(out=wsT[:].rearrange("p a b -> p (a b)"), in_=ps[0][:]), reads=['ps0'], writes=['wsT'])
        for tt in range(NTT):
            ts = slice(tt * 128, (tt + 1) * 128)
            v_, vk = vt.next()
            u_, uk = ut.next()
            k.dma('sp', v_[:], vn[ts, :], reads=['vn'], writes=[vk])
            k.dma('sp', u_[:], uT[:, :, ts].rearrange("q p t -> p q t"), reads=['uT'], writes=[uk])
            pb = 1 + tt % 2
            for hh in range(4):
                k.op('pe', lambda: PE.matmul(ps[pb][:, hh * 128:(hh + 1) * 128], v_[:, hh * 128:(hh + 1) * 128], wsT[:, hh, :], start=True, stop=True),
                     reads=[vk, 'wsT'], writes=[pk[pb]])
            t_, tk = tm.next()
            k.op('dve', lambda: V.tensor_tensor(out=t_[:], in0=ps[pb][:], in1=bsb[:], op=OP.add), reads=[pk[pb], 'bsb'], writes=[tk])
            y_, yk = yo.next()
            k.op('dve', lambda: V.tensor_tensor(out=y_[:].rearrange("p a b -> p (a b)"), in0=t_[:], in1=u_[:].rearrange("p a b -> p (a b)"), op=OP.mult),
                 reads=[tk, uk], writes=[yk])
            k.dma('sp', ymixT[0:4, :, ts].rearrange("q p t -> p q t"), y_[:], reads=[yk], writes=['ymixT'], semkey=(yk, 'st'))

    def phase_fnet(l):
        stg = k.sb("stg", [128, 8, 1024], F32)
        CLb = k.sb("CLb", [128, 8, 1024], BF16)
        SLb = k.sb("SLb", [128, 8, 1024], BF16)
        CBb = k.sb("CBb", [128, 2, 256], BF16)
        SBb = k.sb("SBb", [128, 2, 256], BF16)
        CSb = k.sb("CSb", [128, 256], BF16)
        Atm = k.sb("Atm", [128, NTT, 4, 256], BF16)
        zb = Ring(k, "zb", [128, 4, 128], BF16, 2)
        yo = Ring(k, "yo", [128, 512], BF16, 2)
        for src, dst, key, shp in [(CL_in, CLb, 'CLb', (8, 1024)), (nSL_in, SLb, 'SLb', (8, 1024)), (CB_in, CBb, 'CBb', (2, 256)), (nSB_in, SBb, 'SBb', (2, 256))]:
            a, b = shp
            k.dma('sp', stg[:, 0:a, 0:b], src.rearrange("(sc p) t -> p sc t", p=128), writes=['stg'])
            k.op('pool', lambda: G.tensor_copy(out=dst[:], in_=stg[:, 0:a, 0:b]), reads=['stg'], writes=[key])
        k.dma('sp', stg[:, 0, 0:256], CSd_in, writes=['stg'])
        k.op('pool', lambda: G.tensor_copy(out=CSb[:], in_=stg[:, 0, 0:256]), reads=['stg'], writes=['CSb'])
        for tt in range(NTT):
            ts = slice(tt * 128, (tt + 1) * 128)
            z_, zk = zb.next()
            k.dma('sp', z_[:], zbT[:, :, ts].rearrange("q p t -> p q t"), reads=['zbT'], writes=[zk])
            for half in range(2):
                pb = 1 + (2 * tt + half) % 4
                for gg in range(2):
                    g_ = half * 2 + gg
                    k.op('pe', lambda: PE.matmul(ps[pb][:, gg * 256:(gg + 1) * 256], z_[:, g_, :], CSb[:], start=True, stop=True), reads=[zk, 'CSb'], writes=[pk[pb]])
                evac(tt + half, Atm[:, tt, half * 2:half * 2 + 2, :].rearrange("p a b -> p (a b)"), ps[pb][:], [pk[pb]], ['Atm%d' % tt])
        n = 0
        for g_ in range(4):
            for half in range(2):
                pb = 5 + n % 2
                n += 1
                cs_ = slice(half * 512, (half + 1) * 512)
                for sc in range(8):
                    k.op('pe', lambda: PE.matmul(ps[pb][:], Atm[:, sc, g_, 0:128], CLb[:, sc, cs_], start=(sc == 0), stop=False), reads=['Atm%d' % sc, 'CLb'], writes=[pk[pb]])
                    k.op('pe', lambda: PE.matmul(ps[pb][:], Atm[:, sc, g_, 128:256], SLb[:, sc, cs_], start=False, stop=(sc == 7)), reads=['Atm%d' % sc, 'SLb'], writes=[pk[pb]])
                y_, yk = yo.next()
                evac(n, y_[:], ps[pb][:], [pk[pb]], [yk])
                k.dma('sp', ymixT[4 + g_, :, cs_], y_[:], reads=[yk], writes=['ymixT'], semkey=(yk, 'st'))
            for seg in range(2):
                pb = 5 + n % 2
                n += 1
                for sc in range(2):
                    tt = 8 + seg * 2 + sc
                    k.op('pe', lambda: PE.matmul(ps[pb][:, 0:256], Atm[:, tt, g_, 0:128], CBb[:, sc, :], start=(sc == 0), stop=False), reads=['Atm%d' % tt, 'CBb'], writes=[pk[pb]])
                    k.op('pe', lambda: PE.matmul(ps[pb][:, 0:256], Atm[:, tt, g_, 128:256], SBb[:, sc, :], start=False, stop=(sc == 1)), reads=['Atm%d' % tt, 'SBb'], writes=[pk[pb]])
                y_, yk = yo.next()
                evac(n, y_[:, 0:256], ps[pb][:, 0:256], [pk[pb]], [yk])
                k.dma('sp', ymixT[4 + g_, :, 1024 + seg * 256:1280 + seg * 256], y_[:, 0:256], reads=[yk], writes=['ymixT'], semkey=(yk, 'st'))

    def phase_shift(l):
        HC = CIN // 2
        mub = k.sb("mub", [128, HC], F32)
        omm = k.sb("omm", [128, HC], F32)
        cm = k.sb("cm", [128, 2, 4, HC], F32)
        zl = [Ring(k, "zl%d" % o, [128, HC], F32, 2) for o in range(5)]
        acc = Ring(k, "acc", [128, HC], F32, 2)
        tmp = Ring(k, "tmp", [128, HC], F32, 2)
        offs = [-1, 1, -64, 64]
        for hc in range(2):
            cs_ = slice(hc * HC, (hc + 1) * HC)
            bcast_load(mub[:], W["shift_mu"][l][cs_], 'mub')
            for ty in range(2):
                for o in range(4):
                    bcast_load(cm[:, ty, o, :], ind_in[ty, o, cs_], 'cm')
            k.op('dve', lambda: V.tensor_scalar(out=omm[:], in0=mub[:], scalar1=-1.0, scalar2=1.0, op0=OP.mult, op1=OP.add), reads=['mub'], writes=['omm'])
            k.op('dve', lambda: V.tensor_tensor(out=cm[:].rearrange("p a b c -> p (a b) c"), in0=cm[:].rearrange("p a b c -> p (a b) c"),
                                                in1=mub[:].unsqueeze(1).to_broadcast([128, 8, HC]), op=OP.mult), reads=['cm', 'mub'], writes=['cm'])
            for tt in range(NTT):
                ty = 0 if tt < 8 else 1
                r0 = 64 + tt * 128
                a_, ak = acc.next()
                z0, z0k = zl[4].next()
                k.dma('sp', z0[:], zc[r0:r0 + 128, cs_], reads=['zc'], writes=[z0k])
                k.op('dve', lambda: V.tensor_tensor(out=a_[:], in0=z0[:], in1=omm[:], op=OP.mult), reads=[z0k, 'omm'], writes=[ak])
                for o in range(4 if ty == 0 else 2):
                    zo, zok = zl[o].next()
                    k.dma('sp', zo[:], zc[r0 + offs[o]:r0 + offs[o] + 128, cs_], reads=['zc'], writes=[zok])
                    t_, tk = tmp.next()
                    k.op('pool', lambda: G.tensor_tensor(out=t_[:], in0=zo[:], in1=cm[:, ty, o, :], op=OP.mult), reads=[zok, 'cm'], writes=[tk])
                    k.op('dve', lambda: V.scalar_tensor_tensor(out=a_[:], in0=t_[:], scalar=valid[:, tt, o:o + 1], in1=a_[:], op0=OP.mult, op1=OP.add),
                         reads=[ak, tk, 'valid'], writes=[ak])
                k.dma('sp', ztm[tt * 128:(tt + 1) * 128, cs_], a_[:], reads=[ak], writes=['ztm'], semkey=(ak, 'st'))


    CDEC = 0.6065306597126334

    def phase_scan(l):
        import os
        SS = int(os.environ.get('SCAN_STOP', '99'))
        NTS = int(os.environ.get('SCAN_TILES', str(NTT)))
        bc = {}
        for nm, src in [('w0f', W["decay_w0"][l, 0]), ('w0b', W["decay_w0"][l, 1]), ('a0f', W["iclr_a0"][l, 0]), ('a0b', W["iclr_a0"][l, 1]),
                        ('kk', W["k_k"][l]), ('ka', W["k_a"][l]), ('rk', W["r_k"][l].rearrange("h n -> (h n)"))]:
            bc[nm] = k.sb("bc_" + nm, [128, 1024], F32)
            bcast_load(bc[nm][:], src, 'bc_' + nm)
        f1 = k.sb("f1", [128, 1024], F32)
        f2 = k.sb("f2", [128, 1024], F32)
        stg = f1
        w2b = k.sb("w2b", [128, 1024], BF16)
        a2b = k.sb("a2b", [128, 1024], BF16)
        g2b = k.sb("g2b", [128, 2, 1024], BF16)
        idb = k.sb("idb", [128, 128], BF16)
        onec = k.sb("onec", [128, 1], F32)
        k.op('dve', lambda: V.memset(onec[:], 1.0), writes=['onec'])
        k.op('dve', lambda: V.tensor_copy(out=idb[:], in_=ident[:]), reads=['ident'], writes=['idb'])
        for src, dst, key in [(W["decay_w2"][l].rearrange("d r c -> (d r) c"), w2b[:], 'w2b'), (W["iclr_a2"][l].rearrange("d r c -> (d r) c"), a2b[:], 'a2b'),
                              (W["gate_w2"][l][0:128, :], g2b[:, 0, :], 'g2b')]:
            k.dma('sp', stg[:], src, writes=['f1'])
            k.op('pool', lambda: G.tensor_copy(out=dst, in_=stg[:]), reads=['f1'], writes=[key])
        k.dma('sp', stg[0:32, :], W["gate_w2"][l][128:160, :], writes=['f1'])
        k.op('pool', lambda: G.tensor_copy(out=g2b[0:32, 1, :], in_=stg[0:32, :]), reads=['f1'], writes=['g2b'])
        z = k.sb("z", [128, CIN], F32)
        t128 = k.sb("t128", [128, 416], F32)
        trT = k.sb("trT", [128, 4, 128], BF16)
        sig = k.sb("sig", [128, 2, 1024], F32)
        av = k.sb("av", [128, 2, 1024], F32)
        kkv = k.sb("kkv", [128, 1024], F32)
        kmv = k.sb("kmv", [128, 2, 1024], F32)
        bv = k.sb("bv", [128, 2, 1024], F32)
        sm = k.sb("sm", [128, 64], F32)
        Vb = k.sb("Vb", [128, 1024], BF16)
        At = k.sb("At", [128, 1024], F32)
        Bt = k.sb("Bt", [128, 1024], F32)
        Kt = k.sb("Kt", [128, 1024], F32)
        Rt = k.sb("Rt", [128, 1024], F32)
        Btb = k.sb("Btb", [128, 1024], BF16)
        Ktb = k.sb("Ktb", [128, 1024], BF16)
        Rtb = k.sb("Rtb", [128, 1024], BF16)
        XT4 = k.sb("XT4", [128, 4, 8, 128], BF16)
        M5 = k.sb("M5", [128, 5, 16, 128], BF16)
        Xp = k.sb("Xp", [128, 2, 2, 16, 128], BF16)
        Zf = k.sb("Zf", [128, 16, 128], F32)
        Zb = k.sb("Zb", [128, 16, 128], BF16)
        nZb = k.sb("nZb", [128, 16, 128], BF16)
        o64 = Ring(k, "o64", [64, 16, 128], F32, 1)
        oy = Ring(k, "oy", [128, 1024], F32, 1)
        od = Ring(k, "od", [64, 16], F32, 2)
        pbn = [0]

        def nb():
            pbn[0] += 1
            return pbn[0] % 8

        for tt in range(NTS):
            ts = slice(tt * 128, (tt + 1) * 128)
            k.dma('sp', z[:], ztm[ts, :], reads=['ztm'], writes=['z'])
            r_ = z[:, 0:1024]
            k_ = z[:, 1024:2048]
            v_ = z[:, 2048:3072]
            k.op('act', lambda: A.activation(out=t128[:, 0:128], in_=z[:, 3072:3200], func=AF.Tanh), reads=['z'], writes=['t128'])
            k.op('act', lambda: A.activation(out=t128[:, 256:416], in_=z[:, 3328:3488], func=AF.Sigmoid), reads=['z'], writes=['t128'])
            k.op('dve', lambda: V.tensor_copy(out=t128[:, 128:256], in_=z[:, 3200:3328]), reads=['z'], writes=['t128'])
            p0 = nb()
            for q, (c0, n) in enumerate([(0, 128), (128, 128), (256, 128), (384, 32)]):
                k.op('pe', lambda: PE.transpose(out=ps[p0][0:n, q * 128:(q + 1) * 128], in_=t128[:, c0:c0 + n], identity=ident[:]), reads=['t128', 'ident'], writes=[pk[p0]])
            k.op('dve', lambda: V.tensor_copy(out=trT[:, 0:3, :].rearrange("p a b -> p (a b)"), in_=ps[p0][:, 0:384]), reads=[pk[p0]], writes=['trT'])
            k.op('dve', lambda: V.tensor_copy(out=trT[0:32, 3, :], in_=ps[p0][0:32, 384:512]), reads=[pk[p0]], writes=['trT'])
            for d in range(2):
                for (wsrc, wkey, bsrc, dst, q) in [(w2b, 'w2b', bc['w0f' if d == 0 else 'w0b'], sig, 0), (a2b, 'a2b', bc['a0f' if d == 0 else 'a0b'], av, 1)]:
                    for hf in range(2):
                        p1 = nb()
                        cs_ = slice(hf * 512, (hf + 1) * 512)
                        k.op('pe', lambda: PE.matmul(ps[p1][:], trT[d * 64:(d + 1) * 64, q, :], wsrc[d * 64:(d + 1) * 64, cs_], start=True, stop=True),
                             reads=['trT', wkey], writes=[pk[p1]])
                        k.op('dve', lambda: V.tensor_tensor(out=f1[:, cs_], in0=ps[p1][:], in1=bsrc[:, cs_], op=OP.add), reads=[pk[p1], 'bc_w0f', 'bc_w0b', 'bc_a0f', 'bc_a0b'], writes=['f1'])
                    k.op('act', lambda: A.activation(out=dst[:, d, :], in_=f1[:], func=AF.Sigmoid), reads=['f1'], writes=['sig' if q == 0 else 'av'])
            g_, gk = oy.next()
            for hf in range(2):
                p1 = nb()
                cs_ = slice(hf * 512, (hf + 1) * 512)
                k.op('pe', lambda: PE.matmul(ps[p1][:], trT[:, 2, :], g2b[:, 0, cs_], start=True, stop=False), reads=['trT', 'g2b'], writes=[pk[p1]])
                k.op('pe', lambda: PE.matmul(ps[p1][:], trT[0:32, 3, :], g2b[0:32, 1, cs_], start=False, stop=True), reads=['trT', 'g2b'], writes=[pk[p1]])
                evac(hf, g_[:, cs_], ps[p1][:], [pk[p1]], [gk])
            k.dma('sp', scGt[ts, :], g_[:], reads=[gk], writes=['scGt'], semkey=(gk, 'st'))
            k.op('dve', lambda: V.tensor_tensor(out=kkv[:], in0=k_, in1=bc['kk'][:], op=OP.mult), reads=['z', 'bc_kk'], writes=['kkv'])
            k.op('pool', lambda: G.tensor_tensor(out=f2[:], in0=kkv[:], in1=kkv[:], op=OP.mult), reads=['kkv'], writes=['f2'])
            k.op('dve', lambda: V.tensor_reduce(out=sm[:, 0:16], in_=f2[:].rearrange("p (h n) -> p h n", n=64), axis=AX.X, op=OP.add), reads=['f2'], writes=['sm'])
            k.op('act', lambda: A.activation(out=sm[:, 0:16], in_=sm[:, 0:16], func=AF.Sqrt), reads=['sm'], writes=['sm'])
            k.op('dve', lambda: V.tensor_scalar(out=sm[:, 0:16], in0=sm[:, 0:16], scalar1=1e-12, scalar2=None, op0=OP.max), reads=['sm'], writes=['sm'])
            k.op('dve', lambda: V.reciprocal(out=sm[:, 0:16], in_=sm[:, 0:16]), reads=['sm'], writes=['sm'])
            k.op('dve', lambda: V.tensor_tensor(out=kkv[:].rearrange("p (h n) -> p h n", n=64), in0=kkv[:].rearrange("p (h n) -> p h n", n=64),
                                                in1=sm[:, 0:16].unsqueeze(2).to_broadcast([128, 16, 64]), op=OP.mult), reads=['kkv', 'sm'], writes=['kkv'])
            k.op('pool', lambda: G.tensor_tensor(out=f2[:], in0=r_, in1=bc['rk'][:], op=OP.mult), reads=['z', 'bc_rk'], writes=['f2'])
            for d in range(2):
                k.op('dve', lambda: V.scalar_tensor_tensor(out=f1[:], in0=av[:, d, :], scalar=-1.0, in1=bc['ka'][:], op0=OP.add, op1=OP.mult), reads=['av', 'bc_ka'], writes=['f1'])
                k.op('dve', lambda: V.scalar_tensor_tensor(out=kmv[:, d, :], in0=f1[:], scalar=1.0, in1=k_, op0=OP.add, op1=OP.mult), reads=['f1', 'z'], writes=['kmv'])
                k.op('pool', lambda: G.tensor_tensor(out=bv[:, d, :], in0=kkv[:], in1=av[:, d, :], op=OP.mult), reads=['kkv', 'av'], writes=['bv'])
                k.op('dve', lambda: V.tensor_tensor(out=f1[:], in0=kmv[:, d, :], in1=f2[:], op=OP.mult), reads=['kmv', 'f2'], writes=['f1'])
                k.op('dve', lambda: V.tensor_reduce(out=sm[:, 16 + 16 * d:32 + 16 * d], in_=f1[:].rearrange("p (h n) -> p h n", n=64), axis=AX.X, op=OP.add), reads=['f1'], writes=['sm'])
            k.op('dve', lambda: V.tensor_tensor(out=sm[:, 48:64], in0=sm[:, 16:32], in1=sm[:, 32:48], op=OP.add), reads=['sm'], writes=['sm'])
            k.dma('sp', scB[ts, :], sm[:, 48:64], reads=['sm'], writes=['scB'], semkey=('scB',))
            k.op('pool', lambda: G.tensor_copy(out=Vb[:], in_=v_), reads=['z'], writes=['Vb'])
            for d in range(2 if SS > 1 else 0):
                tri_i = tri[:, 0 if d == 0 else 2, :]
                tri_e = tri[:, 1 if d == 0 else 3, :]
                m_st = tri[:, 1 if d == 0 else 3, :]
                m_ts = tri[:, 3 if d == 0 else 1, :]
                m_in = tri[:, 0 if d == 0 else 2, :]
                pe_ = [nb(), nb()]
                pi_ = [nb(), nb()]
                for hf in range(2):
                    cs_ = slice(hf * 512, (hf + 1) * 512)
                    k.op('pe', lambda: PE.matmul(ps[pe_[hf]][:], tri_e, sig[:, d, cs_], start=True, stop=True), reads=['tri', 'sig'], writes=[pk[pe_[hf]]])
                    k.op('pe', lambda: PE.matmul(ps[pi_[hf]][:], tri_i, sig[:, d, cs_], start=True, stop=True), reads=['tri', 'sig'], writes=[pk[pi_[hf]]])
                for hf in range(2):
                    cs_ = slice(hf * 512, (hf + 1) * 512)
                    k.op('act', lambda: A.activation(out=f1[:, cs_], in_=ps[pe_[hf]][:], func=AF.Exp, scale=-CDEC), reads=[pk[pe_[hf]]], writes=['f1'])
                    k.op('dve', lambda: V.tensor_tensor(out=At[:, cs_], in0=kkv[:, cs_], in1=f1[:, cs_], op=OP.mult), reads=['kkv', 'f1'], writes=['At'])
                    k.op('act', lambda: A.activation(out=f2[:, cs_], in_=ps[pi_[hf]][:], func=AF.Exp, scale=CDEC), reads=[pk[pi_[hf]]], writes=['f2'])
                    k.op('dve', lambda: V.tensor_tensor(out=Bt[:, cs_], in0=bv[:, d, cs_], in1=f2[:, cs_], op=OP.mult), reads=['bv', 'f2'], writes=['Bt'])
                    k.op('pool', lambda: G.tensor_tensor(out=Kt[:, cs_], in0=kmv[:, d, cs_], in1=f2[:, cs_], op=OP.mult), reads=['kmv', 'f2'], writes=['Kt'])
                    k.op('act', lambda: A.activation(out=f1[:, cs_], in_=ps[pi_[hf]][:], func=AF.Exp, scale=-CDEC), reads=[pk[pi_[hf]]], writes=['f1'])
                    k.op('dve', lambda: V.tensor_tensor(out=Rt[:, cs_], in0=r_[:, cs_], in1=f1[:, cs_], op=OP.mult), reads=['z', 'f1'], writes=['Rt'])
                k.op('pool', lambda: G.tensor_copy(out=Btb[:], in_=Bt[:]), reads=['Bt'], writes=['Btb'])
                k.op('pool', lambda: G.tensor_copy(out=Ktb[:], in_=Kt[:]), reads=['Kt'], writes=['Ktb'])
                k.op('pool', lambda: G.tensor_copy(out=Rtb[:], in_=Rt[:]), reads=['Rt'], writes=['Rtb'])
                if SS <= 2:
                    continue
                p1 = nb()
                for hh in range(16):
                    k.op('pe', lambda: PE.matmul(ps[p1][0:64, hh:hh + 1], sig[:, d, hh * 64:(hh + 1) * 64], onec[:, 0:1], start=True, stop=True), reads=['sig', 'onec'], writes=[pk[p1]])
                d_, dk = od.next()
                k.op('act', lambda: A.activation(out=d_[:], in_=ps[p1][0:64, 0:16], func=AF.Exp, scale=-CDEC), reads=[pk[p1]], writes=[dk])
                k.dma('sp', scD[tt, d], d_[:], reads=[dk], writes=['scD'], semkey=(dk, 'st'))
                if SS <= 3:
                    continue
                for qi, (src, skey) in enumerate([(At, 'At'), (Bt, 'Bt'), (Kt, 'Kt'), (Rt, 'Rt')]):
                    for hf in range(2):
                        p1 = nb()
                        for q in range(4):
                            pr = hf * 4 + q
                            k.op('pe', lambda: PE.transpose(out=ps[p1][:, q * 128:(q + 1) * 128], in_=src[:, pr * 128:(pr + 1) * 128], identity=ident[:]),
                                 reads=[skey, 'ident'], writes=[pk[p1]])
                        evac(qi + hf, XT4[:, qi, hf * 4:hf * 4 + 4, :].rearrange("p a b -> p (a b)"), ps[p1][:], [pk[p1]], ['XT4'])

                def hpos(hh):
                    return (hh % 2) * 8 + hh // 2

                def fm(qi, hh):
                    return XT4[(hh % 2) * 64:(hh % 2) * 64 + 64, qi, hh // 2, :]
                if SS <= 4:
                    continue
                for mi, (lq, rq, msk) in enumerate([(1, 0, m_st), (0, 1, m_ts), (2, 0, m_st), (1, 3, m_in), (2, 3, m_in)]):
                    for hg in range(4):
                        p1 = nb()
                        for q in range(4):
                            hh = 2 * ((hg % 2) * 4 + q) + hg // 2
                            k.op('pe', lambda: PE.matmul(ps[p1][:, q * 128:(q + 1) * 128], fm(lq, hh), fm(rq, hh), start=True, stop=True), reads=['XT4'], writes=[pk[p1]])
                        k.op('dve', lambda: V.tensor_tensor(out=M5[:, mi, hg * 4:hg * 4 + 4, :], in0=ps[p1][:].rearrange("p (a b) -> p a b", b=128),
                                                            in1=msk.unsqueeze(1).to_broadcast([128, 4, 128]), op=OP.mult), reads=[pk[p1], 'tri'], writes=['M5_%d' % mi])
                if SS <= 5:
                    continue
                k.op('pool', lambda: G.tensor_copy(out=Zf[:, :, 0:64], in_=At[:].rearrange("p (h n) -> p h n", n=64)), reads=['At'], writes=['Zf'])
                for hf in range(2):
                    p1 = nb()
                    for q in range(8):
                        hh = hf * 8 + q
                        k.op('pe', lambda: PE.matmul(ps[p1][:, q * 64:(q + 1) * 64], M5[:, 2, hpos(hh), :], Vb[:, hh * 64:(hh + 1) * 64], start=True, stop=True), reads=['M5_2', 'Vb'], writes=[pk[p1]])
                    k.op('dve', lambda: V.tensor_copy(out=Zf[:, hf * 8:hf * 8 + 8, 64:128], in_=ps[p1][:].rearrange("p (a b) -> p a b", b=64)), reads=[pk[p1]], writes=['Zf'])
                k.op('act', lambda: A.copy(out=Zb[:], in_=Zf[:]), reads=['Zf'], writes=['Zb'])
                if SS <= 6:
                    continue
                cur = None
                for it in range(7):
                    if it == 0:
                        Xc = lambda hh: M5[:, 1, hpos(hh), :]
                        XTc = lambda hh: M5[:, 0, hpos(hh), :]
                        xkeys = ['M5_0', 'M5_1']
                    else:
                        src_i = (it - 1) % 2
                        dst_i = it % 2
                        if it == 1:
                            Xs, XTs, skeys = (lambda hh: M5[:, 1, hpos(hh), :]), (lambda hh: M5[:, 0, hpos(hh), :]), ['M5_0', 'M5_1']
                        else:
                            Xs, XTs, skeys = (lambda hh, si=src_i: Xp[:, si, 0, hh, :]), (lambda hh, si=src_i: Xp[:, si, 1, hh, :]), ['Xp%d' % src_i]
                        for which in range(2):
                            for hg in range(4):
                                p1 = nb()
                                for q in range(4):
                                    hh = hg * 4 + q
                                    if which == 0:
                                        k.op('pe', lambda: PE.matmul(ps[p1][:, q * 128:(q + 1) * 128], XTs(hh), Xs(hh), start=True, stop=True), reads=skeys, writes=[pk[p1]])
                                    else:
                                        k.op('pe', lambda: PE.matmul(ps[p1][:, q * 128:(q + 1) * 128], Xs(hh), XTs(hh), start=True, stop=True), reads=skeys, writes=[pk[p1]])
                                evac(hg, Xp[:, dst_i, which, hg * 4:hg * 4 + 4, :].rearrange("p a b -> p (a b)"), ps[p1][:], [pk[p1]], ['Xp%d' % dst_i])
                        XTc = lambda hh, di=dst_i: Xp[:, di, 1, hh, :]
                        xkeys = ['Xp%d' % dst_i]
                    for hg in range(4):
                        p1 = nb()
                        for q in range(4):
                            hh = hg * 4 + q
                            k.op('pe', lambda: PE.matmul(ps[p1][:, q * 128:(q + 1) * 128], XTc(hh), Zb[:, hh, :], start=True, stop=True), reads=xkeys + ['Zb'], writes=[pk[p1]])
                        k.op('dve', lambda: V.tensor_tensor(out=Zf[:, hg * 4:hg * 4 + 4, :], in0=Zf[:, hg * 4:hg * 4 + 4, :], in1=ps[p1][:].rearrange("p (a b) -> p a b", b=128),
                                                            op=(OP.subtract if it == 0 else OP.add)), reads=[pk[p1], 'Zf'], writes=['Zf'])
                    k.op('act', lambda: A.copy(out=Zb[:], in_=Zf[:]), reads=['Zf'], writes=['Zb'])
                k.op('act', lambda: A.mul(out=nZb[:], in_=Zf[:], mul=-1.0), reads=['Zf'], writes=['nZb'])
                if SS <= 7:
                    continue
                o_, okey = o64.next()
                for hf in range(2):
                    p1 = nb()
                    for q in range(8):
                        hh = hf * 8 + q
                        k.op('pe', lambda: PE.matmul(ps[p1][0:64, q * 64:(q + 1) * 64], Zb[:, hh, 0:64], Btb[:, hh * 64:(hh + 1) * 64], start=True, stop=True), reads=['Zb', 'Btb'], writes=[pk[p1]])
                    k.op('dve', lambda: V.tensor_tensor(out=o_[:, hf * 8:hf * 8 + 8, 0:64], in0=ident[0:64, 0:64].unsqueeze(1).to_broadcast([64, 8, 64]),
                                                        in1=ps[p1][0:64, :].rearrange("p (a b) -> p a b", b=64), op=OP.subtract), reads=[pk[p1], 'ident'], writes=[okey])
                k.dma('sp', scP[tt, d], o_[:, :, 0:64], reads=[okey], writes=['scP'], semkey=(okey, 'st'))
                o_, okey = o64.next()
                for hf in range(2):
                    p1 = nb()
                    for q in range(8):
                        hh = hf * 8 + q
                        k.op('pe', lambda: PE.matmul(ps[p1][0:64, q * 64:(q + 1) * 64], Ktb[:, hh * 64:(hh + 1) * 64], Vb[:, hh * 64:(hh + 1) * 64], start=True, stop=False), reads=['Ktb', 'Vb'], writes=[pk[p1]])
                        k.op('pe', lambda: PE.matmul(ps[p1][0:64, q * 64:(q + 1) * 64], Btb[:, hh * 64:(hh + 1) * 64], nZb[:, hh, 64:128], start=False, stop=True), reads=['Btb', 'nZb'], writes=[pk[p1]])
                    evac(hf, o_[:, hf * 8:hf * 8 + 8, 0:64], ps[p1][0:64, :].rearrange("p (a b) -> p a b", b=64), [pk[p1]], [okey])
                k.dma('sp', scQ[tt, d], o_[:, :, 0:64], reads=[okey], writes=['scQ'], semkey=(okey, 'st'))
                o_, okey = o64.next()
                for hg in range(4):
                    p1 = nb()
                    for q in range(4):
                        hh = hg * 4 + q
                        k.op('pe', lambda: PE.matmul(ps[p1][0:64, q * 128:(q + 1) * 128], Rtb[:, hh * 64:(hh + 1) * 64], idb[:], start=True, stop=False), reads=['Rtb', 'idb'], writes=[pk[p1]])
                        k.op('pe', lambda: PE.matmul(ps[p1][0:64, q * 128:(q + 1) * 128], nZb[:, hh, 0:64], M5[:, 3, hpos(hh), :], start=False, stop=True), reads=['nZb', 'M5_3'], writes=[pk[p1]])
                    evac(hg, o_[:, hg * 4:hg * 4 + 4, :].rearrange("p a b -> p (a b)"), ps[p1][0:64, :], [pk[p1]], [okey])
                k.dma('sp', scG[tt, d], o_[:], reads=[okey], writes=['scG'], semkey=(okey, 'st'))
                y_, yk = oy.next()
                for hf in range(2):
                    p1 = nb()
                    for q in range(8):
                        hh = hf * 8 + q
                        k.op('pe', lambda: PE.matmul(ps[p1][:, q * 64:(q + 1) * 64], M5[:, 4, hpos(hh), :], Vb[:, hh * 64:(hh + 1) * 64], start=True, stop=False), reads=['M5_4', 'Vb'], writes=[pk[p1]])
                        k.op('pe', lambda: PE.matmul(ps[p1][:, q * 64:(q + 1) * 64], M5[:, 3, hpos(hh), :], nZb[:, hh, 64:128], start=False, stop=True), reads=['M5_3', 'nZb'], writes=[pk[p1]])
                    evac(hf, y_[:, hf * 512:(hf + 1) * 512], ps[p1][:], [pk[p1]], [yk])
                k.dma('sp', scY[tt, d], y_[:], reads=[yk], writes=['scY'], semkey=(yk, 'st'))


    def phase_chain(l):
        S = k.sb("S", [64, 2, 16, 64], F32)
        Ys = k.sb("Ys", [128, NTT, 1024], F32)
        Pt = Ring(k, "Pt", [64, 16, 64], F32, 2)
        Qt = Ring(k, "Qt", [64, 16, 64], F32, 2)
        Gt = Ring(k, "Gt", [64, 16, 128], F32, 2)
        Yt = Ring(k, "Yt", [128, 1024], F32, 2)
        Dt = Ring(k, "Dt", [64, 16], F32, 2)
        tq = k.sb("tq", [64, 16, 64], F32)
        s0t = k.sb("s0t", [128, 8, 64], F32)
        so = Ring(k, "so", [128, 8, 64], F32, 2)
        pbn = [0]

        def nb():
            pbn[0] += 1
            return pbn[0] % 8
        first = {}

        def out_state(seg, d):
            p1 = nb()
            for hp in range(8):
                k.op('pe', lambda: PE.transpose(out=ps[p1][:, hp * 64:(hp + 1) * 64], in_=S[:, d, 2 * hp:2 * hp + 2, :].rearrange("p a b -> p (a b)"), identity=ident[0:64, 0:64]),
                     reads=['S%d' % d, 'ident'], writes=[pk[p1]])
            o_, okey = so.next()
            k.op('dve', lambda: V.tensor_copy(out=o_[:].rearrange("p a b -> p (a b)"), in_=ps[p1][:]), reads=[pk[p1]], writes=[okey])
            k.dma('sp', st_out[l, seg, d].rearrange("(hp h2) v kk -> (h2 v) hp kk", h2=2), o_[:], reads=[okey], semkey=(okey, 'st'))

        def step(c, d):
            P_, Pk = Pt.next()
            Q_, Qk = Qt.next()
            G_, Gk = Gt.next()
            Y_, Yk = Yt.next()
            D_, Dk = Dt.next()
            k.dma('sp', P_[:], scP[c, d], reads=['scP'], writes=[Pk])
            k.dma('sp', Q_[:], scQ[c, d], reads=['scQ'], writes=[Qk])
            k.dma('sp', G_[:], scG[c, d], reads=['scG'], writes=[Gk])
            k.dma('sp', Y_[:], scY[c, d], reads=['scY'], writes=[Yk])
            k.dma('sp', D_[:], scD[c, d], reads=['scD'], writes=[Dk])
            sk_ = 'S%d' % d
            for hf in range(2):
                p1 = nb()
                cs_ = slice(hf * 512, (hf + 1) * 512)
                for q in range(8):
                    hh = hf * 8 + q
                    k.op('pe', lambda: PE.matmul(ps[p1][:, q * 64:(q + 1) * 64], G_[:, hh, :], S[:, d, hh, :], start=True, stop=True), reads=[Gk, sk_], writes=[pk[p1]])
                if c not in first:
                    k.op('dve', lambda: V.tensor_tensor(out=Ys[:, c, cs_], in0=ps[p1][:], in1=Y_[:, cs_], op=OP.add), reads=[pk[p1], Yk], writes=['Ys%d' % c])
                else:
                    k.op('dve', lambda: V.tensor_tensor(out=Y_[:, cs_], in0=ps[p1][:], in1=Y_[:, cs_], op=OP.add), reads=[pk[p1], Yk], writes=[Yk])
                    k.op('pool', lambda: G.tensor_tensor(out=Ys[:, c, cs_], in0=Ys[:, c, cs_], in1=Y_[:, cs_], op=OP.add), reads=[Yk, 'Ys%d' % c], writes=['Ys%d' % c])
            first[c] = True
            pp = [nb(), nb()]
            for hf in range(2):
                for q in range(8):
                    hh = hf * 8 + q
                    k.op('pe', lambda: PE.matmul(ps[pp[hf]][0:64, q * 64:(q + 1) * 64], P_[:, hh, :], S[:, d, hh, :], start=True, stop=True), reads=[Pk, sk_], writes=[pk[pp[hf]]])
            for hf in range(2):
                hs = slice(hf * 8, hf * 8 + 8)
                k.op('dve', lambda: V.tensor_tensor(out=tq[:, hs, :], in0=ps[pp[hf]][0:64, :].rearrange("p (a b) -> p a b", b=64), in1=Q_[:, hs, :], op=OP.add),
                     reads=[pk[pp[hf]], Qk], writes=['tq'])
            k.op('dve', lambda: V.tensor_tensor(out=S[:, d], in0=tq[:], in1=D_[:].unsqueeze(2).to_broadcast([64, 16, 64]), op=OP.mult), reads=['tq', Dk], writes=[sk_])

        for d in range(2):
            k.dma('sp', s0t[:], s0_in[l, d].rearrange("(hp h2) v kk -> (h2 v) hp kk", h2=2), writes=['s0t'])
            for hf in range(2):
                p1 = nb()
                for q in range(4):
                    hp = hf * 4 + q
                    k.op('pe', lambda: PE.transpose(out=ps[p1][0:64, q * 128:(q + 1) * 128], in_=s0t[:, hp, :], identity=ident[:]), reads=['s0t', 'ident'], writes=[pk[p1]])
                k.op('dve', lambda: V.tensor_copy(out=S[:, d, hf * 8:hf * 8 + 8, :].rearrange("p a b -> p (a b)"), in_=ps[p1][0:64, :]), reads=[pk[p1]], writes=['S%d' % d])
            order = list(range(8)) if d == 0 else list(range(7, -1, -1))
            for n_, c in enumerate(order):
                step(c, d)
                if n_ % 2 == 1:
                    out_state(c // 2, d)
                    if n_ < 7:
                        k.op('dve', lambda: V.tensor_scalar(out=S[:, d], in0=S[:, d], scalar1=carry[0:64, 0:1], scalar2=None, op0=OP.mult), reads=['S%d' % d, 'carry'], writes=['S%d' % d])
        for seg in range(2):
            for d in range(2):
                k.op('dve', lambda: V.memset(S[:, d], 0.0), writes=['S%d' % d])
                cc = [8 + 2 * seg, 9 + 2 * seg]
                for c in (cc if d == 0 else cc[::-1]):
                    step(c, d)
                out_state(4 + seg, d)
        lg = k.sb("lg", [128, 1024], F32)
        lb = k.sb("lb", [128, 1024], F32)
        bcast_load(lg[:], W["lnx_g"][l], 'lg')
        bcast_load(lb[:], W["lnx_b"][l], 'lb')
        vt = Ring(k, "vt", [128, 1024], F32, 2)
        gt = Ring(k, "gt", [128, 1024], F32, 2)
        bt = Ring(k, "bt", [128, 16], F32, 2)
        sm = k.sb("sm", [128, 48], F32)
        f1 = k.sb("f1", [128, 1024], F32)
        yo = Ring(k, "yo", [128, 8, 128], BF16, 2)
        h3 = lambda ap: ap.rearrange("p (h n) -> p h n", n=64)
        for tt in range(NTT):
            ts = slice(tt * 128, (tt + 1) * 128)
            v_, vk = vt.next()
            g_, gk = gt.next()
            b_, bk = bt.next()
            k.dma('sp', v_[:], ztm[ts, 2048:3072], reads=['ztm'], writes=[vk])
            k.dma('sp', g_[:], scGt[ts, :], reads=['scGt'], writes=[gk])
            k.dma('sp', b_[:], scB[ts, :], reads=['scB'], writes=[bk])
            y = Ys[:, tt, :]
            yk_ = 'Ys%d' % tt
            k.op('dve', lambda: V.tensor_reduce(out=sm[:, 0:16], in_=h3(y), axis=AX.X, op=OP.add), reads=[yk_], writes=['sm'])
            k.op('dve', lambda: V.tensor_scalar(out=sm[:, 0:16], in0=sm[:, 0:16], scalar1=1.0 / 64, scalar2=None, op0=OP.mult), reads=['sm'], writes=['sm'])
            k.op('dve', lambda: V.tensor_tensor(out=h3(y), in0=h3(y), in1=sm[:, 0:16].unsqueeze(2).to_broadcast([128, 16, 64]), op=OP.subtract), reads=[yk_, 'sm'], writes=[yk_])
            k.op('pool', lambda: G.tensor_tensor(out=f1[:], in0=y, in1=y, op=OP.mult), reads=[yk_], writes=['f1'])
            k.op('dve', lambda: V.tensor_reduce(out=sm[:, 16:32], in_=h3(f1[:]), axis=AX.X, op=OP.add), reads=['f1'], writes=['sm'])
            k.op('act', lambda: A.activation(out=sm[:, 16:32], in_=sm[:, 16:32], func=AF.Sqrt, bias=epsx[:, 0:1], scale=1.0 / 64), reads=['sm', 'epsx'], writes=['sm'])
            k.op('dve', lambda: V.reciprocal(out=sm[:, 16:32], in_=sm[:, 16:32]), reads=['sm'], writes=['sm'])
            k.op('dve', lambda: V.tensor_tensor(out=h3(y), in0=h3(y), in1=sm[:, 16:32].unsqueeze(2).to_broadcast([128, 16, 64]), op=OP.mult), reads=[yk_, 'sm'], writes=[yk_])
            k.op('dve', lambda: V.tensor_tensor(out=y, in0=y, in1=lg[:], op=OP.mult), reads=[yk_, 'lg'], writes=[yk_])
            k.op('dve', lambda: V.tensor_tensor(out=h3(v_[:]), in0=h3(v_[:]), in1=b_[:].unsqueeze(2).to_broadcast([128, 16, 64]), op=OP.mult), reads=[vk, bk], writes=[vk])
            k.op('dve', lambda: V.tensor_tensor(out=y, in0=y, in1=lb[:], op=OP.add), reads=[yk_, 'lb'], writes=[yk_])
            k.op('dve', lambda: V.tensor_tensor(out=y, in0=y, in1=v_[:], op=OP.add), reads=[yk_, vk], writes=[yk_])
            k.op('dve', lambda: V.tensor_tensor(out=y, in0=y, in1=g_[:], op=OP.mult), reads=[yk_, gk], writes=[yk_])
            o_, okey = yo.next()
            for hf in range(2):
                p1 = nb()
                for q in range(4):
                    pr = hf * 4 + q
                    k.op('pe', lambda: PE.transpose(out=ps[p1][:, q * 128:(q + 1) * 128], in_=Ys[:, tt, pr * 128:(pr + 1) * 128], identity=ident[:]), reads=[yk_, 'ident'], writes=[pk[p1]])
                evac(hf, o_[:, hf * 4:hf * 4 + 4, :].rearrange("p a b -> p (a b)"), ps[p1][:], [pk[p1]], [okey])
            k.dma('sp', ymixT[8:16, :, ts].rearrange("q p t -> p q t"), o_[:], reads=[okey], writes=['ymixT'], semkey=(okey, 'st'))

    def phase_wout(l):
        x = k.sb("x", [128, KC, 512], F32)
        ym = k.sb("ym", [128, KC, 512], BF16)
        wo = k.sb("wo", [128, KC, D], BF16)
        wst = Ring(k, "wst", [128, KC, 256], F32, 2)
        wv = W["w_out"][l].rearrange("(kc p) n -> p kc n", p=128)
        for b in range(8):
            st, sk = wst.next()
            k.dma('sp', st[:], wv[:, :, b * 256:(b + 1) * 256], writes=[sk])
            k.op('pool', lambda: G.tensor_copy(out=wo[:, :, b * 256:(b + 1) * 256], in_=st[:]), reads=[sk], writes=['wo%d' % b])
        for tg in range(3):
            ci = 0 if tg < 2 else 1
            ts = slice(tg * 512, (tg + 1) * 512)
            k.dma('sp', x[:], xs[:, :, ts].rearrange("kc p t -> p kc t"), reads=['xs'], writes=['x'])
            k.dma('sp', ym[:], ymixT[:, :, ts].rearrange("kc p t -> p kc t"), reads=['ymixT'], writes=['ym'])
            for oc in range(KC):
                po = oc % 4
                for kc in range(KC):
                    k.op('pe', lambda: PE.matmul(ps[po][:], wo[:, kc, oc * 128:(oc + 1) * 128], ym[:, kc, :], start=(kc == 0), stop=(kc == KC - 1)),
                         reads=['wo%d' % (oc // 2), 'ym'], writes=[pk[po]])
                k.op('dve', lambda: V.scalar_tensor_tensor(out=x[:, oc, :], in0=ps[po][:], scalar=gtv[:, l, 1, oc, ci:ci + 1], in1=x[:, oc, :],
                                                           op0=OP.mult, op1=OP.add), reads=[pk[po], 'gtv', 'x'], writes=['x'])
            k.dma('sp', xs[:, :, ts].rearrange("kc p t -> p kc t"), x[:], reads=['x'], writes=['xs'])

    if phases is None or 'mod' in phases:
        phase_mod()
        phase_reset()
    if phases is None or 'in' in phases:
        phase_in()
        phase_reset()
    stages = []
    for l in range(DEPTH):
        stages.append(('ffn', l, 0))
        stages.append(('mix', l, 1))
        stages.append(('ffn', l, 2))
    nst = 0
    for kind, l, i in stages:
        if stop_after is not None and nst >= stop_after:
            break
        nst += 1
        if kind == 'ffn':
            if phases is None or 'ffn' in phases:
                phase_ffn(l, i)
                phase_reset()
        else:
            for nm, fn in [('win', phase_win), ('gmlp', phase_gmlp), ('fnet', phase_fnet), ('shift', phase_shift), ('scan', phase_scan), ('chain', phase_chain), ('wout', phase_wout)]:
                if mix_parts is None or nm in mix_parts:
                    fn(l)
                    phase_reset()
    if phases is None or 'out' in phases:
        phase_out()
    k.finish()
    print("program built: ninst", k.ninst, "dma sems", len(k.dpool))
    nc.used_weight_inputs = list(used_inputs)
    return nc


def host_consts(kind):
    o = {}
    ch = np.arange(CIN)
    ind = np.zeros((2, 4, CIN), np.float32)
    q = CIN // 4
    if kind == 'grid':
        for j in range(4):
            ind[0, j, j * q:(j + 1) * q] = 1
    else:
        ind[0, 0, :CIN // 2] = 1
        ind[0, 1, CIN // 2:] = 1
    ind[1, 0, :CIN // 2] = 1
    ind[1, 1, CIN // 2:] = 1
    o['ind'] = ind
    valid = np.zeros((NT, 4), np.float32)
    t = np.arange(1024)
    if kind == 'grid':
        valid[:1024, 0] = (t % 64 != 0)
        valid[:1024, 1] = (t % 64 != 63)
        valid[:1024, 2] = (t >= 64)
        valid[:1024, 3] = (t < 960)
    else:
        valid[:1024, 0] = (t % 256 != 0)
        valid[:1024, 1] = (t % 256 != 255)
    t2 = np.arange(512)
    valid[1024:, 0] = (t2 % 256 != 0)
    valid[1024:, 1] = (t2 % 256 != 255)
    o['valid'] = valid

    def dft(n):
        a = 2 * np.pi * np.outer(np.arange(n), np.arange(n)) / n
        return (np.cos(a) / np.sqrt(n)), (np.sin(a) / np.sqrt(n))
    if kind == 'grid':
        c, s = dft(1024)
    else:
        c256, s256 = dft(256)
        c = np.zeros((1024, 1024))
        s = np.zeros((1024, 1024))
        for j in range(4):
            c[j * 256:(j + 1) * 256, j * 256:(j + 1) * 256] = c256
            s[j * 256:(j + 1) * 256, j * 256:(j + 1) * 256] = s256
    o['CL'] = c.astype(np.float32)
    o['nSL'] = (-s).astype(np.float32)
    c256, s256 = dft(256)
    o['CB'] = c256.astype(np.float32)
    o['nSB'] = (-s256).astype(np.float32)
    cd, sd = dft(128)
    o['CSd'] = np.concatenate([cd, sd], 1).astype(np.float32)
    i = np.arange(128)
    ui = (i[:, None] <= i[None, :]).astype(np.float32)
    ue = (i[:, None] < i[None, :]).astype(np.float32)
    o['tri'] = np.stack([ui, ue, ui.T.copy(), ue.T.copy()])
    o['ident'] = np.eye(128, dtype=np.float32)
    o['onesm'] = np.full((128, 128), 1.0 / D, np.float32)
    o['carry'] = np.full((128, 1), 1.0 if kind == 'grid' else 0.0, np.float32)
    return o


WNAMES = ["norm_g", "w_mod", "b_mod", "ffn_w_in", "ffn_w_out", "w_in", "w_out", "sgu_ln_g", "sgu_ln_b", "sgu_w", "sgu_b",
          "shift_mu", "decay_w0", "decay_w2", "iclr_a0", "iclr_a2", "k_k", "k_a", "r_k", "gate_w2", "lnx_g", "lnx_b", "final_g"]


def core_seqs(c):
    if c < 4:
        return [2 * c, 2 * c + 1]
    return [8 + (c - 4) * 6 + j for j in range(6)]


def make_in_maps(inputs):
    f = lambda a: np.ascontiguousarray(np.asarray(a, dtype=np.float32))
    xp = f(inputs['x_prompt'])
    xsm = f(inputs['x_sample'])
    stw = f(inputs['state_wkv'])
    cc = f(inputs['c'])
    cctx = f(inputs['c_ctx'])
    wts = {n: f(inputs[n]) for n in WNAMES}
    cg = host_consts('grid')
    cs = host_consts('seq')
    maps = []
    for c in range(8):
        m = dict(wts)
        seqs = core_seqs(c)
        if c < 4:
            m.update(cg)
            m['xin'] = np.concatenate([xsm[c]] + [xp[s] for s in seqs], 0)
            m['cond'] = np.stack([cc[c], cctx])
            m['s0'] = np.ascontiguousarray(stw[c])
        else:
            m.update(cs)
            m['xin'] = np.concatenate([xp[s] for s in seqs], 0)
            m['cond'] = np.stack([cctx, cctx])
            m['s0'] = np.zeros((DEPTH, 2, 16, 64, 64), np.float32)
        maps.append(m)
    return maps


def filter_maps(nc, maps):
    drop = set(WNAMES) - set(nc.used_weight_inputs)
    return [{n: v for n, v in m.items() if n not in drop} for m in maps]


def kernel(**inputs):
    maps = make_in_maps(inputs)
    nc = build_program()
    maps = filter_maps(nc, maps)
    res = run_bass_kernel_spmd(nc, maps, core_ids=list(range(8)))
    B, S = inputs['x_prompt'].shape[0], inputs['x_prompt'].shape[1]
    y_prompt = np.zeros((B, S, D), np.float32)
    y_sample = np.zeros((4, 1024, D), np.float32)
    new_state = np.zeros((B, DEPTH, 2, 16, 64, 64), np.float32)
    for c in range(8):
        r = res.results[c]
        y = r['y']
        st = r['st']
        seqs = core_seqs(c)
        if c < 4:
            y_sample[c] = y[:1024]
            for j, s in enumerate(seqs):
                y_prompt[s] = y[1024 + j * 256:1024 + (j + 1) * 256]
                new_state[s] = st[:, 4 + j]
        else:
            for j, s in enumerate(seqs):
                y_prompt[s] = y[j * 256:(j + 1) * 256]
                new_state[s] = st[:, j]
    return (y_prompt, y_sample, new_state)
```
